# Optimizing a Trainium2 kernel written in Bass

```python
import jax, jax.numpy as jnp
from jax import lax
import numpy as np

D_MODEL = 2048
BATCH = 2
SEQ = 4096
DEPTH = 2
DEC_BATCH = 32
DEC_SEQ = 4
PAST_LEN = 8192
PAGE_SIZE = 128

HEAD_DIM = 128
N_A_LAYERS = DEPTH // 2
N_B_LAYERS = DEPTH - N_A_LAYERS
A_Q_HEADS = D_MODEL // HEAD_DIM
A_KV_HEADS = 4
IDX_HEADS = 16
IDX_DIM = 128
IDX_ROPE_DIM = 64
TOPK_MAX = 256
A_Q_W = A_Q_HEADS * HEAD_DIM
A_KV_W = A_KV_HEADS * HEAD_DIM
IDX_Q_W = IDX_HEADS * IDX_DIM
A_SPLITS = (A_Q_W, A_Q_W + A_KV_W, A_Q_W + 2 * A_KV_W, A_Q_W + 2 * A_KV_W + IDX_Q_W,
            A_Q_W + 2 * A_KV_W + IDX_Q_W + IDX_DIM)
A_IN_W = A_SPLITS[-1] + IDX_HEADS
B_PATTERNS = ((128, 1), (512, 4), (2048, 16))
B_GROUPS = len(B_PATTERNS)
B_HEADS = 8
B_KV_HEADS = B_HEADS
B_OUT_W = B_HEADS * HEAD_DIM
WIN_MAX = max(w for w, _ in B_PATTERNS)
D_FF = 5632
CONV_W = 3
ROPE_THETA = 10000.0
EPS = 1e-6
NEG = -1e30
Q_BLOCK = 128

kernel_name = 'yoco_dsa_dilated_decode_step'


def rms_norm(x, g):
    xf = x.astype(jnp.float32)
    y = xf * lax.rsqrt(jnp.mean(xf * xf, axis=-1, keepdims=True) + EPS)
    return (y * g.astype(jnp.float32)).astype(x.dtype)


def layer_norm(x, g, b):
    xf = x.astype(jnp.float32)
    mu = jnp.mean(xf, axis=-1, keepdims=True)
    var = jnp.mean(jnp.square(xf - mu), axis=-1, keepdims=True)
    y = (xf - mu) * lax.rsqrt(var + EPS) * g.astype(jnp.float32) + b.astype(jnp.float32)
    return y.astype(x.dtype)


def rope(x, pos, rot_dim):
    half = rot_dim // 2
    inv = ROPE_THETA ** (-jnp.arange(half, dtype=jnp.float32) / half)
    ang = pos.astype(jnp.float32)[:, None] * inv[None, :]
    ang = ang.reshape((ang.shape[0],) + (1,) * (x.ndim - 3) + (half,))
    cos, sin = jnp.cos(ang), jnp.sin(ang)
    xf = x.astype(jnp.float32)
    x1, x2 = xf[..., :half], xf[..., half:rot_dim]
    out = jnp.concatenate([x1 * cos - x2 * sin, x2 * cos + x1 * sin, xf[..., rot_dim:]], axis=-1)
    return out.astype(x.dtype)


def ada(c, w, b, n):
    m = jax.nn.silu(c) @ w + b
    return jnp.split(m[:, None, :], n, axis=-1)


def modulate(xn, shift, scale):
    return xn * (1.0 + scale) + shift


def gather_rows(k, idx):
    return jax.vmap(lambda kb, ib: kb[ib])(k, idx)


def map_query_blocks(fn, qs, qidx):
    S = qidx.shape[0]
    if S <= Q_BLOCK:
        return fn(qs, qidx)
    nb = S // Q_BLOCK
    split = lambda a: jnp.moveaxis(a.reshape((a.shape[0], nb, Q_BLOCK) + a.shape[2:]), 1, 0)
    out = lax.map(lambda args: fn(args[0], args[1]),
                  (tuple(split(a) for a in qs), qidx.reshape(nb, Q_BLOCK)))
    return jnp.moveaxis(out, 0, 1).reshape(qs[0].shape[0], S, -1)


def a_project(xn, pos, p, a):
    B, S, _ = xn.shape
    q, k, v, qi, ki, wt = jnp.split(xn @ p['a_w_in'][a], A_SPLITS, axis=-1)
    q = rope(rms_norm(q.reshape(B, S, A_Q_HEADS, HEAD_DIM), p['a_q_g'][a]), pos, HEAD_DIM)
    k = rope(rms_norm(k.reshape(B, S, A_KV_HEADS, HEAD_DIM), p['a_k_g'][a]), pos, HEAD_DIM)
    v = v.reshape(B, S, A_KV_HEADS, HEAD_DIM)
    qi = rope(qi.reshape(B, S, IDX_HEADS, IDX_DIM), pos, IDX_ROPE_DIM)
    ki = rope(layer_norm(ki, p['a_ki_g'][a], p['a_ki_b'][a]), pos, IDX_ROPE_DIM)
    wt = wt * (IDX_HEADS ** -0.5 * IDX_DIM ** -0.5)
    return q, k, v, qi, ki, wt


def index_scores(qi, wt, ki):
    s = jnp.einsum('bqhd,bld->bqhl', qi.astype(jnp.float32), ki.astype(jnp.float32))
    return jnp.einsum('bqhl,bqh->bql', jax.nn.relu(s), wt.astype(jnp.float32))


def sparse_attend(q, kg, vg, valid):
    B, Q = q.shape[:2]
    qg = q.reshape(B, Q, A_KV_HEADS, A_Q_HEADS // A_KV_HEADS, HEAD_DIM)
    logits = jnp.einsum('bqgrd,bqkgd->bqgrk', qg, kg).astype(jnp.float32) * HEAD_DIM ** -0.5
    logits = jnp.where(valid[:, :, None, None, :], logits, NEG)
    prob = jax.nn.softmax(logits, axis=-1).astype(vg.dtype)
    o = jnp.einsum('bqgrk,bqkgd->bqgrd', prob, vg)
    return o.reshape(B, Q, A_Q_W)


def a_mixer_prompt(xn, pos, p, a):
    q, k, v, qi, ki, wt = a_project(xn, pos, p, a)
    S = xn.shape[1]
    topk = min(TOPK_MAX, S // 4)
    key_pos = jnp.arange(S)

    def block(qs, t):
        qb, qib, wtb = qs
        sc = index_scores(qib, wtb, ki)
        sc = jnp.where(key_pos[None, None, :] <= t[None, :, None], sc, NEG)
        _, idx = lax.top_k(sc, topk)
        valid = idx <= t[None, :, None]
        return sparse_attend(qb, gather_rows(k, idx), gather_rows(v, idx), valid)

    o = map_query_blocks(block, (q, qi, wt), pos)
    return o @ p['a_w_o'][a], (k, v, ki)


def a_mixer_sample(xn, pos, p, a, past):
    q, k, v, qi, ki, wt = a_project(xn, pos, p, a)
    page_table = past['page_table']
    Bd, T, _ = xn.shape
    P = page_table.shape[1] * PAGE_SIZE
    L = P + T
    ki_past = past['cache_kidx_a'][a, page_table].reshape(Bd, P, IDX_DIM)
    ki_all = jnp.concatenate([ki_past.astype(ki.dtype), ki], axis=1)
    sc = index_scores(qi, wt, ki_all)
    sc = jnp.where(jnp.arange(L)[None, None, :] <= pos[None, :, None], sc, NEG)
    topk = min(TOPK_MAX, L // 4)
    _, idx = lax.top_k(sc, topk)
    valid = idx <= pos[None, :, None]
    in_past = (idx < P)[..., None, None]
    pidx = jnp.minimum(idx, P - 1)
    phys = jnp.take_along_axis(page_table, (pidx // PAGE_SIZE).reshape(Bd, -1), axis=1).reshape(idx.shape)
    off = pidx % PAGE_SIZE
    nidx = jnp.clip(idx - P, 0, T - 1)
    kg = jnp.where(in_past, past['cache_k_a'][a, phys, off].astype(k.dtype), gather_rows(k, nidx))
    vg = jnp.where(in_past, past['cache_v_a'][a, phys, off].astype(v.dtype), gather_rows(v, nidx))
    o = sparse_attend(q, kg, vg, valid)
    return o @ p['a_w_o'][a], (k, v, ki)


def shared_kv(h, c, pos, p, buf_k, buf_v):
    B, S, _ = h.shape
    sh, sc = ada(c, p['kv_ada_w'], p['kv_ada_b'], 2)
    xn = modulate(rms_norm(h, p['kv_norm_g']), sh, sc)
    k, v = jnp.split(xn @ p['kv_w'], 2, axis=-1)
    k = rope(rms_norm(k.reshape(B, S, B_KV_HEADS, HEAD_DIM), p['kv_k_g']), pos, HEAD_DIM)
    v = v.reshape(B, S, B_KV_HEADS, HEAD_DIM)
    if buf_k is not None:
        k = jnp.concatenate([buf_k.astype(k.dtype), k], axis=1)
        v = jnp.concatenate([buf_v.astype(v.dtype), v], axis=1)
    n_prev = k.shape[1] - S
    qidx = n_prev + jnp.arange(S)
    keep = min(WIN_MAX, k.shape[1])
    return k, v, qidx, k[:, -keep:], v[:, -keep:]


def dilated_block(qb, qidx, k, v):
    outs, lses = [], []
    for g, (win, dil) in enumerate(B_PATTERNS):
        kidx = qidx[:, None] - dil * jnp.arange(win // dil + 1)[None, :]
        valid = kidx >= 0
        kidx = jnp.maximum(kidx, 0)
        kg, vg = k[:, kidx], v[:, kidx]
        logits = jnp.einsum('bqhd,bqkhd->bqhk', qb[:, :, g], kg).astype(jnp.float32) * HEAD_DIM ** -0.5
        logits = jnp.where(valid[None, :, None, :], logits, NEG)
        m = jnp.max(logits, axis=-1, keepdims=True)
        e = jnp.exp(logits - m)
        s = jnp.sum(e, axis=-1, keepdims=True)
        outs.append(jnp.einsum('bqhk,bqkhd->bqhd', (e / s).astype(v.dtype), vg))
        lses.append((m + jnp.log(s))[..., 0])
    alpha = jax.nn.softmax(jnp.stack(lses, axis=0), axis=0)
    o = jnp.einsum('gbqh,gbqhd->bqhd', alpha, jnp.stack(outs, axis=0).astype(jnp.float32))
    return o.astype(v.dtype).reshape(qb.shape[0], qb.shape[1], B_OUT_W)


def b_mixer(xn, pos, kv, p, bi):
    k, v, qidx = kv[0], kv[1], kv[2]
    B, S, _ = xn.shape
    q = (xn @ p['b_w_q'][bi]).reshape(B, S, B_GROUPS, B_HEADS, HEAD_DIM)
    q = rope(rms_norm(q, p['b_q_g'][bi]), pos, HEAD_DIM)
    o = map_query_blocks(lambda qs, qi: dilated_block(qs[0], qi, k, v), (q,), qidx)
    return o @ p['b_w_o'][bi]


def conv_ffn(xn, prev, p, layer):
    S = xn.shape[1]
    u = xn @ p['ffn_w_up'][layer]
    ext = jnp.concatenate([prev.astype(u.dtype), u], axis=1)
    w = p['ffn_conv_w'][layer]
    y = p['ffn_conv_b'][layer]
    for i in range(CONV_W):
        y = y + w[i] * ext[:, i:i + S]
    a, b = jnp.split(y, 2, axis=-1)
    return (jax.nn.silu(a) * b) @ p['ffn_w_down'][layer], ext[:, -(CONV_W - 1):]


def trunk(h, c, pos, p, past):
    B = h.shape[0]
    a_rows, conv_rows = [], []
    kv = None
    for layer in range(DEPTH):
        sh1, sc1, g1, sh2, sc2, g2 = ada(c, p['ada_w'][layer], p['ada_b'][layer], 6)
        xn = modulate(rms_norm(h, p['norm_g'][layer, 0]), sh1, sc1)
        if layer < N_A_LAYERS:
            if past is None:
                mix, rows = a_mixer_prompt(xn, pos, p, layer)
            else:
                mix, rows = a_mixer_sample(xn, pos, p, layer, past)
            a_rows.append(rows)
        else:
            mix = b_mixer(xn, pos, kv, p, layer - N_A_LAYERS)
        h = h + g1 * mix
        xn = modulate(rms_norm(h, p['norm_g'][layer, 1]), sh2, sc2)
        if past is None:
            prev = jnp.zeros((B, CONV_W - 1, 2 * D_FF), h.dtype)
        else:
            prev = past['state_conv'][layer]
        f, conv_new = conv_ffn(xn, prev, p, layer)
        conv_rows.append(conv_new)
        h = h + g2 * f
        if layer == N_A_LAYERS - 1:
            if past is None:
                kv = shared_kv(h, c, pos, p, None, None)
            else:
                kv = shared_kv(h, c, pos, p, past['state_k_b'], past['state_v_b'])
    return h, a_rows, kv, conv_rows


def setup_inputs(seed: int = 0) -> dict:
    key = jax.random.key(seed)
    keys = iter(jax.random.split(key, 48))

    def nrm(shape, scale):
        return jax.random.normal(next(keys), shape, jnp.float32) * scale

    n_pages = PAST_LEN // PAGE_SIZE
    n_used = DEC_BATCH * n_pages
    n_pool = n_used + max(1, n_used // 4)
    win_buf = min(WIN_MAX, PAST_LEN)
    page_table = jax.random.permutation(next(keys), n_pool)[:n_used].reshape(DEC_BATCH, n_pages).astype(jnp.int32)
    dsc = D_MODEL ** -0.5
    return {
        'x_prompt': nrm((BATCH, SEQ, D_MODEL), 1.0),
        'x_sample': nrm((DEC_BATCH, DEC_SEQ, D_MODEL), 1.0),
        'cache_k_a': nrm((N_A_LAYERS, n_pool, PAGE_SIZE, A_KV_HEADS, HEAD_DIM), 1.0),
        'cache_v_a': nrm((N_A_LAYERS, n_pool, PAGE_SIZE, A_KV_HEADS, HEAD_DIM), 1.0),
        'cache_kidx_a': nrm((N_A_LAYERS, n_pool, PAGE_SIZE, IDX_DIM), 1.0),
        'state_k_b': nrm((DEC_BATCH, win_buf, B_KV_HEADS, HEAD_DIM), 1.0),
        'state_v_b': nrm((DEC_BATCH, win_buf, B_KV_HEADS, HEAD_DIM), 1.0),
        'state_conv': nrm((DEPTH, DEC_BATCH, CONV_W - 1, 2 * D_FF), 1.0),
        'page_table': page_table,
        'c_prompt': nrm((BATCH, D_MODEL), 1.0),
        'c_sample': nrm((DEC_BATCH, D_MODEL), 1.0),
        'ada_w': nrm((DEPTH, D_MODEL, 6 * D_MODEL), 0.5 * dsc),
        'ada_b': nrm((DEPTH, 6 * D_MODEL), 0.02),
        'norm_g': 1.0 + nrm((DEPTH, 2, D_MODEL), 0.05),
        'a_w_in': nrm((N_A_LAYERS, D_MODEL, A_IN_W), dsc),
        'a_q_g': 1.0 + nrm((N_A_LAYERS, HEAD_DIM), 0.05),
        'a_k_g': 1.0 + nrm((N_A_LAYERS, HEAD_DIM), 0.05),
        'a_ki_g': 1.0 + nrm((N_A_LAYERS, IDX_DIM), 0.05),
        'a_ki_b': nrm((N_A_LAYERS, IDX_DIM), 0.02),
        'a_w_o': nrm((N_A_LAYERS, A_Q_W, D_MODEL), A_Q_W ** -0.5),
        'kv_ada_w': nrm((D_MODEL, 2 * D_MODEL), 0.5 * dsc),
        'kv_ada_b': nrm((2 * D_MODEL,), 0.02),
        'kv_norm_g': 1.0 + nrm((D_MODEL,), 0.05),
        'kv_w': nrm((D_MODEL, 2 * B_OUT_W), dsc),
        'kv_k_g': 1.0 + nrm((HEAD_DIM,), 0.05),
        'b_w_q': nrm((N_B_LAYERS, D_MODEL, B_GROUPS * B_HEADS * HEAD_DIM), dsc),
        'b_q_g': 1.0 + nrm((N_B_LAYERS, HEAD_DIM), 0.05),
        'b_w_o': nrm((N_B_LAYERS, B_OUT_W, D_MODEL), B_OUT_W ** -0.5),
        'ffn_w_up': nrm((DEPTH, D_MODEL, 2 * D_FF), dsc),
        'ffn_conv_w': nrm((DEPTH, CONV_W, 2 * D_FF), CONV_W ** -0.5),
        'ffn_conv_b': nrm((DEPTH, 2 * D_FF), 0.02),
        'ffn_w_down': nrm((DEPTH, D_FF, D_MODEL), D_FF ** -0.5),
    }


def reference(x_prompt, x_sample, cache_k_a, cache_v_a, cache_kidx_a, state_k_b, state_v_b, state_conv,
              page_table, c_prompt, c_sample, ada_w, ada_b, norm_g, a_w_in, a_q_g, a_k_g, a_ki_g, a_ki_b,
              a_w_o, kv_ada_w, kv_ada_b, kv_norm_g, kv_w, kv_k_g, b_w_q, b_q_g, b_w_o,
              ffn_w_up, ffn_conv_w, ffn_conv_b, ffn_w_down):
    p = dict(ada_w=ada_w, ada_b=ada_b, norm_g=norm_g, a_w_in=a_w_in, a_q_g=a_q_g, a_k_g=a_k_g,
             a_ki_g=a_ki_g, a_ki_b=a_ki_b, a_w_o=a_w_o, kv_ada_w=kv_ada_w, kv_ada_b=kv_ada_b,
             kv_norm_g=kv_norm_g, kv_w=kv_w, kv_k_g=kv_k_g, b_w_q=b_w_q, b_q_g=b_q_g, b_w_o=b_w_o,
             ffn_w_up=ffn_w_up, ffn_conv_w=ffn_conv_w, ffn_conv_b=ffn_conv_b, ffn_w_down=ffn_w_down)
    past = dict(cache_k_a=cache_k_a, cache_v_a=cache_v_a, cache_kidx_a=cache_kidx_a,
                state_k_b=state_k_b, state_v_b=state_v_b, state_conv=state_conv, page_table=page_table)
    pos_p = jnp.arange(x_prompt.shape[1])
    pos_s = page_table.shape[1] * PAGE_SIZE + jnp.arange(x_sample.shape[1])

    y_prompt, rows_p, kv_p, conv_p = trunk(x_prompt, c_prompt, pos_p, p, None)
    y_sample, rows_s, kv_s, conv_s = trunk(x_sample, c_sample, pos_s, p, past)

    k_a_p = jnp.stack([r[0] for r in rows_p])
    v_a_p = jnp.stack([r[1] for r in rows_p])
    kidx_a_p = jnp.stack([r[2] for r in rows_p])
    k_a_s = jnp.stack([r[0] for r in rows_s])
    v_a_s = jnp.stack([r[1] for r in rows_s])
    kidx_a_s = jnp.stack([r[2] for r in rows_s])
    conv_new_p = jnp.stack(conv_p)
    conv_new_s = jnp.stack(conv_s)
    return (y_prompt, y_sample, k_a_p, v_a_p, kidx_a_p, kv_p[3], kv_p[4], conv_new_p,
            k_a_s, v_a_s, kidx_a_s, kv_s[3], kv_s[4], conv_new_s)
```

```python
import numpy as np
from contextlib import ExitStack
import concourse.bass as bass
import concourse.mybir as mybir
from concourse.bass_utils import run_bass_kernel_spmd

F32 = mybir.dt.float32
BF16 = mybir.dt.bfloat16
AF = mybir.ActivationFunctionType
ALU = mybir.AluOpType
AX = mybir.AxisListType

D = 2048
SEQ = 4096
NKC = 16
HD = 128
A_IN_W = 5264
EPS = 1e-6
PAST = 8192
NS = 4
TS = 4


class Buf:
    __slots__ = ("name", "w", "r", "excl")

    def __init__(self, name, excl=False):
        self.name = name
        self.w = None
        self.r = {}
        self.excl = excl


class _Rec:
    def __init__(self):
        self.call = None

    def __getattr__(self, name):
        def f(*a, **k):
            self.call = (name, a, k)
            return None
        return f


def _bind(fn):
    r = _Rec()
    fn(r)
    assert r.call is not None
    return r.call


class Sched:
    ENG = ("pe", "act", "dve", "pool", "sp")
    NDMA = 8

    def __init__(self, nc):
        self.nc = nc
        self.prog = {e: [] for e in self.ENG}
        self.cnt = {}
        self.known = {e: {} for e in self.ENG}
        self.dma_idx = {"sp": 0, "pool": 0}
        self.semnames = ["pe", "act", "dve", "pool"] + [f"sp_d{i}" for i in range(self.NDMA)] + \
                        [f"pool_d{i}" for i in range(self.NDMA)]
        for s in self.semnames:
            self.cnt[s] = 0
        self.out_tokens = []

    def _waits(self, eng, reads, writes, extra=()):
        need = {}

        def add(tok):
            if tok is None:
                return
            s, v = tok
            if need.get(s, 0) < v:
                need[s] = v
        for b in reads:
            add(b.w)
            if b.excl:
                for s, v in b.r.items():
                    if s != eng:
                        add((s, v))
        for b in writes:
            add(b.w)
            for s, v in b.r.items():
                add((s, v))
        for t in extra:
            add(t)
        res = []
        kn = self.known[eng]
        for s, v in need.items():
            if eng == "pe" and s == "pe":
                continue
            if kn.get(s, 0) >= v:
                continue
            kn[s] = v
            res.append((s, v))
        return res

    def _commit(self, tok, reads, writes):
        s, v = tok
        for b in writes:
            b.w = tok
            b.r = {}
        for b in reads:
            if b.r.get(s, 0) < v:
                b.r[s] = v

    def op(self, eng, fn, reads=(), writes=(), signal=True):
        import os
        if "E" in os.environ.get("KSKIP", ""):
            signal = True
        waits = self._waits(eng, reads, writes)
        if signal:
            self.cnt[eng] += 1
            tok = (eng, self.cnt[eng])
            self.prog[eng].append((waits, _bind(fn), (eng, 1)))
        else:
            tok = (eng, self.cnt[eng] + 1)
            self.prog[eng].append((waits, _bind(fn), None))
        self._commit(tok, reads, writes)
        return tok

    def dma(self, q, fn, reads=(), writes=(), is_output=False):
        i = self.dma_idx[q]
        self.dma_idx[q] += 1
        sem = f"{q}_d{i % self.NDMA}"
        extra = []
        if self.cnt[sem] > 0:
            extra.append((sem, self.cnt[sem]))
        waits = self._waits(q, reads, writes, extra)
        self.cnt[sem] += 16
        tok = (sem, self.cnt[sem])
        self.prog[q].append((waits, _bind(fn), (sem, 16)))
        self._commit(tok, reads, writes)
        if is_output:
            self.out_tokens.append(tok)
        return tok

    def emit(self):
        nc = self.nc
        final = [(s, self.cnt[s]) for s in self.semnames if "_d" in s and self.cnt[s] > 0]
        with ExitStack() as st:
            sems = {s: st.enter_context(nc.semaphore(s)) for s in self.semnames}
            block = st.enter_context(nc.Block())

            def run(e, prog, fin=False):
                for waits, fn, sig in prog:
                    for ws, wv in waits:
                        e.wait_ge(sems[ws], wv)
                    name, a, k = fn
                    ins = getattr(e, name)(*a, **k)
                    if sig is not None:
                        ins.then_inc(sems[sig[0]], sig[1])
                if fin:
                    for s, v in final:
                        e.wait_ge(sems[s], v)

            @block.tensor
            def _(e):
                run(e, self.prog["pe"])

            @block.scalar
            def _(e):
                run(e, self.prog["act"])

            @block.vector
            def _(e):
                run(e, self.prog["dve"])

            @block.gpsimd
            def _(e):
                run(e, self.prog["pool"])

            @block.sync
            def _(e):
                run(e, self.prog["sp"], fin=True)


def rope_tables(pos, rot_dim):
    half = rot_dim // 2
    inv = (np.float32(10000.0) ** (-np.arange(half, dtype=np.float32) / np.float32(half))).astype(np.float32)
    ang = pos.astype(np.float32)[:, None] * inv[None, :]
    cos = np.ones((128, len(pos)), np.float32)
    sin = np.zeros((128, len(pos)), np.float32)
    c = np.cos(ang).astype(np.float32).T
    s = np.sin(ang).astype(np.float32).T
    cos[:half] = c
    cos[half:rot_dim] = c
    sin[:half] = s
    sin[half:rot_dim] = s
    return cos, sin


def rot_matT(rot_dim):
    half = rot_dim // 2
    m = np.zeros((128, 128), np.float32)
    for i in range(half):
        m[i + half, i] = -1.0
        m[i, i + half] = 1.0
    return m


GT = 256
NTG = GT // 128
DFF = 5632
NFB = 44
BIG = 30000.0
WT_SCALE = float(16 ** -0.5 * 128 ** -0.5)
QK_SCALE = float(128 ** -0.5)
N_BISECT = 24
B_PAT = ((128, 1), (512, 4), (2048, 16))


def mixb_masks():
    tq = np.arange(128)[:, None]
    sk = np.arange(128)[None, :]
    out = []
    for (win, dil), rs in zip(B_PAT, ((0, 1), (0, 1, 4), (0, 1, 16))):
        for r in rs:
            delta = 128 * r + tq - sk
            valid = (delta >= 0) & (delta <= win) & (delta % dil == 0)
            out.append(np.where(valid, 0.0, -1.0).astype(np.float32))
    return np.stack(out, axis=1)


def mixb_mask_idx(g, r):
    if g == 0:
        return {0: 0, 1: 1}[r]
    if g == 1:
        return 2 if r == 0 else (4 if r == 4 else 3)
    return 5 if r == 0 else (7 if r == 16 else 6)


import os
SKIP = set(os.environ.get("KSKIP", "").split(","))


def build_program(NG=16, stage=9, with_sample=False, kb_row0=2048, debug=False):
    nc = bass.Bass("TRN2", target_bir_lowering=False)

    def din(name, shape, dt=F32):
        return nc.dram_tensor(name, list(shape), dt, kind="ExternalInput").ap()

    def dout(name, shape, dt=F32):
        return nc.dram_tensor(name, list(shape), dt, kind="ExternalOutput").ap()

    def dint(name, shape, dt=F32):
        kind = "ExternalOutput" if (debug and name in ("hmid0", "h1d", "hmid1")) else "Internal"
        return nc.dram_tensor(name, list(shape), dt, kind=kind).ap()

    xp = din("xp", [SEQ, D])
    cT = din("cT", [128, NKC, 8])
    ada_w = din("ada_w", [2, D, 6 * D])
    kv_ada_w = din("kv_ada_w", [D, 2 * D])
    biasF = din("biasF", [128, 10, NKC])
    gate_b = din("gate_b", [4, D])
    norm_gT = din("norm_gT", [128, 5, NKC])
    a_g = din("a_g", [128, 8])
    consts = din("consts", [128, 5, 128])
    constb = din("constb", [128, 1920])
    cs_p = din("cs_p", [4, 128, SEQ])
    convw = din("convw", [128, 2, 88, 4])
    w_in = {"a_w_in": din("a_w_in", [D, A_IN_W]), "a_w_o": din("a_w_o", [D, D]), "kv_w": din("kv_w", [D, D]),
            "b_w_q": din("b_w_q", [D, 3072]), "b_w_o": din("b_w_o", [1024, D]),
            "up0": din("up0", [D, 2 * DFF]), "up1": din("up1", [D, 2 * DFF]),
            "dn0": din("dn0", [DFF, D]), "dn1": din("dn1", [DFF, D])}

    y_p = dout("y_p", [SEQ, D]) if stage >= 6 else None
    k_a_p = dout("k_a_p", [SEQ, 512])
    v_a_p = dout("v_a_p", [SEQ, 512])
    kidx_a_p = dout("kidx_a_p", [SEQ, 128])
    k_b_p = dout("k_b_p", [2048, 1024]) if stage >= 4 else None
    v_b_p = dout("v_b_p", [2048, 1024]) if stage >= 4 else None
    conv_p = dout("conv_p", [2, 2, 2 * DFF]) if stage >= 3 else None

    need_keys = ["a_w_in"] + (["a_w_o"] if stage >= 2 else []) + (["up0", "dn0"] if stage >= 3 else []) + \
        (["kv_w"] if stage >= 4 else []) + (["b_w_q", "b_w_o"] if stage >= 5 else []) + (["up1", "dn1"] if stage >= 6 else [])
    I32 = mybir.dt.int32
    if with_sample:
        xs = din("xs", [GT, D]); b_xs = Buf("xs")
        cs_s = din("cs_s", [4, 128, GT])
        ptab = din("ptab", [NS, 64], I32)
        ck = din("ck", [2560 * 128, 512]); cv = din("cv", [2560 * 128, 512]); cki = din("cki", [2560 * 128, 128])
        skb = din("skb", [NS, 2048, 1024]); svb = din("svb", [NS, 2048, 1024])
        sconv = din("sconv", [2, NS * 2, 2 * DFF])
        smask = din("smask", [128, 24, 128])
        cnegn = din("cnegn", [128, 128])
        pidx = din("pidx", [128, 1])
        y_s = dout("y_s", [GT, D]); b_y_s = Buf("y_s")
        k_a_s = dout("k_a_s", [GT, 512]); v_a_s = dout("v_a_s", [GT, 512]); kidx_a_s = dout("kidx_a_s", [GT, 128])
        k_b_s = dout("k_b_s", [NS, 2048, 1024]); v_b_s = dout("v_b_s", [NS, 2048, 1024])
        conv_s = dout("conv_s", [2, NS * 2, 2 * DFF])
        hs0 = dint("hs0", [GT, D]); b_hs0 = Buf("hs0")
        hs1 = dint("hs1", [GT, D]); b_hs1 = Buf("hs1")
        hs2 = dint("hs2", [GT, D]); b_hs2 = Buf("hs2")
    SM = {"on": False}
    wb = {k: dint("wb_" + k, v.shape, BF16) for k, v in w_in.items() if k in need_keys}
    b_wb = {k: Buf("wb_" + k) for k in w_in}
    NTOK = NG * GT
    hmid0 = dint("hmid0", [NTOK, D]) if stage >= 2 else None; b_hmid0 = Buf("hmid0")
    h1d = dint("h1d", [NTOK, D]) if stage >= 3 else None; b_h1d = Buf("h1d")
    hmid1 = dint("hmid1", [NTOK, D]) if stage >= 5 else None; b_hmid1 = Buf("hmid1")
    kTd = dint("kTd", [4, 128, SEQ], BF16); b_kTd = Buf("kTd")
    vd = dint("vd", [SEQ, 512], BF16) if ("V" not in SKIP and "B" not in SKIP) else None; b_vd = Buf("vd")
    kbTd = dint("kbTd", [8, 128, SEQ], BF16) if stage >= 4 else None; b_kbTd = Buf("kbTd")
    vbd = dint("vbd", [SEQ, 1024], BF16) if stage >= 4 else None; b_vbd = Buf("vbd")
    modR = dint("modR", [4, 8, D]); b_modR = Buf("modR")
    b_xp = Buf("xp")

    S = Sched(nc)
    st = ExitStack()

    def sb(name, shape, dt=F32):
        return st.enter_context(nc.sbuf_tensor(name, list(shape), dt))

    cst = sb("cst", [128, 5, 128]); b_cst = Buf("cst")
    cstb = sb("cstb", [128, 1920], BF16); b_cstb = Buf("cstb")
    cTs = sb("cTs", [128, NKC, 8]); b_cTs = Buf("cTs")
    silc = sb("silc", [128, NKC, 8]); b_silc = Buf("silc")
    bF = sb("bF", [128, 10, NKC]); b_bF = Buf("bF")
    ngT = sb("ngT", [128, 5, NKC]); b_ngT = Buf("ngT")
    ag = sb("ag", [128, 8]); b_ag = Buf("ag")
    modF = sb("modF", [128, 10, NKC, 8]); b_modF = Buf("modF")
    modA = sb("modA", [128, 5, NKC, 8]); b_modA = Buf("modA")
    cw = sb("cw", [128, 2, 88, 4]); b_cw = Buf("cw")
    halo = sb("halo", [128, 2, 88, 2]); b_halo = Buf("halo")
    kiT = sb("kiT", [128, SEQ], BF16); b_kiT = Buf("kiT")
    wsm = sb("wsm", [128, NKC, 144], BF16); b_wsm = Buf("wsm")
    xnT = sb("xnT", [128, NKC, GT], BF16); b_xnT = Buf("xnT")
    cs = sb("cs", [128, 4, GT]); b_cs = Buf("cs")
    wbuf = [sb(f"wbuf{i}", [128, 11264], BF16) for i in range(2)]; b_wbuf = [Buf(f"wbuf{i}") for i in range(2)]
    xt = [sb(f"xt{i}", [128, D]) for i in range(2)]; b_xt = [Buf(f"xt{i}") for i in range(2)]
    xsb = sb("xsb", [128, D], BF16); b_xsb = Buf("xsb")
    gb = sb("gb", [128, D]); b_gb = Buf("gb")
    stat = sb("stat", [128, 8]); b_stat = Buf("stat")
    sq = sb("sq", [128, GT]); b_sq = Buf("sq")
    sq2 = sb("sq2", [128, GT]); b_sq2 = Buf("sq2")
    rstd = sb("rstd", [128, GT]); b_rstd = Buf("rstd")
    mean = sb("mean", [128, GT]); b_mean = Buf("mean")
    kn = sb("kn", [128, GT]); b_kn = Buf("kn")
    t1 = sb("t1", [128, GT]); b_t1 = Buf("t1")
    kr = sb("kr", [128, GT]); b_kr = Buf("kr")
    krb = sb("krb", [128, GT], BF16); b_krb = Buf("krb")
    otok = [sb(f"otok{i}", [128, 512]) for i in range(2)]; b_otok = [Buf(f"otok{i}") for i in range(2)]
    otb = [sb(f"otb{i}", [128, 512], BF16) for i in range(2)]; b_otb = [Buf(f"otb{i}") for i in range(2)]
    slabAB = sb("slabAB", [128, 6144]); b_A = Buf("slabA"); b_B = Buf("slabB")
    slabA = slabAB[:, 0:4096]
    slabB = slabAB[:, 4096:6144].bitcast(BF16)
    slabC = sb("slabC", [128, 4096], BF16); b_C = Buf("slabC")
    slabD = sb("slabD", [128, 4096], BF16); b_D = Buf("slabD")
    if "F" in SKIP:
        sb_real = sb
        def sb(name, shape, dt=F32):
            return sb_real(name, [128, 8], dt)
    oT = sb("oT", [128, 16, GT], BF16); b_oT = Buf("oT")
    rl = [sb(f"rl{i}", [128, 512]) for i in range(2)]; b_rl = [Buf(f"rl{i}") for i in range(2)]
    PT = [sb(f"PT{i}", [128, 512], BF16) for i in range(2)]; b_PT = [Buf(f"PT{i}") for i in range(2)]
    kst = [sb(f"kst{i}", [128, 2176], BF16) for i in range(2)]; b_kst = [Buf(f"kst{i}") for i in range(2)]
    vst = [sb(f"vst{i}", [128, 17, 128], BF16) for i in range(2)]; b_vst = [Buf(f"vst{i}") for i in range(2)]
    rec = sb("rec", [128, 512]); b_rec = Buf("rec")
    if "F" in SKIP:
        sb = sb_real
    wtt = sb("wtt", [128, NTG, 16]); b_wtt = Buf("wtt")
    bis = sb("bis", [128, 16]); b_bis = Buf("bis")

    I_ = slabA
    silcP = slabC[:, :].bitcast(F32).rearrange("p (k m) -> p k m", m=128); b_silcP = b_C
    cstf = slabA[:, 0:1920]; b_cstf = b_A
    mb = slabB
    qT = slabC[:, :].rearrange("p (h t) -> p h t", t=GT)
    qiT = slabD[:, :].rearrange("p (h t) -> p h t", t=GT)
    actT_a = slabAB[:, :].bitcast(BF16)
    slabCf = slabC[:, :].bitcast(F32)
    ua = [slabCf[:, i * 260:i * 260 + GT + 2] for i in range(2)]
    ub = [slabCf[:, 520 + i * 260:520 + i * 260 + GT + 2] for i in range(2)]
    ya = slabCf[:, 1040:1040 + GT]
    yb = slabCf[:, 1300:1300 + GT]
    sa = slabCf[:, 1560:1560 + GT]

    SV = {}

    def alias_buf(name, olds):
        nb_ = Buf(name)
        for o in olds:
            if o.w is not None:
                nb_.r[o.w[0]] = max(nb_.r.get(o.w[0], 0), o.w[1])
            for k_, v_ in o.r.items():
                nb_.r[k_] = max(nb_.r.get(k_, 0), v_)
        return nb_

    pbank = [st.enter_context(nc.psum_tensor(f"pb{i}", [128, 512], F32)) for i in range(4)]
    pbank += [st.enter_context(nc.psum_tensor(f"pb{i}", [128, 1024], BF16)) for i in (4, 5)]
    pbank += [st.enter_context(nc.psum_tensor(f"pb{i}", [128, 512], F32)) for i in (6, 7)]
    b_pb = [Buf(f"pb{i}", excl=True) for i in range(8)]
    rr = {"acc": 0, "aux": 0, "tr": 0, "tr_f32": 0, "to": 0, "w": 0, "x": 0, "o": 0, "rl": 0, "pt": 0, "kv": 0, "u": 0}

    def bank(role):
        base = {"acc": 0, "aux": 2, "tr": 4, "tr_f32": 4, "to": 6}[role]
        i = base + rr[role] % 2
        rr[role] += 1
        if role == "tr_f32":
            return pbank[i][:, :].bitcast(F32), b_pb[i]
        return pbank[i], b_pb[i]

    def nxt(role, n=2):
        i = rr[role] % n
        rr[role] += 1
        return i

    ident = cst[:, 0, :]
    ones = cst[:, 1, :]
    R128T = cst[:, 2, :]
    R64T = cst[:, 3, :]
    cneg = cst[:, 4, :]
    identb = cstb[:, 0:128]
    onesb = cstb[:, 128:256]
    bigI4 = cstb[:, 256:768]
    mBm = cstb[:, 768:1792].rearrange("p (m k) -> p m k", k=128)
    zerosb = cstb[:, 1792:1920]

    S.dma("sp", lambda e: e.dma_start(out=cst[:], in_=consts), writes=[b_cst])
    S.dma("sp", lambda e: e.dma_start(out=cstf, in_=constb), writes=[b_cstf])
    S.dma("sp", lambda e: e.dma_start(out=cTs[:], in_=cT), writes=[b_cTs])
    S.dma("sp", lambda e: e.dma_start(out=bF[:], in_=biasF), writes=[b_bF])
    S.dma("sp", lambda e: e.dma_start(out=ngT[:], in_=norm_gT), writes=[b_ngT])
    S.dma("sp", lambda e: e.dma_start(out=ag[:], in_=a_g), writes=[b_ag])
    S.dma("sp", lambda e: e.dma_start(out=cw[:], in_=convw), writes=[b_cw])
    S.op("dve", lambda e: e.tensor_copy(cstb[:], cstf), reads=[b_cstf], writes=[b_cstb])
    S.op("act", lambda e: e.activation(silc[:], cTs[:], AF.Silu), reads=[b_cTs], writes=[b_silc])
    S.op("dve", lambda e: e.memset(halo[:], 0.0), writes=[b_halo])
    S.op("dve", lambda e: e.memset(silcP, 0.0), writes=[b_silcP])
    S.op("dve", lambda e: e.tensor_copy(silcP[:, :, 0:8], silc[:]), reads=[b_silc], writes=[b_silcP])

    touch = sb("touch", [1, 96]); b_touch = Buf("touch")
    if with_sample:
        all_extra = []
    else:
        all_extra = []
    all_in = all_extra + [xp, cT, ada_w, kv_ada_w, biasF, gate_b, norm_gT, a_g, consts, constb, cs_p, convw] + list(w_in.values())
    for ti, ap_ in enumerate(all_in):
        idx = tuple([0] * (len(ap_.shape) - 1) + [slice(0, 2)])
        S.dma("sp", lambda e, ap_=ap_, idx=idx, ti=ti: e.dma_start(out=touch[0:1, 2 * ti:2 * ti + 2], in_=ap_[idx].unsqueeze(0) if len(ap_[idx].shape) == 1 else ap_[idx]),
              writes=[b_touch])

    def cast2d(key, bcols, rows_per):
        src, dst = w_in[key], wb[key]
        R_ = src.shape[0]
        s3 = src.rearrange("r (a b) -> r a b", b=bcols)
        d3 = dst.rearrange("r (a b) -> r a b", b=bcols)
        for r0 in range(0, R_, rows_per):
            S.dma("pool", lambda e, r0=r0: e.dma_start(out=d3[r0:r0 + rows_per], in_=s3[r0:r0 + rows_per]),
                  writes=[b_wb[key]])
    cast2d("a_w_in", 329, 128)
    if stage >= 2:
        cast2d("a_w_o", 1024, 512)
    if stage >= 3:
        cast2d("up0", 1024, 128)
        cast2d("dn0", 1024, 512)
    if stage >= 4:
        cast2d("kv_w", 1024, 512)
    if stage >= 5:
        cast2d("b_w_q", 1024, 256)
        cast2d("b_w_o", 1024, 512)
    if stage >= 6:
        cast2d("up1", 1024, 128)
        cast2d("dn1", 1024, 512)

    fm_list = [(ada_w[0], 0, 0), (ada_w[0], 2048, 1), (ada_w[0], 6144, 2), (ada_w[0], 8192, 3)]
    gate_list = [(ada_w[0], 4096, 0), (ada_w[0], 10240, 1)]
    if stage >= 4:
        fm_list += [(kv_ada_w, 0, 8), (kv_ada_w, 2048, 9)]
    if stage >= 5:
        fm_list += [(ada_w[1], 0, 4), (ada_w[1], 2048, 5), (ada_w[1], 6144, 6), (ada_w[1], 8192, 7)]
        gate_list += [(ada_w[1], 4096, 2), (ada_w[1], 10240, 3)]
    if "F" in SKIP:
        adaw_t = [sb(f"adaw{i}", [128, NKC, 256]) for i in range(2)]
        adaw_v = [adaw_t[i][:, :, :] for i in range(2)]
    else:
        adaw_v = [wbuf[i][:, :].bitcast(F32)[:, 0:4096].rearrange("p (kc c) -> p kc c", c=256) for i in range(2)]
    for (W, col0, m) in fm_list:
        for c4 in range(8):
            wi = nxt("w"); wv_, bw = adaw_v[wi], b_wbuf[wi]
            src = W[:, col0 + c4 * 256: col0 + (c4 + 1) * 256].rearrange("(kc p) c -> p kc c", p=128)
            S.dma("sp", lambda e, wv_=wv_, src=src: e.dma_start(out=wv_, in_=src), writes=[bw])
            pa, bpa = bank("acc")
            for j in range(2):
                for kc in range(NKC):
                    S.op("pe", lambda e, pa=pa, wv_=wv_, j=j, kc=kc: e.matmul(
                        pa[:, j * 8:(j + 1) * 8], wv_[:, kc, j * 128:(j + 1) * 128], silc[:, kc, :],
                        start=(kc == 0), stop=(kc == NKC - 1)), reads=[bw, b_silc], writes=[bpa], signal=(kc == NKC - 1))
            for j in range(2):
                blk = c4 * 2 + j
                S.op("dve", lambda e, pa=pa, j=j, blk=blk, m=m: e.tensor_scalar(
                    modF[:, m, blk, :], pa[:, j * 8:(j + 1) * 8], bF[:, m, blk:blk + 1], None, ALU.add),
                    reads=[bpa, b_bF], writes=[b_modF])
    gbias = gb[0:8, 0:256]
    if "A" in SKIP:
        gate_list = []
    for (W, col0, gidx) in gate_list:
        for c4 in range(8):
            wi = nxt("w"); wv_, bw = adaw_v[wi], b_wbuf[wi]
            src = W[:, col0 + c4 * 256: col0 + (c4 + 1) * 256].rearrange("(kc p) c -> p kc c", p=128)
            S.dma("sp", lambda e, wv_=wv_, src=src: e.dma_start(out=wv_, in_=src), writes=[bw])
            S.dma("sp", lambda e, gidx=gidx, c4=c4: e.dma_start(
                out=gbias, in_=gate_b[gidx:gidx + 1, c4 * 256:(c4 + 1) * 256].partition_broadcast(8)),
                writes=[b_gb])
            pa, bpa = bank("acc")
            for kc in range(NKC):
                S.op("pe", lambda e, pa=pa, wv_=wv_, kc=kc: e.matmul(
                    pa[:, 0:256], silcP[:, kc, :], wv_[:, kc, :], start=(kc == 0), stop=(kc == NKC - 1)),
                    reads=[bw, b_silcP], writes=[bpa], signal=(kc == NKC - 1))
            oi = nxt("o"); ot, bot = otok[oi], b_otok[oi]
            S.op("dve", lambda e, ot=ot, pa=pa: e.tensor_tensor(ot[0:8, 0:256], pa[0:8, 0:256], gbias, ALU.add),
                 reads=[bpa, b_gb], writes=[bot])
            S.dma("pool", lambda e, ot=ot, gidx=gidx, c4=c4: e.dma_start(
                out=modR[gidx, :, c4 * 256:(c4 + 1) * 256], in_=ot[0:8, 0:256]), reads=[bot], writes=[b_modR])

    def make_modA(n, m_scale):
        for kc in range(NKC):
            S.op("dve", lambda e, kc=kc: e.tensor_scalar(
                modA[:, n, kc, :], modF[:, m_scale, kc, :], 1.0, ngT[:, n, kc:kc + 1], ALU.add, ALU.mult),
                reads=[b_modF, b_ngT], writes=[b_modA])
    NORM_MOD = {0: (0, 1), 1: (2, 3), 2: (4, 5), 3: (6, 7), 4: (8, 9)}
    make_modA(0, 1); make_modA(1, 3)
    if stage >= 4:
        make_modA(4, 9)
    if stage >= 5:
        make_modA(2, 5); make_modA(3, 7)

    S.dma("sp", lambda e: e.dma_start(out=wsm[:, :, :],
                                      in_=wb["a_w_in"][:, 5120:5264].rearrange("(kc p) c -> p kc c", p=128)),
          reads=[b_wb["a_w_in"]], writes=[b_wsm])

    def load_w(key, r0, nrows, c0, ncols):
        wi = nxt("w"); wt_, bw = wbuf[wi], b_wbuf[wi]
        nk = nrows // 128
        view = wt_[:, 0:nk * ncols].rearrange("p (kc c) -> p kc c", c=ncols)
        src = wb[key][r0:r0 + nrows, c0:c0 + ncols].rearrange("(kc p) c -> p kc c", p=128)
        S.dma("sp", lambda e: e.dma_start(out=view, in_=src), reads=[b_wb[key]], writes=[bw])
        return view, bw

    def load_norm_T(src_dram, b_src, tok0, n_idx, seq=0):
        m_shift, _ = NORM_MOD[n_idx]
        for j in range(NTG):
            i = nxt("x"); x_, bx = xt[i], b_xt[i]
            S.dma("sp", lambda e, x_=x_, j=j: e.dma_start(out=x_[:, :], in_=src_dram[tok0 + j * 128: tok0 + (j + 1) * 128, :]),
                  reads=[b_src], writes=[bx])
            S.op("act", lambda e, x_=x_: e.activation(xsb[:, :], x_[:, :], AF.Square, accum_out=stat[:, 0:1]),
                 reads=[bx], writes=[b_xsb, b_stat])
            S.op("act", lambda e: e.activation(stat[:, 1:2], stat[:, 0:1], AF.Sqrt, bias=EPS, scale=1.0 / D),
                 reads=[b_stat], writes=[b_stat])
            S.op("dve", lambda e: e.reciprocal(stat[:, 2:3], stat[:, 1:2]), reads=[b_stat], writes=[b_stat])
            S.op("act", lambda e, x_=x_: e.activation(xsb[:, :], x_[:, :], AF.Copy, scale=stat[:, 2:3]),
                 reads=[bx, b_stat], writes=[b_xsb])
            for half in range(2):
                pt_, bpt = bank("tr")
                ptb = pt_
                for k8 in range(8):
                    kc = half * 8 + k8
                    S.op("pe", lambda e, ptb=ptb, kc=kc, k8=k8: e.transpose(
                        ptb[:, k8 * 128:(k8 + 1) * 128], xsb[:, kc * 128:(kc + 1) * 128], identb),
                        reads=[b_xsb, b_cstb], writes=[bpt])
                if SM["on"] and j == 0:
                    ranges = [(0, 4, 1), (4, 8, 2), (8, 12, 3), (12, 16, 4), (16, 128, 1)]
                elif SM["on"]:
                    ranges = [(0, 128, 1)]
                else:
                    ranges = [(0, 128, seq)]
                for k8 in range(8):
                    kc = half * 8 + k8
                    for (c0, c1, sq_) in ranges:
                        S.op("dve", lambda e, ptb=ptb, kc=kc, k8=k8, j=j, c0=c0, c1=c1, sq_=sq_: e.tensor_scalar(
                            xnT[:, kc, j * 128 + c0:j * 128 + c1], ptb[:, k8 * 128 + c0:k8 * 128 + c1],
                            modA[:, n_idx, kc, sq_:sq_ + 1], modF[:, m_shift, kc, sq_:sq_ + 1], ALU.mult, ALU.add),
                            reads=[bpt, b_modA, b_modF], writes=[b_xnT])

    def proj_ws(wview, bw, col_lo, nkc=NKC, rhs_fn=None, brhs=None):
        pa, bpa = bank("acc")
        for kc in range(nkc):
            rhs = xnT[:, kc, :] if rhs_fn is None else rhs_fn(kc)
            S.op("pe", lambda e, pa=pa, kc=kc, rhs=rhs: e.matmul(pa[:, 0:GT], wview[:, kc, col_lo:col_lo + 128], rhs,
                                                                start=(kc == 0), stop=(kc == nkc - 1)),
                 reads=[bw, b_xnT if brhs is None else brhs], writes=[bpa], signal=(kc == nkc - 1))
        return pa, bpa

    def bcast_sum(src_ap, bsrc):
        pa, bpa = bank("aux")
        S.op("pe", lambda e: e.matmul(pa[:, 0:GT], ones, src_ap, start=True, stop=True),
             reads=[bsrc, b_cst], writes=[bpa])
        return pa, bpa

    def rms_head(pa, bpa, gcol):
        S.op("act", lambda e: e.activation(sq[:, :], pa[:, 0:GT], AF.Square), reads=[bpa], writes=[b_sq])
        pq, bpq = bcast_sum(sq[:, :], b_sq)
        S.op("act", lambda e: e.activation(rstd[:, :], pq[:, 0:GT], AF.Sqrt, bias=EPS, scale=1.0 / HD),
             reads=[bpq], writes=[b_rstd])
        S.op("dve", lambda e: e.reciprocal(rstd[:, :], rstd[:, :]), reads=[b_rstd], writes=[b_rstd])
        S.op("dve", lambda e: e.scalar_tensor_tensor(kn[:, :], pa[:, 0:GT], ag[:, gcol:gcol + 1], rstd[:, :],
                                                    ALU.mult, ALU.mult),
             reads=[bpa, b_ag, b_rstd], writes=[b_kn])

    def ln_head(pa, bpa):
        S.op("act", lambda e: e.activation(sq2[:, :], pa[:, 0:GT], AF.Copy), reads=[bpa], writes=[b_sq2])
        S.op("act", lambda e: e.activation(sq[:, :], pa[:, 0:GT], AF.Square), reads=[bpa], writes=[b_sq])
        p1, bp1 = bcast_sum(sq2[:, :], b_sq2)
        p2, bp2 = bcast_sum(sq[:, :], b_sq)
        S.op("act", lambda e: e.activation(mean[:, :], p1[:, 0:GT], AF.Copy, scale=1.0 / HD),
             reads=[bp1], writes=[b_mean])
        S.op("dve", lambda e: e.tensor_tensor(t1[:, :], mean[:, :], mean[:, :], ALU.mult),
             reads=[b_mean], writes=[b_t1])
        S.op("dve", lambda e: e.scalar_tensor_tensor(rstd[:, :], p2[:, 0:GT], 1.0 / HD, t1[:, :],
                                                    ALU.mult, ALU.subtract),
             reads=[bp2, b_t1], writes=[b_rstd])
        S.op("act", lambda e: e.activation(rstd[:, :], rstd[:, :], AF.Sqrt, bias=EPS, scale=1.0),
             reads=[b_rstd], writes=[b_rstd])
        S.op("dve", lambda e: e.reciprocal(rstd[:, :], rstd[:, :]), reads=[b_rstd], writes=[b_rstd])
        S.op("dve", lambda e: e.tensor_tensor(kn[:, :], sq2[:, :], mean[:, :], ALU.subtract),
             reads=[b_sq2, b_mean], writes=[b_kn])
        S.op("dve", lambda e: e.tensor_tensor(kn[:, :], kn[:, :], rstd[:, :], ALU.mult),
             reads=[b_kn, b_rstd], writes=[b_kn])
        S.op("dve", lambda e: e.tensor_scalar(kn[:, :], kn[:, :], ag[:, 2:3], ag[:, 3:4], ALU.mult, ALU.add),
             reads=[b_kn, b_ag], writes=[b_kn])

    def rope(src, bsrc, cidx, RT, dst_bf=None, bdst=None, out_fn=None):
        if bsrc in b_pb:
            S.op("act", lambda e: e.activation(kn[:, :], src, AF.Copy), reads=[bsrc], writes=[b_kn])
            src, bsrc = kn[:, :], b_kn
        pa, bpa = bank("aux")
        S.op("pe", lambda e: e.matmul(pa[:, 0:GT], RT, src, start=True, stop=True),
             reads=[bsrc, b_cst], writes=[bpa])
        S.op("dve", lambda e: e.tensor_tensor(t1[:, :], src, cs[:, cidx, :], ALU.mult),
             reads=[bsrc, b_cs], writes=[b_t1])
        S.op("dve", lambda e: e.tensor_tensor(kr[:, :], pa[:, 0:GT], cs[:, cidx + 1, :], ALU.mult),
             reads=[bpa, b_cs], writes=[b_kr])
        if dst_bf is not None and out_fn is None:
            S.op("dve", lambda e: e.tensor_tensor(dst_bf, kr[:, :], t1[:, :], ALU.add),
                 reads=[b_kr, b_t1], writes=bdst)
            return
        S.op("dve", lambda e: e.tensor_tensor(kr[:, :], kr[:, :], t1[:, :], ALU.add),
             reads=[b_kr, b_t1], writes=[b_kr])
        if dst_bf is not None:
            S.op("act", lambda e: e.activation(dst_bf, kr[:, :], AF.Copy), reads=[b_kr], writes=bdst)
        if out_fn is not None:
            po, bpo = bank("to")
            for j in range(NTG):
                S.op("pe", lambda e, j=j: e.transpose(po[:, j * 128:(j + 1) * 128], kr[:, j * 128:(j + 1) * 128], ident),
                     reads=[b_kr, b_cst], writes=[bpo])
            oi = nxt("o"); ot, bot = otok[oi], b_otok[oi]
            S.op("act", lambda e: e.activation(ot[:, 0:GT], po[:, 0:GT], AF.Copy), reads=[bpo], writes=[bot])
            for j in range(NTG):
                out_fn(j, ot[:, j * 128:(j + 1) * 128], bot)

    def out_dma(dst_ap, src_ap, bsrc, bdst=None):
        S.dma("pool", lambda e: e.dma_start(out=dst_ap, in_=src_ap), reads=[bsrc],
              writes=[] if bdst is None else [bdst], is_output=True)

    def l0_project(G):
        tok0 = G * GT
        smode = SM["on"]
        if smode:
            csrc, xsrc, bxsrc, k_a_o, v_a_o, ki_a_o = cs_s, xs, b_xs, k_a_s, v_a_s, kidx_a_s
        else:
            csrc, xsrc, bxsrc, k_a_o, v_a_o, ki_a_o = cs_p, xp, b_xp, k_a_p, v_a_p, kidx_a_p
        S.dma("sp", lambda e: e.dma_start(out=cs[:, :, :], in_=csrc[:, :, tok0:tok0 + GT].rearrange("a p t -> p a t")),
              writes=[b_cs])
        load_norm_T(xsrc, bxsrc, tok0, 0)
        KSTOP = os.environ.get("KSTOP", "")
        if KSTOP == "a":
            return
        wv_, bw = load_w("a_w_in", 0, D, 2048, 512)
        for g in range(4):
            pa, bpa = proj_ws(wv_, bw, g * 128)
            rms_head(pa, bpa, 1)

            def k_out(j, ap_, b_, g=g):
                out_dma(k_a_o[tok0 + j * 128: tok0 + (j + 1) * 128, g * 128:(g + 1) * 128], ap_, b_)
            rope(kn[:, :], b_kn, 0, R128T, krb[:, :], [b_krb], k_out)
            if smode:
                S.op("act", lambda e, g=g: e.activation(SV["kTn"][:, g, :], krb[:, 0:128], AF.Copy),
                     reads=[b_krb], writes=[SV["b_kTn"]])
            else:
                S.dma("pool", lambda e, g=g: e.dma_start(out=kTd[g, :, tok0:tok0 + GT], in_=krb[:, :]),
                      reads=[b_krb], writes=[b_kTd])
        if KSTOP == "b":
            return
        pa, bpa = proj_ws(wsm, b_wsm, 0)
        ln_head(pa, bpa)

        def ki_out(j, ap_, b_):
            out_dma(ki_a_o[tok0 + j * 128: tok0 + (j + 1) * 128, :], ap_, b_)
        if smode:
            rope(kn[:, :], b_kn, 2, R64T, kiT[:, 3584:3584 + GT], [b_kiT], ki_out)
        else:
            rope(kn[:, :], b_kn, 2, R64T, kiT[:, tok0:tok0 + GT], [b_kiT], ki_out)
        if KSTOP == "c":
            return
        wv_, bw = load_w("a_w_in", 0, D, 2560, 512)
        for j in range(NTG):
            pa, bpa = bank("to")
            for kc in range(NKC):
                S.op("pe", lambda e, pa=pa, kc=kc, j=j: e.matmul(pa[:, :], xnT[:, kc, j * 128:(j + 1) * 128], wv_[:, kc, :],
                                                             start=(kc == 0), stop=(kc == NKC - 1)),
                     reads=[b_xnT, bw], writes=[bpa], signal=(kc == NKC - 1))
            oi = nxt("o"); ot, bot, ob, bob = otok[oi], b_otok[oi], otb[oi], b_otb[oi]
            S.op("act", lambda e, ot=ot, pa=pa: e.activation(ot[:, :], pa[:, :], AF.Copy), reads=[bpa], writes=[bot])
            if "V" not in SKIP:
                S.op("dve", lambda e, ob=ob, pa=pa: e.tensor_copy(ob[:, :], pa[:, :]), reads=[bpa], writes=[bob])
            out_dma(v_a_o[tok0 + j * 128: tok0 + (j + 1) * 128, :], ot[:, :], bot)
            if smode:
                if j == 0:
                    S.op("pool", lambda e, ob=ob: e.tensor_copy(SV["vnew"][:, :], ob[:, :]), reads=[bob], writes=[SV["b_vnew"]])
            else:
                S.dma("pool", lambda e, ob=ob, j=j: e.dma_start(out=vd[tok0 + j * 128: tok0 + (j + 1) * 128, :], in_=ob[:, :]),
                      reads=[bob], writes=[b_vd])
            if stage >= 2:
                pa, bpa = bank("acc")
                for kc in range(NKC):
                    S.op("pe", lambda e, pa=pa, kc=kc, j=j: e.matmul(pa[:, 0:16], xnT[:, kc, j * 128:(j + 1) * 128],
                                                                 wsm[:, kc, 128:144],
                                                                 start=(kc == 0), stop=(kc == NKC - 1)),
                         reads=[b_xnT, b_wsm], writes=[bpa], signal=(kc == NKC - 1))
                S.op("act", lambda e, pa=pa, j=j: e.activation(wtt[:, j, :], pa[:, 0:16], AF.Copy, scale=WT_SCALE),
                     reads=[bpa], writes=[b_wtt])
        if stage < 2:
            return
        for c4 in range(4):
            wv_, bw = load_w("a_w_in", 0, D, c4 * 512, 512)
            for hh in range(4):
                h = c4 * 4 + hh
                pa, bpa = proj_ws(wv_, bw, hh * 128)
                rms_head(pa, bpa, 0)
                rope(kn[:, :], b_kn, 0, R128T, qT[:, h, :], [b_C])
        for c4 in range(4):
            wv_, bw = load_w("a_w_in", 0, D, 3072 + c4 * 512, 512)
            for hh in range(4):
                h = c4 * 4 + hh
                pa, bpa = proj_ws(wv_, bw, hh * 128)
                rope(pa[:, 0:GT], bpa, 2, R64T, qiT[:, h, :], [b_D])

    def l0_attention(G):
        tok0 = G * GT
        for j in range(NTG):
            i = G * NTG + j
            nk = 128 * (i + 1)
            nb = i + 1
            for c0 in range(0, nk, 512):
                c1 = min(nk, c0 + 512)
                for h in range(16):
                    pa, bpa = bank("acc")
                    S.op("pe", lambda e, pa=pa, h=h, c0=c0, c1=c1: e.matmul(
                        pa[:, 0:c1 - c0], qiT[:, h, j * 128:(j + 1) * 128], kiT[:, c0:c1], start=True, stop=True),
                        reads=[b_D, b_kiT], writes=[bpa])
                    ri = nxt("rl"); r_, br = rl[ri], b_rl[ri]
                    S.op("act", lambda e, pa=pa, r_=r_, c0=c0, c1=c1: e.activation(r_[:, 0:c1 - c0], pa[:, 0:c1 - c0], AF.Relu),
                         reads=[bpa], writes=[br])
                    if h == 0:
                        S.op("dve", lambda e, r_=r_, c0=c0, c1=c1: e.tensor_scalar(
                            I_[:, c0:c1], r_[:, 0:c1 - c0], wtt[:, j, 0:1], None, ALU.mult),
                            reads=[br, b_wtt], writes=[b_A])
                    else:
                        S.op("dve", lambda e, r_=r_, h=h, c0=c0, c1=c1: e.scalar_tensor_tensor(
                            I_[:, c0:c1], r_[:, 0:c1 - c0], wtt[:, j, h:h + 1], I_[:, c0:c1], ALU.mult, ALU.add),
                            reads=[br, b_wtt, b_A], writes=[b_A])
            S.op("dve", lambda e: e.tensor_reduce(bis[:, 0:1], I_[:, 0:nk], AX.X, ALU.min), reads=[b_A], writes=[b_bis])
            S.op("dve", lambda e: e.tensor_tensor(I_[:, nk - 128:nk], I_[:, nk - 128:nk], cneg, ALU.add),
                 reads=[b_A, b_cst], writes=[b_A])
            S.op("dve", lambda e: e.tensor_scalar(bis[:, 0:1], bis[:, 0:1], -1.0, None, ALU.add),
                 reads=[b_bis], writes=[b_bis])
            if i >= 2:
                S.op("dve", lambda e: e.tensor_reduce(bis[:, 1:2], I_[:, 0:nk], AX.X, ALU.max), reads=[b_A], writes=[b_bis])
                S.op("dve", lambda e: e.scalar_tensor_tensor(bis[:, 2:3], bis[:, 1:2], 1.0, bis[:, 0:1], ALU.add, ALU.subtract),
                     reads=[b_bis], writes=[b_bis])
                for it in range(N_BISECT):
                    S.op("dve", lambda e, it=it: e.tensor_scalar(bis[:, 3:4], bis[:, 2:3], float(0.5 ** (it + 1)), None, ALU.mult),
                         reads=[b_bis], writes=[b_bis])
                    S.op("dve", lambda e: e.tensor_tensor(bis[:, 4:5], bis[:, 0:1], bis[:, 3:4], ALU.add),
                         reads=[b_bis], writes=[b_bis])
                    S.op("dve", lambda e: e.tensor_scalar(mb[:, 0:nk], I_[:, 0:nk], bis[:, 4:5], 0.0, ALU.is_ge, ALU.add,
                                                         accum_out=bis[:, 5:6]),
                         reads=[b_A, b_bis], writes=[b_B, b_bis])
                    S.op("dve", lambda e: e.tensor_scalar(bis[:, 6:7], bis[:, 5:6], 255.5, None, ALU.is_ge),
                         reads=[b_bis], writes=[b_bis])
                    S.op("dve", lambda e: e.scalar_tensor_tensor(bis[:, 0:1], bis[:, 6:7], bis[:, 3:4], bis[:, 0:1], ALU.mult, ALU.add),
                         reads=[b_bis], writes=[b_bis])
            S.op("dve", lambda e: e.tensor_scalar(mb[:, 0:nk], I_[:, 0:nk], bis[:, 0:1], -1.0, ALU.is_ge, ALU.add),
                 reads=[b_A, b_bis], writes=[b_B])
            for g in range(4):
                po, bpo = bank("aux")
                psm, bpsm = bank("tr_f32")
                for ch0 in range(0, nb, 16):
                    ch1 = min(nb, ch0 + 16)
                    ki_ = nxt("kv"); ks, bks, vs, bvs = kst[ki_], b_kst[ki_], vst[ki_], b_vst[ki_]
                    S.dma("sp", lambda e, ks=ks, g=g, ch0=ch0, ch1=ch1: e.dma_start(
                        out=ks[:, 0:(ch1 - ch0) * 128], in_=kTd[g, :, ch0 * 128:ch1 * 128]),
                        reads=[b_kTd], writes=[bks])
                    S.dma("sp", lambda e, vs=vs, g=g, ch0=ch0, ch1=ch1: e.dma_start(
                        out=vs[:, 0:ch1 - ch0, :],
                        in_=vd[ch0 * 128:ch1 * 128, g * 128:(g + 1) * 128].rearrange("(b p) d -> p b d", p=128)),
                        reads=[b_vd], writes=[bvs])
                    for b in range(ch0, ch1):
                        bl = b - ch0
                        pS, bpS = bank("acc")
                        S.op("pe", lambda e, pS=pS, ks=ks, bl=bl, g=g: e.matmul(
                            pS[:, :].rearrange("p (h t) -> p h t", t=128), ks[:, bl * 128:(bl + 1) * 128],
                            qT[:, 4 * g:4 * g + 4, j * 128:(j + 1) * 128], start=True, stop=False),
                            reads=[bks, b_C], writes=[bpS], signal=False)
                        S.op("pe", lambda e, pS=pS, b=b: e.matmul(pS[:, :], mb[:, b * 128:(b + 1) * 128], bigI4,
                                                               start=False, stop=True),
                             reads=[b_B, b_cstb], writes=[bpS])
                        pi_ = nxt("pt"); P_, bP = PT[pi_], b_PT[pi_]
                        S.op("act", lambda e, P_=P_, pS=pS: e.activation(P_[:, :], pS[:, :], AF.Exp, scale=QK_SCALE),
                             reads=[bpS], writes=[bP])
                        S.op("pe", lambda e, vs=vs, bl=bl, P_=P_, b=b: e.matmul(po[:, :], vs[:, bl, :], P_[:, :],
                                                                           start=(b == 0), stop=(b == nb - 1)),
                             reads=[bvs, bP], writes=[bpo], signal=False)
                        S.op("pe", lambda e, P_=P_, b=b: e.matmul(psm[:, :], onesb, P_[:, :],
                                                               start=(b == 0), stop=(b == nb - 1)),
                             reads=[b_cstb, bP], writes=[bpsm])
                S.op("dve", lambda e: e.reciprocal(rec[:, :], psm[:, :]), reads=[bpsm], writes=[b_rec])
                S.op("dve", lambda e, g=g: e.tensor_tensor(
                    oT[:, 4 * g:4 * g + 4, j * 128:(j + 1) * 128], po[:, :].rearrange("p (h t) -> p h t", t=128),
                    rec[:, :].rearrange("p (h t) -> p h t", t=128), ALU.mult),
                    reads=[bpo, b_rec], writes=[b_oT])

    def out_proj_residual(G, wkey, nk_, gate_idx, src_dram, b_src, dst_dram, b_dst, lhs_fn, b_lhs, seq=0, final_out=False):
        tok0 = G * GT
        sq0 = 1 if SM["on"] else seq
        S.dma("sp", lambda e: e.dma_start(out=gb[:, :], in_=modR[gate_idx, sq0:sq0 + 1, :].partition_broadcast(128)),
              reads=[b_modR], writes=[b_gb])
        if SM["on"]:
            for s_ in range(1, NS):
                S.dma("sp", lambda e, s_=s_: e.dma_start(out=gb[4 * s_:4 * s_ + 4, :],
                                                        in_=modR[gate_idx, 1 + s_:2 + s_, :].partition_broadcast(4)),
                      reads=[b_modR], writes=[b_gb])
        xs_ = []
        for j in range(NTG):
            i = nxt("x"); x_, bx = xt[i], b_xt[i]
            S.dma("sp", lambda e, x_=x_, j=j: e.dma_start(out=x_[:, :], in_=src_dram[tok0 + j * 128: tok0 + (j + 1) * 128, :]),
                  reads=[b_src], writes=[bx])
            xs_.append((x_, bx))
        for c4 in range(4):
            halves = [(0, nk_)] if nk_ <= 22 else [(0, 22), (22, nk_)]
            accs = [bank("to") for _ in range(NTG)]
            for (k0, k1) in halves:
                wv_, bw = load_w(wkey, k0 * 128, (k1 - k0) * 128, c4 * 512, 512)
                for j in range(NTG):
                    pa, bpa = accs[j]
                    for kc in range(k0, k1):
                        S.op("pe", lambda e, pa=pa, kc=kc, j=j, wv_=wv_, k0=k0: e.matmul(
                            pa[:, :], lhs_fn(kc, j), wv_[:, kc - k0, :], start=(kc == 0), stop=(kc == nk_ - 1)),
                            reads=[b_lhs, bw], writes=[bpa], signal=(kc == nk_ - 1 or kc == k1 - 1))
            for j in range(NTG):
                pa, bpa = accs[j]
                x_, bx = xs_[j]
                oi = nxt("o"); ot, bot = otok[oi], b_otok[oi]
                S.op("dve", lambda e, ot=ot, pa=pa, c4=c4: e.tensor_tensor(ot[:, :], pa[:, :], gb[:, c4 * 512:(c4 + 1) * 512], ALU.mult),
                     reads=[bpa, b_gb], writes=[bot])
                S.op("pool", lambda e, ot=ot, x_=x_, c4=c4: e.tensor_tensor(
                    x_[:, c4 * 512:(c4 + 1) * 512], ot[:, :], x_[:, c4 * 512:(c4 + 1) * 512], ALU.add),
                    reads=[bot, bx], writes=[bx])
        for j in range(NTG):
            x_, bx = xs_[j]
            S.dma("pool", lambda e, x_=x_, j=j: e.dma_start(out=dst_dram[tok0 + j * 128: tok0 + (j + 1) * 128, :], in_=x_[:, :]),
                  reads=[bx], writes=[b_dst], is_output=final_out)

    def ffn(G, layer, src_dram, b_src, dst_dram, b_dst, n_idx, gate_idx, final_out=False, last=False):
        tok0 = G * GT
        upk, dnk = ("up0", "dn0") if layer == 0 else ("up1", "dn1")
        load_norm_T(src_dram, b_src, tok0, n_idx)
        if SM["on"]:
            for q_ in range(8):
                for c4 in range(4):
                    S.dma("sp", lambda e, q_=q_, c4=c4: e.dma_start(
                        out=SV["sh"][:, q_, c4 * 22:(c4 + 1) * 22],
                        in_=sconv[layer, q_, c4 * 2816:(c4 + 1) * 2816].rearrange("(b p) -> p b", p=128),
                        allow_slow_non_contiguous=True), writes=[SV["b_sh"]])
        actT = actT_a[:, 0:NFB * GT].rearrange("p (f t) -> p f t", t=GT)
        b_act = [b_A, b_B]
        for c4 in range(11):
            wa, bwa = load_w(upk, 0, D, c4 * 512, 512)
            wb2, bwb2 = load_w(upk, 0, D, DFF + c4 * 512, 512)
            for hh in range(4):
                jf = c4 * 4 + hh
                pa, bpa = proj_ws(wa, bwa, hh * 128)
                pb_, bpb = proj_ws(wb2, bwb2, hh * 128)
                ui = nxt("u"); ua_, ub_ = ua[ui], ub[ui]
                for (u_, p_, bp_, blk) in ((ua_, pa, bpa, jf), (ub_, pb_, bpb, NFB + jf)):
                    S.op("act", lambda e, u_=u_, p_=p_: e.activation(u_[:, 2:2 + GT], p_[:, 0:GT], AF.Copy),
                         reads=[bp_], writes=[b_C])
                    S.op("pool", lambda e, u_=u_, blk=blk: e.tensor_copy(u_[:, 0:2], halo[:, layer, blk, :]),
                         reads=[b_halo], writes=[b_C])
                    S.op("pool", lambda e, u_=u_, blk=blk: e.tensor_copy(halo[:, layer, blk, :], u_[:, GT:GT + 2]),
                         reads=[b_C], writes=[b_halo])
                for (u_, y_, blk) in ((ua_, ya, jf), (ub_, yb, NFB + jf)):
                    S.op("dve", lambda e, u_=u_, y_=y_, blk=blk: e.tensor_scalar(
                        y_, u_[:, 2:2 + GT], cw[:, layer, blk, 2:3], cw[:, layer, blk, 3:4], ALU.mult, ALU.add),
                        reads=[b_C, b_cw], writes=[b_C])
                    if SM["on"]:
                        ue = SV["ue"]; sh = SV["sh"]; so = SV["so"]
                        S.op("dve", lambda e, blk=blk: e.tensor_copy(ue[:, :, 0:2], sh[:, :, blk].rearrange("p (s r) -> p s r", r=2)),
                             reads=[SV["b_sh"]], writes=[SV["b_ue"]])
                        S.op("dve", lambda e, u_=u_: e.tensor_copy(ue[:, :, 2:6], u_[:, 2:18].rearrange("p (s t) -> p s t", t=4)),
                             reads=[b_C], writes=[SV["b_ue"]])
                        y3 = y_[:, 0:16].rearrange("p (s t) -> p s t", t=4)
                        S.op("dve", lambda e, y3=y3, blk=blk: e.tensor_scalar(
                            y3, ue[:, :, 2:6], cw[:, layer, blk, 2:3], cw[:, layer, blk, 3:4], ALU.mult, ALU.add),
                            reads=[SV["b_ue"], b_cw], writes=[b_C])
                        S.op("dve", lambda e, y3=y3, blk=blk: e.scalar_tensor_tensor(
                            y3, ue[:, :, 1:5], cw[:, layer, blk, 1:2], y3, ALU.mult, ALU.add),
                            reads=[SV["b_ue"], b_cw, b_C], writes=[b_C])
                        S.op("dve", lambda e, y3=y3, blk=blk: e.scalar_tensor_tensor(
                            y3, ue[:, :, 0:4], cw[:, layer, blk, 0:1], y3, ALU.mult, ALU.add),
                            reads=[SV["b_ue"], b_cw, b_C], writes=[b_C])
                        S.op("dve", lambda e, blk=blk: e.tensor_copy(so[:, :, blk].rearrange("p (s r) -> p s r", r=2), ue[:, :, 4:6]),
                             reads=[SV["b_ue"]], writes=[SV["b_so"]])
                        continue
                    S.op("dve", lambda e, u_=u_, y_=y_, blk=blk: e.scalar_tensor_tensor(
                        y_, u_[:, 1:1 + GT], cw[:, layer, blk, 1:2], y_, ALU.mult, ALU.add),
                        reads=[b_C, b_cw], writes=[b_C])
                    S.op("dve", lambda e, u_=u_, y_=y_, blk=blk: e.scalar_tensor_tensor(
                        y_, u_[:, 0:GT], cw[:, layer, blk, 0:1], y_, ALU.mult, ALU.add),
                        reads=[b_C, b_cw], writes=[b_C])
                S.op("act", lambda e: e.activation(sa, ya, AF.Silu), reads=[b_C], writes=[b_C])
                S.op("dve", lambda e, jf=jf: e.tensor_tensor(actT[:, jf, :], sa, yb, ALU.mult),
                     reads=[b_C], writes=b_act)
        out_proj_residual(G, dnk, NFB, gate_idx, src_dram, b_src, dst_dram, b_dst,
                          lambda kc, j: actT[:, kc, j * 128:(j + 1) * 128], b_A, final_out=final_out)
        if SM["on"]:
            for q_ in range(8):
                for c4 in range(4):
                    S.dma("pool", lambda e, q_=q_, c4=c4: e.dma_start(
                        out=conv_s[layer, q_, c4 * 2816:(c4 + 1) * 2816].rearrange("(b p) -> p b", p=128),
                        in_=SV["so"][:, q_, c4 * 22:(c4 + 1) * 22], allow_slow_non_contiguous=True),
                        reads=[SV["b_so"]], is_output=True)
        if last:
            ht = otok[0][:, 0:176].rearrange("p (r b) -> p r b", b=88)
            S.op("dve", lambda e: e.tensor_copy(ht, halo[:, layer, :, :].rearrange("p b r -> p r b")),
                 reads=[b_halo], writes=[b_otok[0]])
            for r in range(2):
                for q4 in range(4):
                    S.dma("pool", lambda e, r=r, q4=q4: e.dma_start(
                        out=conv_p[layer, r, q4 * 2816:(q4 + 1) * 2816].rearrange("(b p) -> p b", p=128),
                        in_=ht[:, r, q4 * 22:(q4 + 1) * 22], allow_slow_non_contiguous=True),
                        reads=[b_otok[0]], is_output=True)

    def shared_kv(G):
        tok0 = G * GT
        smode = SM["on"]
        csrc = cs_s if smode else cs_p
        S.dma("sp", lambda e: e.dma_start(out=cs[:, :, :], in_=csrc[:, :, tok0:tok0 + GT].rearrange("a p t -> p a t")),
              writes=[b_cs])
        if smode:
            load_norm_T(hs1, b_hs1, 0, 4)
        else:
            load_norm_T(h1d, b_h1d, tok0, 4)
        do_out = smode or tok0 >= kb_row0
        for c4 in range(2):
            wv_, bw = load_w("kv_w", 0, D, c4 * 512, 512)
            for hh in range(4):
                h = c4 * 4 + hh
                pa, bpa = proj_ws(wv_, bw, hh * 128)
                rms_head(pa, bpa, 4)

                def kb_out(j, ap_, b_, h=h):
                    if smode:
                        if j == 0:
                            for s_ in range(NS):
                                out_dma(k_b_s[s_, 2044:2048, h * 128:(h + 1) * 128], ap_[4 * s_:4 * s_ + 4, :], b_)
                        return
                    out_dma(k_b_p[tok0 - kb_row0 + j * 128: tok0 - kb_row0 + (j + 1) * 128, h * 128:(h + 1) * 128], ap_, b_)
                rope(kn[:, :], b_kn, 0, R128T, krb[:, :], [b_krb], kb_out if do_out else None)
                if smode:
                    S.op("act", lambda e, h=h: e.activation(SV["kbn"][:, h, :], krb[:, 0:128], AF.Copy),
                         reads=[b_krb], writes=[SV["b_kbn"]])
                else:
                    S.dma("pool", lambda e, h=h: e.dma_start(out=kbTd[h, :, tok0:tok0 + GT], in_=krb[:, :]),
                          reads=[b_krb], writes=[b_kbTd])
        for c4 in range(2):
            wv_, bw = load_w("kv_w", 0, D, 1024 + c4 * 512, 512)
            for j in range(NTG):
                pa, bpa = bank("to")
                for kc in range(NKC):
                    S.op("pe", lambda e, pa=pa, kc=kc, j=j: e.matmul(pa[:, :], xnT[:, kc, j * 128:(j + 1) * 128], wv_[:, kc, :],
                                                                 start=(kc == 0), stop=(kc == NKC - 1)),
                         reads=[b_xnT, bw], writes=[bpa], signal=(kc == NKC - 1))
                oi = nxt("o"); ot, bot, ob, bob = otok[oi], b_otok[oi], otb[oi], b_otb[oi]
                S.op("dve", lambda e, ob=ob, pa=pa: e.tensor_copy(ob[:, :], pa[:, :]), reads=[bpa], writes=[bob])
                if smode:
                    if j == 0:
                        S.op("pool", lambda e, ob=ob, c4=c4: e.tensor_copy(SV["vbn"][:, c4 * 512:(c4 + 1) * 512], ob[:, :]),
                             reads=[bob], writes=[SV["b_vbn"]])
                        S.op("act", lambda e, ot=ot, pa=pa: e.activation(ot[:, :], pa[:, :], AF.Copy), reads=[bpa], writes=[bot])
                        for s_ in range(NS):
                            out_dma(v_b_s[s_, 2044:2048, c4 * 512:(c4 + 1) * 512], ot[4 * s_:4 * s_ + 4, :], bot)
                    continue
                S.dma("pool", lambda e, ob=ob, j=j, c4=c4: e.dma_start(
                    out=vbd[tok0 + j * 128: tok0 + (j + 1) * 128, c4 * 512:(c4 + 1) * 512], in_=ob[:, :]),
                    reads=[bob], writes=[b_vbd])
                if do_out:
                    S.op("act", lambda e, ot=ot, pa=pa: e.activation(ot[:, :], pa[:, :], AF.Copy), reads=[bpa], writes=[bot])
                    out_dma(v_b_p[tok0 - kb_row0 + j * 128: tok0 - kb_row0 + (j + 1) * 128, c4 * 512:(c4 + 1) * 512],
                            ot[:, :], bot)

    qbT = slabC[:, :]
    def qb_view(hidx):
        if hidx < 16:
            return slabC[:, hidx * GT:(hidx + 1) * GT], b_C
        return slabD[:, (hidx - 16) * GT:(hidx - 15) * GT], b_D

    def l1_mixer(G):
        tok0 = G * GT
        smode = SM["on"]
        csrc = cs_s if smode else cs_p
        S.dma("sp", lambda e: e.dma_start(out=cs[:, :, :], in_=csrc[:, :, tok0:tok0 + GT].rearrange("a p t -> p a t")),
              writes=[b_cs])
        if smode:
            load_norm_T(hs1, b_hs1, 0, 2)
        else:
            load_norm_T(h1d, b_h1d, tok0, 2)
        for c4 in range(6):
            wv_, bw = load_w("b_w_q", 0, D, c4 * 512, 512)
            for hh in range(4):
                h = c4 * 4 + hh
                pa, bpa = proj_ws(wv_, bw, hh * 128)
                rms_head(pa, bpa, 5)
                dst, bd = qb_view(h)
                rope(kn[:, :], b_kn, 0, R128T, dst, [bd])
        if smode:
            sample_mixer_B()
            return
        for j in range(NTG):
            i = G * NTG + j
            blocks = [r for r in range(17) if i - r >= 0]
            lo_blk = i - blocks[-1]
            nbl = len(blocks)
            for s4 in range(2):
                po, bpo = bank("aux")
                psm, bpsm = bank("tr_f32")
                for sl in range(4):
                    s = s4 * 4 + sl
                    ki_ = nxt("kv"); ks, bks, vs, bvs = kst[ki_], b_kst[ki_], vst[ki_], b_vst[ki_]
                    S.dma("sp", lambda e, ks=ks, s=s: e.dma_start(
                        out=ks[:, 0:nbl * 128], in_=kbTd[s, :, lo_blk * 128:(i + 1) * 128]),
                        reads=[b_kbTd], writes=[bks])
                    S.dma("sp", lambda e, vs=vs, s=s: e.dma_start(
                        out=vs[:, 0:nbl, :],
                        in_=vbd[lo_blk * 128:(i + 1) * 128, s * 128:(s + 1) * 128].rearrange("(b p) d -> p b d", p=128)),
                        reads=[b_vbd], writes=[bvs])
                    first = True
                    for r in blocks:
                        bl = (i - r) - lo_blk
                        gs = [g for g in range(3) if r <= (1, 4, 16)[g]]
                        ng = len(gs)
                        pS, bpS = bank("acc")
                        for gi, g in enumerate(gs):
                            qv, bq = qb_view(g * 8 + s)
                            S.op("pe", lambda e, pS=pS, ks=ks, bl=bl, qv=qv, gi=gi: e.matmul(
                                pS[:, gi * 128:(gi + 1) * 128], ks[:, bl * 128:(bl + 1) * 128],
                                qv[:, j * 128:(j + 1) * 128], start=(gi == 0), stop=False),
                                reads=[bks, bq], writes=[bpS], signal=False)
                        for gi, g in enumerate(gs):
                            mi = mixb_mask_idx(g, r)
                            S.op("pe", lambda e, pS=pS, mi=mi, gi=gi, ng=ng: e.matmul(
                                pS[:, gi * 128:(gi + 1) * 128], mBm[:, mi, :], bigI4[:, 0:128],
                                start=False, stop=(gi == ng - 1)),
                                reads=[b_cstb], writes=[bpS], signal=(gi == ng - 1))
                        pi_ = nxt("pt"); P_, bP = PT[pi_], b_PT[pi_]
                        S.op("act", lambda e, P_=P_, pS=pS, ng=ng: e.activation(P_[:, 0:ng * 128], pS[:, 0:ng * 128], AF.Exp, scale=QK_SCALE),
                             reads=[bpS], writes=[bP])
                        for gi in range(ng):
                            lastmm = (r == blocks[-1]) and (gi == ng - 1)
                            S.op("pe", lambda e, vs=vs, bl=bl, P_=P_, gi=gi, sl=sl, first=first, lastmm=lastmm: e.matmul(
                                po[:, sl * 128:(sl + 1) * 128], vs[:, bl, :], P_[:, gi * 128:(gi + 1) * 128],
                                start=first, stop=lastmm), reads=[bvs, bP], writes=[bpo], signal=False)
                            S.op("pe", lambda e, P_=P_, gi=gi, sl=sl, first=first, lastmm=lastmm: e.matmul(
                                psm[:, sl * 128:(sl + 1) * 128], onesb, P_[:, gi * 128:(gi + 1) * 128],
                                start=first, stop=lastmm), reads=[b_cstb, bP], writes=[bpsm])
                            first = False
                S.op("dve", lambda e: e.reciprocal(rec[:, :], psm[:, :]), reads=[bpsm], writes=[b_rec])
                S.op("dve", lambda e, s4=s4: e.tensor_tensor(
                    oT[:, 4 * s4:4 * s4 + 4, j * 128:(j + 1) * 128], po[:, :].rearrange("p (h t) -> p h t", t=128),
                    rec[:, :].rearrange("p (h t) -> p h t", t=128), ALU.mult),
                    reads=[bpo, b_rec], writes=[b_oT])

    def smB_mask_idx(g, blk):
        if g == 0:
            return {15: 0, 16: 1}[blk]
        if g == 1:
            return 2 + (blk - 12)
        return 7 + blk

    def sample_mixer_B():
        kf, vf = SV["kf"], SV["vf"]
        S.op("dve", lambda e: e.memset(oT[:, :, :], 0.0), writes=[b_oT])
        for s_ in range(NS):
            po8, bpo8 = bank("aux")
            ps8, bps8 = bank("aux")
            S.op("pe", lambda e, po8=po8: e.matmul(po8[:, 0:32], zerosb, onesb[:, 0:32], start=True, stop=False),
                 reads=[b_cstb], writes=[bpo8], signal=False)
            S.op("pe", lambda e, ps8=ps8: e.matmul(ps8[:, 0:32], zerosb, onesb[:, 0:32], start=True, stop=False),
                 reads=[b_cstb], writes=[bps8], signal=False)
            for blk in range(17):
                if blk < 16:
                    bi = blk % 2
                    S.dma("sp", lambda e, bi=bi, blk=blk, s_=s_: e.dma_start(out=kf[bi][:, :], in_=skb[s_, blk * 128:(blk + 1) * 128, :]),
                          writes=[SV["b_kf"][bi]])
                    S.dma("sp", lambda e, bi=bi, blk=blk, s_=s_: e.dma_start(out=vf[bi][:, :], in_=svb[s_, blk * 128:(blk + 1) * 128, :]),
                          writes=[SV["b_vf"][bi]])
                    for hh in range(2):
                        pt_, bpt = bank("tr_f32")
                        for h4 in range(4):
                            h = hh * 4 + h4
                            S.op("pe", lambda e, pt_=pt_, bi=bi, h=h, h4=h4: e.transpose(
                                pt_[:, h4 * 128:(h4 + 1) * 128], kf[bi][:, h * 128:(h + 1) * 128], ident),
                                reads=[SV["b_kf"][bi], b_cst], writes=[bpt])
                        S.op("act", lambda e, pt_=pt_, hh=hh: e.activation(
                            SV["kblk"][:, hh * 4:(hh + 1) * 4, :], pt_[:, :].rearrange("p (h t) -> p h t", t=128), AF.Copy),
                            reads=[bpt], writes=[SV["b_kblk"]])
                    S.op("pool", lambda e, bi=bi: e.tensor_copy(SV["vblk"][:, :], vf[bi][:, :]),
                         reads=[SV["b_vf"][bi]], writes=[SV["b_vblk"]])
                    kb_, bkb, vb_, bvb = SV["kblk"], SV["b_kblk"], SV["vblk"], SV["b_vblk"]
                else:
                    kb_, bkb, vb_, bvb = SV["kbn"], SV["b_kbn"], SV["vbn"], SV["b_vbn"]
                gs = [g for g in range(3) if (g == 2 or blk == 16 or (g == 1 and blk >= 12) or (g == 0 and blk == 15))]
                ng = len(gs)
                for sl in range(8):
                    pS, bpS = bank("acc")
                    for gi, g in enumerate(gs):
                        qv, bq = qb_view(g * 8 + sl)
                        S.op("pe", lambda e, pS=pS, kb_=kb_, sl=sl, qv=qv, gi=gi, s_=s_: e.matmul(
                            pS[:, gi * 4:(gi + 1) * 4], kb_[:, sl, :], qv[:, 4 * s_:4 * s_ + 4], start=(gi == 0), stop=False),
                            reads=[bkb, bq], writes=[bpS], signal=False)
                    for gi, g in enumerate(gs):
                        mi = smB_mask_idx(g, blk)
                        S.op("pe", lambda e, pS=pS, mi=mi, gi=gi, ng=ng, s_=s_: e.matmul(
                            pS[:, gi * 4:(gi + 1) * 4], SV["msB"][:, mi, :], bigI4[:, 4 * s_:4 * s_ + 4],
                            start=False, stop=(gi == ng - 1)),
                            reads=[b_kiT, b_cstb], writes=[bpS], signal=(gi == ng - 1))
                    pi_ = nxt("pt"); P_, bP = PT[pi_], b_PT[pi_]
                    S.op("act", lambda e, P_=P_, pS=pS, ng=ng: e.activation(P_[:, 0:ng * 4], pS[:, 0:ng * 4], AF.Exp, scale=QK_SCALE),
                         reads=[bpS], writes=[bP])
                    for gi in range(ng):
                        lastmm = (blk == 16) and (sl == 7) and (gi == ng - 1)
                        S.op("pe", lambda e, vb_=vb_, P_=P_, gi=gi, sl=sl, lastmm=lastmm, po8=po8: e.matmul(
                            po8[:, sl * 4:(sl + 1) * 4], vb_[:, sl * 128:(sl + 1) * 128], P_[:, gi * 4:(gi + 1) * 4],
                            start=False, stop=lastmm), reads=[bvb, bP], writes=[bpo8], signal=False)
                        S.op("pe", lambda e, P_=P_, gi=gi, sl=sl, lastmm=lastmm, ps8=ps8: e.matmul(
                            ps8[:, sl * 4:(sl + 1) * 4], onesb, P_[:, gi * 4:(gi + 1) * 4],
                            start=False, stop=lastmm), reads=[b_cstb, bP], writes=[bps8])
            S.op("dve", lambda e, ps8=ps8: e.reciprocal(rec[:, 0:32], ps8[:, 0:32]), reads=[bps8], writes=[b_rec])
            S.op("dve", lambda e, po8=po8, s_=s_: e.tensor_tensor(
                oT[:, 0:8, 4 * s_:4 * s_ + 4], po8[:, 0:32].rearrange("p (h t) -> p h t", t=4),
                rec[:, 0:32].rearrange("p (h t) -> p h t", t=4), ALU.mult),
                reads=[bpo8, b_rec], writes=[b_oT])

    def sample_attn_A():
        I0 = slabA
        I1 = wbuf[0][:, :].bitcast(F32)[:, 0:4224]
        mb0 = slabB
        mb1 = wbuf[1][:, 0:4224]
        bI = [b_A, b_wbuf[0]]
        bM = [b_B, b_wbuf[1]]
        kin = kiT[:, 3584:3712]
        kpg, vpg, kipg = SV["kpg"], SV["vpg"], SV["kipg"]

        def Iseg(c0, c1):
            if c1 <= 4096:
                return I0[:, c0:c1], bI[0]
            return I1[:, c0 - 4096:c1 - 4096], bI[1]
        for s_ in range(NS):
            S.dma("sp", lambda e, s_=s_: e.dma_start(out=SV["pti"][:, :], in_=ptab[s_:s_ + 1, :].partition_broadcast(128)),
                  writes=[SV["b_pti"]])
            S.op("dve", lambda e: e.tensor_scalar(SV["idx"][:, :], SV["pti"][:, :], 128.0, SV["pidx"][:, 0:1], ALU.mult, ALU.add),
                 reads=[SV["b_pti"], SV["b_pidx"]], writes=[SV["b_idx"]])
            for c in range(17):
                if c < 16:
                    pt_, bpt = bank("tr_f32")
                    for pg in range(4):
                        jpg = c * 4 + pg
                        bi = nxt("kv")
                        S.dma("pool", lambda e, bi=bi, jpg=jpg: e.indirect_dma_start(
                            out=kipg[bi][:, :], out_offset=None, in_=cki,
                            in_offset=bass.IndirectOffsetOnAxis(ap=SV["idx"][:, jpg:jpg + 1], axis=0)),
                            reads=[SV["b_idx"]], writes=[SV["b_kipg"][bi]])
                        S.op("pe", lambda e, pt_=pt_, bi=bi, pg=pg: e.transpose(
                            pt_[:, pg * 128:(pg + 1) * 128], kipg[bi][:, :], ident),
                            reads=[SV["b_kipg"][bi], b_cst], writes=[bpt])
                    S.op("act", lambda e, pt_=pt_: e.activation(SV["kiC"][:, :], pt_[:, :], AF.Copy),
                         reads=[bpt], writes=[SV["b_kiC"]])
                    keys, bkeys, width = SV["kiC"][:, :], SV["b_kiC"], 512
                else:
                    keys, bkeys, width = kin, b_kiT, 128
                c0 = c * 512
                seg, bseg = Iseg(c0, c0 + width)
                for h in range(16):
                    pa, bpa = bank("acc")
                    S.op("pe", lambda e, pa=pa, h=h, keys=keys, width=width: e.matmul(
                        pa[:, 0:width], qiT[:, h, 0:128], keys, start=True, stop=True),
                        reads=[b_D, bkeys], writes=[bpa])
                    ri = nxt("rl"); r_, br = rl[ri], b_rl[ri]
                    S.op("act", lambda e, pa=pa, r_=r_, width=width: e.activation(r_[:, 0:width], pa[:, 0:width], AF.Relu),
                         reads=[bpa], writes=[br])
                    if h == 0:
                        S.op("dve", lambda e, r_=r_, seg=seg, width=width: e.tensor_scalar(
                            seg, r_[:, 0:width], wtt[:, 0, 0:1], None, ALU.mult), reads=[br, b_wtt], writes=[bseg])
                    else:
                        S.op("dve", lambda e, r_=r_, h=h, seg=seg, width=width: e.scalar_tensor_tensor(
                            seg, r_[:, 0:width], wtt[:, 0, h:h + 1], seg, ALU.mult, ALU.add),
                            reads=[br, b_wtt, bseg], writes=[bseg])
            segs = [(I0[:, 0:4096], bI[0], mb0[:, 0:4096], bM[0]), (I1[:, 0:4224], bI[1], mb1[:, 0:4224], bM[1])]
            S.op("dve", lambda e: e.tensor_reduce(bis[:, 0:1], I0[:, 0:4096], AX.X, ALU.min), reads=[bI[0]], writes=[b_bis])
            S.op("dve", lambda e: e.tensor_reduce(bis[:, 8:9], I1[:, 0:4224], AX.X, ALU.min), reads=[bI[1]], writes=[b_bis])
            S.op("dve", lambda e: e.tensor_tensor(bis[:, 0:1], bis[:, 0:1], bis[:, 8:9], ALU.min), reads=[b_bis], writes=[b_bis])
            S.op("dve", lambda e: e.tensor_tensor(I1[:, 4096:4224], I1[:, 4096:4224], SV["cnegN"][:, :], ALU.add),
                 reads=[bI[1], SV["b_cnegN"]], writes=[bI[1]])
            S.op("dve", lambda e: e.tensor_scalar(bis[:, 0:1], bis[:, 0:1], -1.0, None, ALU.add), reads=[b_bis], writes=[b_bis])
            S.op("dve", lambda e: e.tensor_reduce(bis[:, 1:2], I0[:, 0:4096], AX.X, ALU.max), reads=[bI[0]], writes=[b_bis])
            S.op("dve", lambda e: e.tensor_reduce(bis[:, 8:9], I1[:, 0:4224], AX.X, ALU.max), reads=[bI[1]], writes=[b_bis])
            S.op("dve", lambda e: e.tensor_tensor(bis[:, 1:2], bis[:, 1:2], bis[:, 8:9], ALU.max), reads=[b_bis], writes=[b_bis])
            S.op("dve", lambda e: e.scalar_tensor_tensor(bis[:, 2:3], bis[:, 1:2], 1.0, bis[:, 0:1], ALU.add, ALU.subtract),
                 reads=[b_bis], writes=[b_bis])
            for it in range(N_BISECT):
                S.op("dve", lambda e, it=it: e.tensor_scalar(bis[:, 3:4], bis[:, 2:3], float(0.5 ** (it + 1)), None, ALU.mult),
                     reads=[b_bis], writes=[b_bis])
                S.op("dve", lambda e: e.tensor_tensor(bis[:, 4:5], bis[:, 0:1], bis[:, 3:4], ALU.add), reads=[b_bis], writes=[b_bis])
                for si, (Is, bIs, ms, bms) in enumerate(segs):
                    S.op("dve", lambda e, Is=Is, ms=ms, si=si: e.tensor_scalar(ms, Is, bis[:, 4:5], 0.0, ALU.is_ge, ALU.add,
                                                                           accum_out=bis[:, 9 + si:10 + si]),
                         reads=[bIs, b_bis], writes=[bms, b_bis])
                S.op("dve", lambda e: e.tensor_tensor(bis[:, 5:6], bis[:, 9:10], bis[:, 10:11], ALU.add), reads=[b_bis], writes=[b_bis])
                S.op("dve", lambda e: e.tensor_scalar(bis[:, 6:7], bis[:, 5:6], 255.5, None, ALU.is_ge), reads=[b_bis], writes=[b_bis])
                S.op("dve", lambda e: e.scalar_tensor_tensor(bis[:, 0:1], bis[:, 6:7], bis[:, 3:4], bis[:, 0:1], ALU.mult, ALU.add),
                     reads=[b_bis], writes=[b_bis])
            for (Is, bIs, ms, bms) in segs:
                S.op("dve", lambda e, Is=Is, ms=ms: e.tensor_scalar(ms, Is, bis[:, 0:1], -1.0, ALU.is_ge, ALU.add),
                     reads=[bIs, b_bis], writes=[bms])
            po4, bpo4 = bank("aux")
            ps4, bps4 = bank("aux")
            S.op("pe", lambda e, po4=po4: e.matmul(po4[:, 0:64], zerosb, onesb[:, 0:64], start=True, stop=False),
                 reads=[b_cstb], writes=[bpo4], signal=False)
            S.op("pe", lambda e, ps4=ps4: e.matmul(ps4[:, 0:64], zerosb, onesb[:, 0:64], start=True, stop=False),
                 reads=[b_cstb], writes=[bps4], signal=False)
            for jb in range(65):
                if jb < 64:
                    bi = jb % 2
                    S.dma("pool", lambda e, bi=bi, jb=jb: e.indirect_dma_start(
                        out=kpg[bi][:, :], out_offset=None, in_=ck,
                        in_offset=bass.IndirectOffsetOnAxis(ap=SV["idx"][:, jb:jb + 1], axis=0)),
                        reads=[SV["b_idx"]], writes=[SV["b_kpg"][bi]])
                    S.dma("pool", lambda e, bi=bi, jb=jb: e.indirect_dma_start(
                        out=vpg[bi][:, :], out_offset=None, in_=cv,
                        in_offset=bass.IndirectOffsetOnAxis(ap=SV["idx"][:, jb:jb + 1], axis=0)),
                        reads=[SV["b_idx"]], writes=[SV["b_vpg"][bi]])
                    pt_, bpt = bank("tr_f32")
                    for g in range(4):
                        S.op("pe", lambda e, pt_=pt_, bi=bi, g=g: e.transpose(
                            pt_[:, g * 128:(g + 1) * 128], kpg[bi][:, g * 128:(g + 1) * 128], ident),
                            reads=[SV["b_kpg"][bi], b_cst], writes=[bpt])
                    S.op("act", lambda e, pt_=pt_: e.activation(SV["kpT"][:, :, :], pt_[:, :].rearrange("p (g t) -> p g t", t=128), AF.Copy),
                         reads=[bpt], writes=[SV["b_kpT"]])
                    S.op("pool", lambda e, bi=bi: e.tensor_copy(SV["vpb"][:, :], vpg[bi][:, :]),
                         reads=[SV["b_vpg"][bi]], writes=[SV["b_vpb"]])
                    kT_, bkT, vb_, bvb = SV["kpT"], SV["b_kpT"], SV["vpb"], SV["b_vpb"]
                else:
                    kT_, bkT, vb_, bvb = SV["kTn"], SV["b_kTn"], SV["vnew"], SV["b_vnew"]
                if jb < 32:
                    mseg, bmseg = mb0[:, jb * 128:(jb + 1) * 128], bM[0]
                else:
                    mseg, bmseg = mb1[:, (jb - 32) * 128:(jb - 31) * 128], bM[1]
                for g in range(4):
                    pS, bpS = bank("acc")
                    S.op("pe", lambda e, pS=pS, kT_=kT_, g=g, s_=s_: e.matmul(
                        pS[:, 0:16].rearrange("p (h t) -> p h t", t=4), kT_[:, g, :],
                        qT[:, 4 * g:4 * g + 4, 4 * s_:4 * s_ + 4], start=True, stop=False),
                        reads=[bkT, b_C], writes=[bpS], signal=False)
                    S.op("pe", lambda e, pS=pS, mseg=mseg, s_=s_: e.matmul(
                        pS[:, 0:16].rearrange("p (h t) -> p h t", t=4), mseg,
                        bigI4.rearrange("p (h t) -> p h t", t=128)[:, :, 4 * s_:4 * s_ + 4], start=False, stop=True),
                        reads=[bmseg, b_cstb], writes=[bpS])
                    pi_ = nxt("pt"); P_, bP = PT[pi_], b_PT[pi_]
                    S.op("act", lambda e, P_=P_, pS=pS: e.activation(P_[:, 0:16], pS[:, 0:16], AF.Exp, scale=QK_SCALE),
                         reads=[bpS], writes=[bP])
                    lastmm = (jb == 64) and (g == 3)
                    S.op("pe", lambda e, vb_=vb_, P_=P_, g=g, lastmm=lastmm, po4=po4: e.matmul(
                        po4[:, g * 16:(g + 1) * 16], vb_[:, g * 128:(g + 1) * 128], P_[:, 0:16], start=False, stop=lastmm),
                        reads=[bvb, bP], writes=[bpo4], signal=False)
                    S.op("pe", lambda e, P_=P_, g=g, lastmm=lastmm, ps4=ps4: e.matmul(
                        ps4[:, g * 16:(g + 1) * 16], onesb, P_[:, 0:16], start=False, stop=lastmm),
                        reads=[b_cstb, bP], writes=[bps4])
            S.op("dve", lambda e, ps4=ps4: e.reciprocal(rec[:, 0:64], ps4[:, 0:64]), reads=[bps4], writes=[b_rec])
            S.op("dve", lambda e, po4=po4, s_=s_: e.tensor_tensor(
                oT[:, :, 4 * s_:4 * s_ + 4], po4[:, 0:64].rearrange("p (h t) -> p h t", t=4),
                rec[:, 0:64].rearrange("p (h t) -> p h t", t=4), ALU.mult),
                reads=[bpo4, b_rec], writes=[b_oT])

    def sample_setup():
        kv_olds = b_kst + b_vst
        r0 = kst[0][:, :].bitcast(F32)
        r1 = kst[1][:, :].bitcast(F32)
        r2 = vst[0][:, :, :].rearrange("p a b -> p (a b)").bitcast(F32)
        r3 = vst[1][:, :, :].rearrange("p a b -> p (a b)")
        SV["kpg"] = [r0[:, 0:512], r0[:, 512:1024]]; SV["b_kpg"] = [alias_buf("kpg0", kv_olds), alias_buf("kpg1", kv_olds)]
        SV["vpg"] = [r1[:, 0:512], r1[:, 512:1024]]; SV["b_vpg"] = [alias_buf("vpg0", kv_olds), alias_buf("vpg1", kv_olds)]
        SV["kipg"] = [r2[:, 0:128], r2[:, 128:256]]; SV["b_kipg"] = [alias_buf("kipg0", kv_olds), alias_buf("kipg1", kv_olds)]
        SV["cnegN"] = r2[:, 256:384]; SV["b_cnegN"] = alias_buf("cnegN", kv_olds)
        SV["ue"] = r2[:, 384:408].rearrange("p (s t) -> p s t", t=6); SV["b_ue"] = alias_buf("ue", kv_olds)
        SV["pidx"] = r2[:, 408:409]; SV["b_pidx"] = alias_buf("pidx", kv_olds)
        SV["pti"] = r2[:, 416:480].bitcast(I32); SV["b_pti"] = alias_buf("pti", kv_olds)
        SV["idx"] = r2[:, 480:544].bitcast(I32); SV["b_idx"] = alias_buf("idx", kv_olds)
        SV["kiC"] = r2[:, 544:800].bitcast(BF16); SV["b_kiC"] = alias_buf("kiC", kv_olds)
        SV["kpT"] = r3[:, 0:512].rearrange("p (g t) -> p g t", t=128); SV["b_kpT"] = alias_buf("kpT", kv_olds)
        SV["vpb"] = r3[:, 512:1024]; SV["b_vpb"] = alias_buf("vpb", kv_olds)
        SV["kTn"] = r3[:, 1024:1536].rearrange("p (g t) -> p g t", t=128); SV["b_kTn"] = alias_buf("kTn", kv_olds)
        SV["vnew"] = r3[:, 1536:2048]; SV["b_vnew"] = alias_buf("vnew", kv_olds)
        SV["sh"] = slabD[:, :].bitcast(F32)[:, 0:704].rearrange("p (q b) -> p q b", b=88); SV["b_sh"] = b_D
        SV["so"] = oT[:, :, :].rearrange("p a b -> p (a b)").bitcast(F32)[:, 0:704].rearrange("p (q b) -> p q b", b=88)
        SV["b_so"] = b_oT
        ab = slabAB[:, :]
        SV["kf"] = [ab[:, 0:1024], ab[:, 1024:2048]]; SV["vf"] = [ab[:, 2048:3072], ab[:, 3072:4096]]
        abb = ab.bitcast(BF16)
        SV["kblk"] = abb[:, 8192:9216].rearrange("p (h t) -> p h t", t=128)
        SV["vblk"] = abb[:, 9216:10240]
        SV["kbn"] = abb[:, 10240:11264].rearrange("p (h t) -> p h t", t=128)
        SV["vbn"] = abb[:, 11264:12288]
        for nm in ("kblk", "vblk", "kbn", "vbn"):
            SV["b_" + nm] = b_B
        SV["b_kf"] = [b_A, b_A]; SV["b_vf"] = [b_A, b_A]
        SV["msB"] = kiT[:, 0:3072].rearrange("p (m k) -> p m k", k=128)
        for m8 in range(3):
            S.dma("pool", lambda e, m8=m8: e.dma_start(out=SV["msB"][:, m8 * 8:(m8 + 1) * 8, :], in_=smask[:, m8 * 8:(m8 + 1) * 8, :]),
                  writes=[b_kiT])
        S.dma("sp", lambda e: e.dma_start(out=SV["cnegN"], in_=cnegn), writes=[SV["b_cnegN"]])
        S.dma("sp", lambda e: e.dma_start(out=SV["pidx"], in_=pidx), writes=[SV["b_pidx"]])
        S.op("dve", lambda e: e.memset(oT[:, :, :], 0.0), writes=[b_oT])
        for s_ in range(NS):
            for q4 in range(4):
                for (dst_, src_) in ((k_b_s, skb), (v_b_s, svb)):
                    S.dma("sp", lambda e, dst_=dst_, src_=src_, s_=s_, q4=q4: e.dma_start(
                        out=dst_[s_, q4 * 511:(q4 + 1) * 511, :], in_=src_[s_, 4 + q4 * 511:4 + (q4 + 1) * 511, :]),
                        is_output=True)

    def sample_group():
        SM["on"] = True
        sample_setup()
        l0_project(0)
        sample_attn_A()
        out_proj_residual(0, "a_w_o", 16, 0, xs, b_xs, hs0, b_hs0,
                          lambda kc, j: oT[:, kc, j * 128:(j + 1) * 128], b_oT)
        ffn(0, 0, hs0, b_hs0, hs1, b_hs1, 1, 1)
        shared_kv(0)
        l1_mixer(0)
        out_proj_residual(0, "b_w_o", 8, 2, hs1, b_hs1, hs2, b_hs2,
                          lambda kc, j: oT[:, kc, j * 128:(j + 1) * 128], b_oT)
        ffn(0, 1, hs2, b_hs2, y_s, b_y_s, 3, 3, final_out=True)
        SM["on"] = False

    for G in range(NG):
        l0_project(G)
        if stage >= 2:
            l0_attention(G)
            out_proj_residual(G, "a_w_o", 16, 0, xp, b_xp, hmid0, b_hmid0,
                              lambda kc, j: oT[:, kc, j * 128:(j + 1) * 128], b_oT)
        if stage >= 3:
            ffn(G, 0, hmid0, b_hmid0, h1d, b_h1d, 1, 1, last=(G == NG - 1))
        if stage >= 4:
            shared_kv(G)
    if stage >= 5:
        for G in range(NG):
            l1_mixer(G)
            out_proj_residual(G, "b_w_o", 8, 2, h1d, b_h1d, hmid1, b_hmid1,
                              lambda kc, j: oT[:, kc, j * 128:(j + 1) * 128], b_oT)
            if stage >= 6:
                ffn(G, 1, hmid1, b_hmid1, y_p, Buf("y_p"), 3, 3, final_out=True, last=(G == NG - 1))

    if with_sample:
        sample_group()

    S.emit()
    st.close()
    return nc


def host_inputs(inp, core):
    b = core % 2
    f32 = np.float32
    c_all = np.zeros((8, D), f32)
    c_all[0] = inp["c_prompt"][b]
    if "c_sample" in inp:
        c_all[1:1 + NS] = inp["c_sample"][core * NS:(core + 1) * NS]
    cT = np.ascontiguousarray(c_all.reshape(8, NKC, 128).transpose(2, 1, 0))
    ab = inp["ada_b"]
    rows = [ab[0, 0:2048], ab[0, 2048:4096], ab[0, 6144:8192], ab[0, 8192:10240],
            ab[1, 0:2048], ab[1, 2048:4096], ab[1, 6144:8192], ab[1, 8192:10240],
            inp["kv_ada_b"][0:2048], inp["kv_ada_b"][2048:4096]]
    biasF = np.ascontiguousarray(np.stack(rows).reshape(10, NKC, 128).transpose(2, 0, 1)).astype(f32)
    gate_b = np.stack([ab[0, 4096:6144], ab[0, 10240:12288], ab[1, 4096:6144], ab[1, 10240:12288]]).astype(f32)
    ng = np.concatenate([inp["norm_g"].reshape(4, D), inp["kv_norm_g"].reshape(1, D)])
    norm_gT = np.ascontiguousarray(ng.reshape(5, NKC, 128).transpose(2, 0, 1)).astype(f32)
    a_g = np.zeros((128, 8), f32)
    for i, k in enumerate(["a_q_g", "a_k_g", "a_ki_g", "a_ki_b"]):
        a_g[:, i] = inp[k][0]
    a_g[:, 4] = inp["kv_k_g"]
    a_g[:, 5] = inp["b_q_g"][0]
    consts = np.zeros((128, 5, 128), f32)
    consts[:, 0, :] = np.eye(128, dtype=f32)
    consts[:, 1, :] = 1.0
    consts[:, 2, :] = rot_matT(128)
    consts[:, 3, :] = rot_matT(64)
    consts[:, 4, :] = np.where(np.arange(128)[None, :] > np.arange(128)[:, None], -1e30, 0.0)
    constb = np.zeros((128, 1920), f32)
    constb[:, 0:128] = np.eye(128)
    constb[:, 128:256] = 1.0
    for r in range(4):
        constb[:, 256 + r * 128:256 + (r + 1) * 128] = BIG * np.eye(128)
    constb[:, 768:1792] = mixb_masks().reshape(128, 1024)
    pos_p = np.arange(SEQ)
    c128, s128 = rope_tables(pos_p, 128)
    c64, s64 = rope_tables(pos_p, 64)
    cs_p = np.stack([c128, s128, c64, s64]).astype(f32)
    cwl = np.concatenate([inp["ffn_conv_w"], inp["ffn_conv_b"][:, None, :]], axis=1)
    convw = np.ascontiguousarray(cwl.reshape(2, 4, 88, 128).transpose(3, 0, 2, 1)).astype(f32)
    extra = {}
    if "x_sample" in inp:
        xs = np.zeros((GT, D), f32)
        xs[0:NS * TS] = inp["x_sample"][core * NS:(core + 1) * NS].reshape(NS * TS, D)
        pos_s = np.zeros(GT, np.int64)
        pos_s[0:NS * TS] = np.tile(PAST + np.arange(TS), NS)
        c128, s128 = rope_tables(pos_s, 128)
        c64, s64 = rope_tables(pos_s, 64)
        q = np.arange(128)[:, None]
        kk = np.arange(128)[None, :]
        real = (q < 16) & (kk < 16) & (q // 4 == kk // 4)
        cnegn = np.where(real & (kk % 4 <= q % 4), 0.0, -1e30).astype(f32)
        tiles = []
        tq = q % 4
        for g, (win, dil) in enumerate(B_PAT):
            blks = {0: [15], 1: [12, 13, 14, 15], 2: list(range(16))}[g]
            for blk in blks:
                delta = 2048 + tq - 128 * blk - kk
                valid = (q < 16) & (delta >= 0) & (delta <= win) & (delta % dil == 0)
                tiles.append(np.where(valid, 0.0, -1.0))
            dn = tq - kk % 4
            valid = real & (dn >= 0) & (dn % dil == 0)
            tiles.append(np.where(valid, 0.0, -1.0))
        smask = np.ascontiguousarray(np.stack(tiles, axis=1)).astype(f32)
        sl = slice(core * NS, (core + 1) * NS)
        extra = {
            "xs": xs, "cs_s": np.stack([c128, s128, c64, s64]).astype(f32),
            "ptab": np.ascontiguousarray(inp["page_table"][sl]).astype(np.int32),
            "ck": inp["cache_k_a"][0].reshape(2560 * 128, 512), "cv": inp["cache_v_a"][0].reshape(2560 * 128, 512),
            "cki": inp["cache_kidx_a"][0].reshape(2560 * 128, 128),
            "skb": inp["state_k_b"][sl].reshape(NS, 2048, 1024), "svb": inp["state_v_b"][sl].reshape(NS, 2048, 1024),
            "sconv": np.ascontiguousarray(inp["state_conv"][:, sl].reshape(2, NS * 2, 2 * DFF)),
            "smask": smask, "cnegn": cnegn, "pidx": np.arange(128, dtype=f32).reshape(128, 1),
        }
    return {
        **extra,
        "xp": np.ascontiguousarray(inp["x_prompt"][b]),
        "cT": cT, "ada_w": inp["ada_w"], "kv_ada_w": inp["kv_ada_w"], "biasF": biasF, "gate_b": gate_b,
        "norm_gT": norm_gT, "a_g": a_g, "consts": consts, "constb": constb, "cs_p": cs_p, "convw": convw,
        "a_w_in": np.ascontiguousarray(inp["a_w_in"][0]), "a_w_o": np.ascontiguousarray(inp["a_w_o"][0]),
        "kv_w": inp["kv_w"], "b_w_q": np.ascontiguousarray(inp["b_w_q"][0]),
        "b_w_o": np.ascontiguousarray(inp["b_w_o"][0]),
        "up0": np.ascontiguousarray(inp["ffn_w_up"][0]), "up1": np.ascontiguousarray(inp["ffn_w_up"][1]),
        "dn0": np.ascontiguousarray(inp["ffn_w_down"][0]), "dn1": np.ascontiguousarray(inp["ffn_w_down"][1]),
    }


def kernel(**inputs):
    inp = {k: np.asarray(v) for k, v in inputs.items()}
    nc = build_program(NG=16, stage=6, with_sample=True)
    in_maps = [host_inputs(inp, c) for c in range(8)]
    res = run_bass_kernel_spmd(nc, in_maps, core_ids=list(range(8)))
    r = res.results
    y_prompt = np.stack([r[b]["y_p"] for b in range(2)])
    k_a_p = np.stack([r[b]["k_a_p"].reshape(SEQ, 4, 128) for b in range(2)])[None]
    v_a_p = np.stack([r[b]["v_a_p"].reshape(SEQ, 4, 128) for b in range(2)])[None]
    kidx_a_p = np.stack([r[b]["kidx_a_p"] for b in range(2)])[None]
    k_b_p = np.stack([r[b]["k_b_p"].reshape(2048, 8, 128) for b in range(2)])
    v_b_p = np.stack([r[b]["v_b_p"].reshape(2048, 8, 128) for b in range(2)])
    conv_p = np.stack([r[b]["conv_p"] for b in range(2)], axis=1)
    n16 = NS * TS
    y_sample = np.concatenate([r[c]["y_s"][:n16].reshape(NS, TS, D) for c in range(8)])
    k_a_s = np.concatenate([r[c]["k_a_s"][:n16].reshape(NS, TS, 4, 128) for c in range(8)])[None]
    v_a_s = np.concatenate([r[c]["v_a_s"][:n16].reshape(NS, TS, 4, 128) for c in range(8)])[None]
    kidx_a_s = np.concatenate([r[c]["kidx_a_s"][:n16].reshape(NS, TS, 128) for c in range(8)])[None]
    k_b_s = np.concatenate([r[c]["k_b_s"].reshape(NS, 2048, 8, 128) for c in range(8)])
    v_b_s = np.concatenate([r[c]["v_b_s"].reshape(NS, 2048, 8, 128) for c in range(8)])
    conv_s = np.concatenate([r[c]["conv_s"].reshape(2, NS, 2, 2 * DFF) for c in range(8)], axis=1)
    return (y_prompt, y_sample, k_a_p, v_a_p, kidx_a_p, k_b_p, v_b_p, conv_p,
            k_a_s, v_a_s, kidx_a_s, k_b_s, v_b_s, conv_s)
```

```python
import numpy as np
from contextlib import ExitStack
import concourse.bass as bass
import concourse.mybir as mybir
from concourse.bass_utils import run_bass_kernel_spmd

F32 = mybir.dt.float32
BF16 = mybir.dt.bfloat16
AF = mybir.ActivationFunctionType
ALU = mybir.AluOpType
AX = mybir.AxisListType

D = 2048
SEQ = 4096
NKC = 16
HD = 128
A_IN_W = 5264
EPS = 1e-6
PAST = 8192
NS = 4
TS = 4


class Buf:
    __slots__ = ("name", "w", "r", "excl")

    def __init__(self, name, excl=False):
        self.name = name
        self.w = None
        self.r = {}
        self.excl = excl


class _Rec:
    def __init__(self):
        self.call = None

    def __getattr__(self, name):
        def f(*a, **k):
            self.call = (name, a, k)
            return None
        return f


def _bind(fn):
    r = _Rec()
    fn(r)
    assert r.call is not None
    return r.call


class Sched:
    ENG = ("pe", "act", "dve", "pool", "sp")
    NDMA = 8

    def __init__(self, nc):
        self.nc = nc
        self.prog = {e: [] for e in self.ENG}
        self.cnt = {}
        self.known = {e: {} for e in self.ENG}
        self.dma_idx = {"sp": 0, "pool": 0}
        self.semnames = ["pe", "act", "dve", "pool"] + [f"sp_d{i}" for i in range(self.NDMA)] + \
                        [f"pool_d{i}" for i in range(self.NDMA)]
        for s in self.semnames:
            self.cnt[s] = 0
        self.out_tokens = []

    def _waits(self, eng, reads, writes, extra=()):
        need = {}

        def add(tok):
            if tok is None:
                return
            s, v = tok
            if need.get(s, 0) < v:
                need[s] = v
        for b in reads:
            add(b.w)
            if b.excl:
                for s, v in b.r.items():
                    if s != eng:
                        add((s, v))
        for b in writes:
            add(b.w)
            for s, v in b.r.items():
                add((s, v))
        for t in extra:
            add(t)
        res = []
        kn = self.known[eng]
        for s, v in need.items():
            if eng == "pe" and s == "pe":
                continue
            if kn.get(s, 0) >= v:
                continue
            kn[s] = v
            res.append((s, v))
        return res

    def _commit(self, tok, reads, writes):
        s, v = tok
        for b in writes:
            b.w = tok
            b.r = {}
        for b in reads:
            if b.r.get(s, 0) < v:
                b.r[s] = v

    def op(self, eng, fn, reads=(), writes=(), signal=True):
        import os
        if "E" in os.environ.get("KSKIP", ""):
            signal = True
        waits = self._waits(eng, reads, writes)
        if signal:
            self.cnt[eng] += 1
            tok = (eng, self.cnt[eng])
            self.prog[eng].append((waits, _bind(fn), (eng, 1)))
        else:
            tok = (eng, self.cnt[eng] + 1)
            self.prog[eng].append((waits, _bind(fn), None))
        self._commit(tok, reads, writes)
        return tok

    def dma(self, q, fn, reads=(), writes=(), is_output=False):
        i = self.dma_idx[q]
        self.dma_idx[q] += 1
        sem = f"{q}_d{i % self.NDMA}"
        extra = []
        if self.cnt[sem] > 0:
            extra.append((sem, self.cnt[sem]))
        waits = self._waits(q, reads, writes, extra)
        self.cnt[sem] += 16
        tok = (sem, self.cnt[sem])
        self.prog[q].append((waits, _bind(fn), (sem, 16)))
        self._commit(tok, reads, writes)
        if is_output:
            self.out_tokens.append(tok)
        return tok

    def emit(self):
        nc = self.nc
        final = [(s, self.cnt[s]) for s in self.semnames if "_d" in s and self.cnt[s] > 0]
        with ExitStack() as st:
            sems = {s: st.enter_context(nc.semaphore(s)) for s in self.semnames}
            block = st.enter_context(nc.Block())

            def run(e, prog, fin=False):
                for waits, fn, sig in prog:
                    for ws, wv in waits:
                        e.wait_ge(sems[ws], wv)
                    name, a, k = fn
                    ins = getattr(e, name)(*a, **k)
                    if sig is not None:
                        ins.then_inc(sems[sig[0]], sig[1])
                if fin:
                    for s, v in final:
                        e.wait_ge(sems[s], v)

            @block.tensor
            def _(e):
                run(e, self.prog["pe"])

            @block.scalar
            def _(e):
                run(e, self.prog["act"])

            @block.vector
            def _(e):
                run(e, self.prog["dve"])

            @block.gpsimd
            def _(e):
                run(e, self.prog["pool"])

            @block.sync
            def _(e):
                run(e, self.prog["sp"], fin=True)


def rope_tables(pos, rot_dim):
    half = rot_dim // 2
    inv = (np.float32(10000.0) ** (-np.arange(half, dtype=np.float32) / np.float32(half))).astype(np.float32)
    ang = pos.astype(np.float32)[:, None] * inv[None, :]
    cos = np.ones((128, len(pos)), np.float32)
    sin = np.zeros((128, len(pos)), np.float32)
    c = np.cos(ang).astype(np.float32).T
    s = np.sin(ang).astype(np.float32).T
    cos[:half] = c
    cos[half:rot_dim] = c
    sin[:half] = s
    sin[half:rot_dim] = s
    return cos, sin


def rot_matT(rot_dim):
    half = rot_dim // 2
    m = np.zeros((128, 128), np.float32)
    for i in range(half):
        m[i + half, i] = -1.0
        m[i, i + half] = 1.0
    return m


GT = 256
NTG = GT // 128
DFF = 5632
NFB = 44
BIG = 30000.0
WT_SCALE = float(16 ** -0.5 * 128 ** -0.5)
QK_SCALE = float(128 ** -0.5)
N_BISECT = 24
B_PAT = ((128, 1), (512, 4), (2048, 16))


def mixb_masks():
    tq = np.arange(128)[:, None]
    sk = np.arange(128)[None, :]
    out = []
    for (win, dil), rs in zip(B_PAT, ((0, 1), (0, 1, 4), (0, 1, 16))):
        for r in rs:
            delta = 128 * r + tq - sk
            valid = (delta >= 0) & (delta <= win) & (delta % dil == 0)
            out.append(np.where(valid, 0.0, -1.0).astype(np.float32))
    return np.stack(out, axis=1)


def mixb_mask_idx(g, r):
    if g == 0:
        return {0: 0, 1: 1}[r]
    if g == 1:
        return 2 if r == 0 else (4 if r == 4 else 3)
    return 5 if r == 0 else (7 if r == 16 else 6)


import os
SKIP = set(os.environ.get("KSKIP", "").split(","))


def build_program(NG=16, stage=9, with_sample=False, kb_row0=2048, debug=False):
    nc = bass.Bass("TRN2", target_bir_lowering=False)

    def din(name, shape, dt=F32):
        return nc.dram_tensor(name, list(shape), dt, kind="ExternalInput").ap()

    def dout(name, shape, dt=F32):
        return nc.dram_tensor(name, list(shape), dt, kind="ExternalOutput").ap()

    def dint(name, shape, dt=F32):
        kind = "ExternalOutput" if (debug and name in ("hmid0", "h1d", "hmid1")) else "Internal"
        return nc.dram_tensor(name, list(shape), dt, kind=kind).ap()

    xp = din("xp", [SEQ, D])
    cT = din("cT", [128, NKC, 8])
    ada_w = din("ada_w", [2, D, 6 * D])
    kv_ada_w = din("kv_ada_w", [D, 2 * D])
    biasF = din("biasF", [128, 10, NKC])
    gate_b = din("gate_b", [4, D])
    norm_gT = din("norm_gT", [128, 5, NKC])
    a_g = din("a_g", [128, 8])
    consts = din("consts", [128, 5, 128])
    constb = din("constb", [128, 1920])
    cs_p = din("cs_p", [4, 128, SEQ])
    convw = din("convw", [128, 2, 88, 4])
    w_in = {"a_w_in": din("a_w_in", [D, A_IN_W]), "a_w_o": din("a_w_o", [D, D]), "kv_w": din("kv_w", [D, D]),
            "b_w_q": din("b_w_q", [D, 3072]), "b_w_o": din("b_w_o", [1024, D]),
            "up0": din("up0", [D, 2 * DFF]), "up1": din("up1", [D, 2 * DFF]),
            "dn0": din("dn0", [DFF, D]), "dn1": din("dn1", [DFF, D])}

    y_p = dout("y_p", [SEQ, D]) if stage >= 6 else None
    k_a_p = dout("k_a_p", [SEQ, 512])
    v_a_p = dout("v_a_p", [SEQ, 512])
    kidx_a_p = dout("kidx_a_p", [SEQ, 128])
    k_b_p = dout("k_b_p", [2048, 1024]) if stage >= 4 else None
    v_b_p = dout("v_b_p", [2048, 1024]) if stage >= 4 else None
    conv_p = dout("conv_p", [2, 2, 2 * DFF]) if stage >= 3 else None

    need_keys = ["a_w_in"] + (["a_w_o"] if stage >= 2 else []) + (["up0", "dn0"] if stage >= 3 else []) + \
        (["kv_w"] if stage >= 4 else []) + (["b_w_q", "b_w_o"] if stage >= 5 else []) + (["up1", "dn1"] if stage >= 6 else [])
    I32 = mybir.dt.int32
    if with_sample:
        xs = din("xs", [GT, D]); b_xs = Buf("xs")
        cs_s = din("cs_s", [4, 128, GT])
        ptab = din("ptab", [NS, 64], I32)
        ck = din("ck", [2560 * 128, 512]); cv = din("cv", [2560 * 128, 512]); cki = din("cki", [2560 * 128, 128])
        skb = din("skb", [NS, 2048, 1024]); svb = din("svb", [NS, 2048, 1024])
        sconv = din("sconv", [2, NS * 2, 2 * DFF])
        smask = din("smask", [128, 24, 128])
        cnegn = din("cnegn", [128, 128])
        pidx = din("pidx", [128, 1])
        y_s = dout("y_s", [GT, D]); b_y_s = Buf("y_s")
        k_a_s = dout("k_a_s", [GT, 512]); v_a_s = dout("v_a_s", [GT, 512]); kidx_a_s = dout("kidx_a_s", [GT, 128])
        k_b_s = dout("k_b_s", [NS, 2048, 1024]); v_b_s = dout("v_b_s", [NS, 2048, 1024])
        conv_s = dout("conv_s", [2, NS * 2, 2 * DFF])
        hs0 = dint("hs0", [GT, D]); b_hs0 = Buf("hs0")
        hs1 = dint("hs1", [GT, D]); b_hs1 = Buf("hs1")
        hs2 = dint("hs2", [GT, D]); b_hs2 = Buf("hs2")
    SM = {"on": False}
    wb = {k: dint("wb_" + k, v.shape, BF16) for k, v in w_in.items() if k in need_keys}
    b_wb = {k: Buf("wb_" + k) for k in w_in}
    NTOK = NG * GT
    hmid0 = dint("hmid0", [NTOK, D]) if stage >= 2 else None; b_hmid0 = Buf("hmid0")
    h1d = dint("h1d", [NTOK, D]) if stage >= 3 else None; b_h1d = Buf("h1d")
    hmid1 = dint("hmid1", [NTOK, D]) if stage >= 5 else None; b_hmid1 = Buf("hmid1")
    kTd = dint("kTd", [4, 128, SEQ], BF16); b_kTd = Buf("kTd")
    vd = dint("vd", [SEQ, 512], BF16) if ("V" not in SKIP and "B" not in SKIP) else None; b_vd = Buf("vd")
    kbTd = dint("kbTd", [8, 128, SEQ], BF16) if stage >= 4 else None; b_kbTd = Buf("kbTd")
    vbd = dint("vbd", [SEQ, 1024], BF16) if stage >= 4 else None; b_vbd = Buf("vbd")
    modR = dint("modR", [4, 8, D]); b_modR = Buf("modR")
    b_xp = Buf("xp")

    S = Sched(nc)
    st = ExitStack()

    def sb(name, shape, dt=F32):
        return st.enter_context(nc.sbuf_tensor(name, list(shape), dt))

    cst = sb("cst", [128, 5, 128]); b_cst = Buf("cst")
    cstb = sb("cstb", [128, 1920], BF16); b_cstb = Buf("cstb")
    cTs = sb("cTs", [128, NKC, 8]); b_cTs = Buf("cTs")
    silc = sb("silc", [128, NKC, 8]); b_silc = Buf("silc")
    bF = sb("bF", [128, 10, NKC]); b_bF = Buf("bF")
    ngT = sb("ngT", [128, 5, NKC]); b_ngT = Buf("ngT")
    ag = sb("ag", [128, 8]); b_ag = Buf("ag")
    modF = sb("modF", [128, 10, NKC, 8]); b_modF = Buf("modF")
    modA = sb("modA", [128, 5, NKC, 8]); b_modA = Buf("modA")
    cw = sb("cw", [128, 2, 88, 4]); b_cw = Buf("cw")
    halo = sb("halo", [128, 2, 88, 2]); b_halo = Buf("halo")
    kiT = sb("kiT", [128, SEQ], BF16); b_kiT = Buf("kiT")
    wsm = sb("wsm", [128, NKC, 144], BF16); b_wsm = Buf("wsm")
    xnT = sb("xnT", [128, NKC, GT], BF16); b_xnT = Buf("xnT")
    cs = sb("cs", [128, 4, GT]); b_cs = Buf("cs")
    wbuf = [sb(f"wbuf{i}", [128, 11264], BF16) for i in range(2)]; b_wbuf = [Buf(f"wbuf{i}") for i in range(2)]
    xt = [sb(f"xt{i}", [128, D]) for i in range(2)]; b_xt = [Buf(f"xt{i}") for i in range(2)]
    xsb = sb("xsb", [128, D], BF16); b_xsb = Buf("xsb")
    gb = sb("gb", [128, D]); b_gb = Buf("gb")
    stat = sb("stat", [128, 8]); b_stat = Buf("stat")
    sq = sb("sq", [128, GT]); b_sq = Buf("sq")
    sq2 = sb("sq2", [128, GT]); b_sq2 = Buf("sq2")
    rstd = sb("rstd", [128, GT]); b_rstd = Buf("rstd")
    mean = sb("mean", [128, GT]); b_mean = Buf("mean")
    kn = sb("kn", [128, GT]); b_kn = Buf("kn")
    t1 = sb("t1", [128, GT]); b_t1 = Buf("t1")
    kr = sb("kr", [128, GT]); b_kr = Buf("kr")
    krb = sb("krb", [128, GT], BF16); b_krb = Buf("krb")
    otok = [sb(f"otok{i}", [128, 512]) for i in range(2)]; b_otok = [Buf(f"otok{i}") for i in range(2)]
    otb = [sb(f"otb{i}", [128, 512], BF16) for i in range(2)]; b_otb = [Buf(f"otb{i}") for i in range(2)]
    slabAB = sb("slabAB", [128, 6144]); b_A = Buf("slabA"); b_B = Buf("slabB")
    slabA = slabAB[:, 0:4096]
    slabB = slabAB[:, 4096:6144].bitcast(BF16)
    slabC = sb("slabC", [128, 4096], BF16); b_C = Buf("slabC")
    slabD = sb("slabD", [128, 4096], BF16); b_D = Buf("slabD")
    if "F" in SKIP:
        sb_real = sb
        def sb(name, shape, dt=F32):
            return sb_real(name, [128, 8], dt)
    oT = sb("oT", [128, 16, GT], BF16); b_oT = Buf("oT")
    rl = [sb(f"rl{i}", [128, 512]) for i in range(2)]; b_rl = [Buf(f"rl{i}") for i in range(2)]
    PT = [sb(f"PT{i}", [128, 512], BF16) for i in range(2)]; b_PT = [Buf(f"PT{i}") for i in range(2)]
    kst = [sb(f"kst{i}", [128, 2176], BF16) for i in range(2)]; b_kst = [Buf(f"kst{i}") for i in range(2)]
    vst = [sb(f"vst{i}", [128, 17, 128], BF16) for i in range(2)]; b_vst = [Buf(f"vst{i}") for i in range(2)]
    rec = sb("rec", [128, 512]); b_rec = Buf("rec")
    if "F" in SKIP:
        sb = sb_real
    wtt = sb("wtt", [128, NTG, 16]); b_wtt = Buf("wtt")
    bis = sb("bis", [128, 16]); b_bis = Buf("bis")

    I_ = slabA
    silcP = slabC[:, :].bitcast(F32).rearrange("p (k m) -> p k m", m=128); b_silcP = b_C
    cstf = slabA[:, 0:1920]; b_cstf = b_A
    mb = slabB
    qT = slabC[:, :].rearrange("p (h t) -> p h t", t=GT)
    qiT = slabD[:, :].rearrange("p (h t) -> p h t", t=GT)
    actT_a = slabAB[:, :].bitcast(BF16)
    slabCf = slabC[:, :].bitcast(F32)
    ua = [slabCf[:, i * 260:i * 260 + GT + 2] for i in range(2)]
    ub = [slabCf[:, 520 + i * 260:520 + i * 260 + GT + 2] for i in range(2)]
    ya = slabCf[:, 1040:1040 + GT]
    yb = slabCf[:, 1300:1300 + GT]
    sa = slabCf[:, 1560:1560 + GT]

    SV = {}

    def alias_buf(name, olds):
        nb_ = Buf(name)
        for o in olds:
            if o.w is not None:
                nb_.r[o.w[0]] = max(nb_.r.get(o.w[0], 0), o.w[1])
            for k_, v_ in o.r.items():
                nb_.r[k_] = max(nb_.r.get(k_, 0), v_)
        return nb_

    pbank = [st.enter_context(nc.psum_tensor(f"pb{i}", [128, 512], F32)) for i in range(4)]
    pbank += [st.enter_context(nc.psum_tensor(f"pb{i}", [128, 1024], BF16)) for i in (4, 5)]
    pbank += [st.enter_context(nc.psum_tensor(f"pb{i}", [128, 512], F32)) for i in (6, 7)]
    b_pb = [Buf(f"pb{i}", excl=True) for i in range(8)]
    rr = {"acc": 0, "aux": 0, "tr": 0, "tr_f32": 0, "to": 0, "w": 0, "x": 0, "o": 0, "rl": 0, "pt": 0, "kv": 0, "u": 0}

    def bank(role):
        base = {"acc": 0, "aux": 2, "tr": 4, "tr_f32": 4, "to": 6}[role]
        i = base + rr[role] % 2
        rr[role] += 1
        if role == "tr_f32":
            return pbank[i][:, :].bitcast(F32), b_pb[i]
        return pbank[i], b_pb[i]

    def nxt(role, n=2):
        i = rr[role] % n
        rr[role] += 1
        return i

    ident = cst[:, 0, :]
    ones = cst[:, 1, :]
    R128T = cst[:, 2, :]
    R64T = cst[:, 3, :]
    cneg = cst[:, 4, :]
    identb = cstb[:, 0:128]
    onesb = cstb[:, 128:256]
    bigI4 = cstb[:, 256:768]
    mBm = cstb[:, 768:1792].rearrange("p (m k) -> p m k", k=128)
    zerosb = cstb[:, 1792:1920]

    S.dma("sp", lambda e: e.dma_start(out=cst[:], in_=consts), writes=[b_cst])
    S.dma("sp", lambda e: e.dma_start(out=cstf, in_=constb), writes=[b_cstf])
    S.dma("sp", lambda e: e.dma_start(out=cTs[:], in_=cT), writes=[b_cTs])
    S.dma("sp", lambda e: e.dma_start(out=bF[:], in_=biasF), writes=[b_bF])
    S.dma("sp", lambda e: e.dma_start(out=ngT[:], in_=norm_gT), writes=[b_ngT])
    S.dma("sp", lambda e: e.dma_start(out=ag[:], in_=a_g), writes=[b_ag])
    S.dma("sp", lambda e: e.dma_start(out=cw[:], in_=convw), writes=[b_cw])
    S.op("dve", lambda e: e.tensor_copy(cstb[:], cstf), reads=[b_cstf], writes=[b_cstb])
    S.op("act", lambda e: e.activation(silc[:], cTs[:], AF.Silu), reads=[b_cTs], writes=[b_silc])
    S.op("dve", lambda e: e.memset(halo[:], 0.0), writes=[b_halo])
    S.op("dve", lambda e: e.memset(silcP, 0.0), writes=[b_silcP])
    S.op("dve", lambda e: e.tensor_copy(silcP[:, :, 0:8], silc[:]), reads=[b_silc], writes=[b_silcP])

    touch = sb("touch", [1, 96]); b_touch = Buf("touch")
    if with_sample:
        all_extra = []
    else:
        all_extra = []
    all_in = all_extra + [xp, cT, ada_w, kv_ada_w, biasF, gate_b, norm_gT, a_g, consts, constb, cs_p, convw] + list(w_in.values())
    for ti, ap_ in enumerate(all_in):
        idx = tuple([0] * (len(ap_.shape) - 1) + [slice(0, 2)])
        S.dma("sp", lambda e, ap_=ap_, idx=idx, ti=ti: e.dma_start(out=touch[0:1, 2 * ti:2 * ti + 2], in_=ap_[idx].unsqueeze(0) if len(ap_[idx].shape) == 1 else ap_[idx]),
              writes=[b_touch])

    def cast2d(key, bcols, rows_per):
        src, dst = w_in[key], wb[key]
        R_ = src.shape[0]
        s3 = src.rearrange("r (a b) -> r a b", b=bcols)
        d3 = dst.rearrange("r (a b) -> r a b", b=bcols)
        for r0 in range(0, R_, rows_per):
            S.dma("pool", lambda e, r0=r0: e.dma_start(out=d3[r0:r0 + rows_per], in_=s3[r0:r0 + rows_per]),
                  writes=[b_wb[key]])
    cast2d("a_w_in", 329, 128)
    if stage >= 2:
        cast2d("a_w_o", 1024, 512)
    if stage >= 3:
        cast2d("up0", 1024, 128)
        cast2d("dn0", 1024, 512)
    if stage >= 4:
        cast2d("kv_w", 1024, 512)
    if stage >= 5:
        cast2d("b_w_q", 1024, 256)
        cast2d("b_w_o", 1024, 512)
    if stage >= 6:
        cast2d("up1", 1024, 128)
        cast2d("dn1", 1024, 512)

    fm_list = [(ada_w[0], 0, 0), (ada_w[0], 2048, 1), (ada_w[0], 6144, 2), (ada_w[0], 8192, 3)]
    gate_list = [(ada_w[0], 4096, 0), (ada_w[0], 10240, 1)]
    if stage >= 4:
        fm_list += [(kv_ada_w, 0, 8), (kv_ada_w, 2048, 9)]
    if stage >= 5:
        fm_list += [(ada_w[1], 0, 4), (ada_w[1], 2048, 5), (ada_w[1], 6144, 6), (ada_w[1], 8192, 7)]
        gate_list += [(ada_w[1], 4096, 2), (ada_w[1], 10240, 3)]
    if "F" in SKIP:
        adaw_t = [sb(f"adaw{i}", [128, NKC, 256]) for i in range(2)]
        adaw_v = [adaw_t[i][:, :, :] for i in range(2)]
    else:
        adaw_v = [wbuf[i][:, :].bitcast(F32)[:, 0:4096].rearrange("p (kc c) -> p kc c", c=256) for i in range(2)]
    for (W, col0, m) in fm_list:
        for c4 in range(8):
            wi = nxt("w"); wv_, bw = adaw_v[wi], b_wbuf[wi]
            src = W[:, col0 + c4 * 256: col0 + (c4 + 1) * 256].rearrange("(kc p) c -> p kc c", p=128)
            S.dma("sp", lambda e, wv_=wv_, src=src: e.dma_start(out=wv_, in_=src), writes=[bw])
            pa, bpa = bank("acc")
            for j in range(2):
                for kc in range(NKC):
                    S.op("pe", lambda e, pa=pa, wv_=wv_, j=j, kc=kc: e.matmul(
                        pa[:, j * 8:(j + 1) * 8], wv_[:, kc, j * 128:(j + 1) * 128], silc[:, kc, :],
                        start=(kc == 0), stop=(kc == NKC - 1)), reads=[bw, b_silc], writes=[bpa], signal=(kc == NKC - 1))
            for j in range(2):
                blk = c4 * 2 + j
                S.op("dve", lambda e, pa=pa, j=j, blk=blk, m=m: e.tensor_scalar(
                    modF[:, m, blk, :], pa[:, j * 8:(j + 1) * 8], bF[:, m, blk:blk + 1], None, ALU.add),
                    reads=[bpa, b_bF], writes=[b_modF])
    gbias = gb[0:8, 0:256]
    if "A" in SKIP:
        gate_list = []
    for (W, col0, gidx) in gate_list:
        for c4 in range(8):
            wi = nxt("w"); wv_, bw = adaw_v[wi], b_wbuf[wi]
            src = W[:, col0 + c4 * 256: col0 + (c4 + 1) * 256].rearrange("(kc p) c -> p kc c", p=128)
            S.dma("sp", lambda e, wv_=wv_, src=src: e.dma_start(out=wv_, in_=src), writes=[bw])
            S.dma("sp", lambda e, gidx=gidx, c4=c4: e.dma_start(
                out=gbias, in_=gate_b[gidx:gidx + 1, c4 * 256:(c4 + 1) * 256].partition_broadcast(8)),
                writes=[b_gb])
            pa, bpa = bank("acc")
            for kc in range(NKC):
                S.op("pe", lambda e, pa=pa, wv_=wv_, kc=kc: e.matmul(
                    pa[:, 0:256], silcP[:, kc, :], wv_[:, kc, :], start=(kc == 0), stop=(kc == NKC - 1)),
                    reads=[bw, b_silcP], writes=[bpa], signal=(kc == NKC - 1))
            oi = nxt("o"); ot, bot = otok[oi], b_otok[oi]
            S.op("dve", lambda e, ot=ot, pa=pa: e.tensor_tensor(ot[0:8, 0:256], pa[0:8, 0:256], gbias, ALU.add),
                 reads=[bpa, b_gb], writes=[bot])
            S.dma("pool", lambda e, ot=ot, gidx=gidx, c4=c4: e.dma_start(
                out=modR[gidx, :, c4 * 256:(c4 + 1) * 256], in_=ot[0:8, 0:256]), reads=[bot], writes=[b_modR])

    def make_modA(n, m_scale):
        for kc in range(NKC):
            S.op("dve", lambda e, kc=kc: e.tensor_scalar(
                modA[:, n, kc, :], modF[:, m_scale, kc, :], 1.0, ngT[:, n, kc:kc + 1], ALU.add, ALU.mult),
                reads=[b_modF, b_ngT], writes=[b_modA])
    NORM_MOD = {0: (0, 1), 1: (2, 3), 2: (4, 5), 3: (6, 7), 4: (8, 9)}
    make_modA(0, 1); make_modA(1, 3)
    if stage >= 4:
        make_modA(4, 9)
    if stage >= 5:
        make_modA(2, 5); make_modA(3, 7)

    S.dma("sp", lambda e: e.dma_start(out=wsm[:, :, :],
                                      in_=wb["a_w_in"][:, 5120:5264].rearrange("(kc p) c -> p kc c", p=128)),
          reads=[b_wb["a_w_in"]], writes=[b_wsm])

    def load_w(key, r0, nrows, c0, ncols):
        wi = nxt("w"); wt_, bw = wbuf[wi], b_wbuf[wi]
        nk = nrows // 128
        view = wt_[:, 0:nk * ncols].rearrange("p (kc c) -> p kc c", c=ncols)
        src = wb[key][r0:r0 + nrows, c0:c0 + ncols].rearrange("(kc p) c -> p kc c", p=128)
        S.dma("sp", lambda e: e.dma_start(out=view, in_=src), reads=[b_wb[key]], writes=[bw])
        return view, bw

    def load_norm_T(src_dram, b_src, tok0, n_idx, seq=0):
        m_shift, _ = NORM_MOD[n_idx]
        for j in range(NTG):
            i = nxt("x"); x_, bx = xt[i], b_xt[i]
            S.dma("sp", lambda e, x_=x_, j=j: e.dma_start(out=x_[:, :], in_=src_dram[tok0 + j * 128: tok0 + (j + 1) * 128, :]),
                  reads=[b_src], writes=[bx])
            S.op("act", lambda e, x_=x_: e.activation(xsb[:, :], x_[:, :], AF.Square, accum_out=stat[:, 0:1]),
                 reads=[bx], writes=[b_xsb, b_stat])
            S.op("act", lambda e: e.activation(stat[:, 1:2], stat[:, 0:1], AF.Sqrt, bias=EPS, scale=1.0 / D),
                 reads=[b_stat], writes=[b_stat])
            S.op("dve", lambda e: e.reciprocal(stat[:, 2:3], stat[:, 1:2]), reads=[b_stat], writes=[b_stat])
            S.op("act", lambda e, x_=x_: e.activation(xsb[:, :], x_[:, :], AF.Copy, scale=stat[:, 2:3]),
                 reads=[bx, b_stat], writes=[b_xsb])
            for half in range(2):
                pt_, bpt = bank("tr")
                ptb = pt_
                for k8 in range(8):
                    kc = half * 8 + k8
                    S.op("pe", lambda e, ptb=ptb, kc=kc, k8=k8: e.transpose(
                        ptb[:, k8 * 128:(k8 + 1) * 128], xsb[:, kc * 128:(kc + 1) * 128], identb),
                        reads=[b_xsb, b_cstb], writes=[bpt])
                if SM["on"] and j == 0:
                    ranges = [(0, 4, 1), (4, 8, 2), (8, 12, 3), (12, 16, 4), (16, 128, 1)]
                elif SM["on"]:
                    ranges = [(0, 128, 1)]
                else:
                    ranges = [(0, 128, seq)]
                for k8 in range(8):
                    kc = half * 8 + k8
                    for (c0, c1, sq_) in ranges:
                        S.op("dve", lambda e, ptb=ptb, kc=kc, k8=k8, j=j, c0=c0, c1=c1, sq_=sq_: e.tensor_scalar(
                            xnT[:, kc, j * 128 + c0:j * 128 + c1], ptb[:, k8 * 128 + c0:k8 * 128 + c1],
                            modA[:, n_idx, kc, sq_:sq_ + 1], modF[:, m_shift, kc, sq_:sq_ + 1], ALU.mult, ALU.add),
                            reads=[bpt, b_modA, b_modF], writes=[b_xnT])

    def proj_ws(wview, bw, col_lo, nkc=NKC, rhs_fn=None, brhs=None):
        pa, bpa = bank("acc")
        for kc in range(nkc):
            rhs = xnT[:, kc, :] if rhs_fn is None else rhs_fn(kc)
            S.op("pe", lambda e, pa=pa, kc=kc, rhs=rhs: e.matmul(pa[:, 0:GT], wview[:, kc, col_lo:col_lo + 128], rhs,
                                                                start=(kc == 0), stop=(kc == nkc - 1)),
                 reads=[bw, b_xnT if brhs is None else brhs], writes=[bpa], signal=(kc == nkc - 1))
        return pa, bpa

    def bcast_sum(src_ap, bsrc):
        pa, bpa = bank("aux")
        S.op("pe", lambda e: e.matmul(pa[:, 0:GT], ones, src_ap, start=True, stop=True),
             reads=[bsrc, b_cst], writes=[bpa])
        return pa, bpa

    def rms_head(pa, bpa, gcol):
        S.op("act", lambda e: e.activation(sq[:, :], pa[:, 0:GT], AF.Square), reads=[bpa], writes=[b_sq])
        pq, bpq = bcast_sum(sq[:, :], b_sq)
        S.op("act", lambda e: e.activation(rstd[:, :], pq[:, 0:GT], AF.Sqrt, bias=EPS, scale=1.0 / HD),
             reads=[bpq], writes=[b_rstd])
        S.op("dve", lambda e: e.reciprocal(rstd[:, :], rstd[:, :]), reads=[b_rstd], writes=[b_rstd])
        S.op("dve", lambda e: e.scalar_tensor_tensor(kn[:, :], pa[:, 0:GT], ag[:, gcol:gcol + 1], rstd[:, :],
                                                    ALU.mult, ALU.mult),
             reads=[bpa, b_ag, b_rstd], writes=[b_kn])

    def ln_head(pa, bpa):
        S.op("act", lambda e: e.activation(sq2[:, :], pa[:, 0:GT], AF.Copy), reads=[bpa], writes=[b_sq2])
        S.op("act", lambda e: e.activation(sq[:, :], pa[:, 0:GT], AF.Square), reads=[bpa], writes=[b_sq])
        p1, bp1 = bcast_sum(sq2[:, :], b_sq2)
        p2, bp2 = bcast_sum(sq[:, :], b_sq)
        S.op("act", lambda e: e.activation(mean[:, :], p1[:, 0:GT], AF.Copy, scale=1.0 / HD),
             reads=[bp1], writes=[b_mean])
        S.op("dve", lambda e: e.tensor_tensor(t1[:, :], mean[:, :], mean[:, :], ALU.mult),
             reads=[b_mean], writes=[b_t1])
        S.op("dve", lambda e: e.scalar_tensor_tensor(rstd[:, :], p2[:, 0:GT], 1.0 / HD, t1[:, :],
                                                    ALU.mult, ALU.subtract),
             reads=[bp2, b_t1], writes=[b_rstd])
        S.op("act", lambda e: e.activation(rstd[:, :], rstd[:, :], AF.Sqrt, bias=EPS, scale=1.0),
             reads=[b_rstd], writes=[b_rstd])
        S.op("dve", lambda e: e.reciprocal(rstd[:, :], rstd[:, :]), reads=[b_rstd], writes=[b_rstd])
        S.op("dve", lambda e: e.tensor_tensor(kn[:, :], sq2[:, :], mean[:, :], ALU.subtract),
             reads=[b_sq2, b_mean], writes=[b_kn])
        S.op("dve", lambda e: e.tensor_tensor(kn[:, :], kn[:, :], rstd[:, :], ALU.mult),
             reads=[b_kn, b_rstd], writes=[b_kn])
        S.op("dve", lambda e: e.tensor_scalar(kn[:, :], kn[:, :], ag[:, 2:3], ag[:, 3:4], ALU.mult, ALU.add),
             reads=[b_kn, b_ag], writes=[b_kn])

    def rope(src, bsrc, cidx, RT, dst_bf=None, bdst=None, out_fn=None):
        if bsrc in b_pb:
            S.op("act", lambda e: e.activation(kn[:, :], src, AF.Copy), reads=[bsrc], writes=[b_kn])
            src, bsrc = kn[:, :], b_kn
        pa, bpa = bank("aux")
        S.op("pe", lambda e: e.matmul(pa[:, 0:GT], RT, src, start=True, stop=True),
             reads=[bsrc, b_cst], writes=[bpa])
        S.op("dve", lambda e: e.tensor_tensor(t1[:, :], src, cs[:, cidx, :], ALU.mult),
             reads=[bsrc, b_cs], writes=[b_t1])
        S.op("dve", lambda e: e.tensor_tensor(kr[:, :], pa[:, 0:GT], cs[:, cidx + 1, :], ALU.mult),
             reads=[bpa, b_cs], writes=[b_kr])
        if dst_bf is not None and out_fn is None:
            S.op("dve", lambda e: e.tensor_tensor(dst_bf, kr[:, :], t1[:, :], ALU.add),
                 reads=[b_kr, b_t1], writes=bdst)
            return
        S.op("dve", lambda e: e.tensor_tensor(kr[:, :], kr[:, :], t1[:, :], ALU.add),
             reads=[b_kr, b_t1], writes=[b_kr])
        if dst_bf is not None:
            S.op("act", lambda e: e.activation(dst_bf, kr[:, :], AF.Copy), reads=[b_kr], writes=bdst)
        if out_fn is not None:
            po, bpo = bank("to")
            for j in range(NTG):
                S.op("pe", lambda e, j=j: e.transpose(po[:, j * 128:(j + 1) * 128], kr[:, j * 128:(j + 1) * 128], ident),
                     reads=[b_kr, b_cst], writes=[bpo])
            oi = nxt("o"); ot, bot = otok[oi], b_otok[oi]
            S.op("act", lambda e: e.activation(ot[:, 0:GT], po[:, 0:GT], AF.Copy), reads=[bpo], writes=[bot])
            for j in range(NTG):
                out_fn(j, ot[:, j * 128:(j + 1) * 128], bot)

    def out_dma(dst_ap, src_ap, bsrc, bdst=None):
        S.dma("pool", lambda e: e.dma_start(out=dst_ap, in_=src_ap), reads=[bsrc],
              writes=[] if bdst is None else [bdst], is_output=True)

    def l0_project(G):
        tok0 = G * GT
        smode = SM["on"]
        if smode:
            csrc, xsrc, bxsrc, k_a_o, v_a_o, ki_a_o = cs_s, xs, b_xs, k_a_s, v_a_s, kidx_a_s
        else:
            csrc, xsrc, bxsrc, k_a_o, v_a_o, ki_a_o = cs_p, xp, b_xp, k_a_p, v_a_p, kidx_a_p
        S.dma("sp", lambda e: e.dma_start(out=cs[:, :, :], in_=csrc[:, :, tok0:tok0 + GT].rearrange("a p t -> p a t")),
              writes=[b_cs])
        load_norm_T(xsrc, bxsrc, tok0, 0)
        KSTOP = os.environ.get("KSTOP", "")
        if KSTOP == "a":
            return
        wv_, bw = load_w("a_w_in", 0, D, 2048, 512)
        projs = {0: proj_ws(wv_, bw, 0)}
        for g in range(4):
            pa, bpa = projs[g]
            if g + 1 < 4:
                projs[g + 1] = proj_ws(wv_, bw, (g + 1) * 128)
            rms_head(pa, bpa, 1)

            def k_out(j, ap_, b_, g=g):
                out_dma(k_a_o[tok0 + j * 128: tok0 + (j + 1) * 128, g * 128:(g + 1) * 128], ap_, b_)
            rope(kn[:, :], b_kn, 0, R128T, krb[:, :], [b_krb], k_out)
            if smode:
                S.op("act", lambda e, g=g: e.activation(SV["kTn"][:, g, :], krb[:, 0:128], AF.Copy),
                     reads=[b_krb], writes=[SV["b_kTn"]])
            else:
                S.dma("pool", lambda e, g=g: e.dma_start(out=kTd[g, :, tok0:tok0 + GT], in_=krb[:, :]),
                      reads=[b_krb], writes=[b_kTd])
        if KSTOP == "b":
            return
        pa, bpa = proj_ws(wsm, b_wsm, 0)
        ln_head(pa, bpa)

        def ki_out(j, ap_, b_):
            out_dma(ki_a_o[tok0 + j * 128: tok0 + (j + 1) * 128, :], ap_, b_)
        if smode:
            rope(kn[:, :], b_kn, 2, R64T, kiT[:, 3584:3584 + GT], [b_kiT], ki_out)
        else:
            rope(kn[:, :], b_kn, 2, R64T, kiT[:, tok0:tok0 + GT], [b_kiT], ki_out)
        if KSTOP == "c":
            return
        wv_, bw = load_w("a_w_in", 0, D, 2560, 512)
        for j in range(NTG):
            pa, bpa = bank("to")
            for kc in range(NKC):
                S.op("pe", lambda e, pa=pa, kc=kc, j=j: e.matmul(pa[:, :], xnT[:, kc, j * 128:(j + 1) * 128], wv_[:, kc, :],
                                                             start=(kc == 0), stop=(kc == NKC - 1)),
                     reads=[b_xnT, bw], writes=[bpa], signal=(kc == NKC - 1))
            oi = nxt("o"); ot, bot, ob, bob = otok[oi], b_otok[oi], otb[oi], b_otb[oi]
            S.op("act", lambda e, ot=ot, pa=pa: e.activation(ot[:, :], pa[:, :], AF.Copy), reads=[bpa], writes=[bot])
            if "V" not in SKIP:
                S.op("dve", lambda e, ob=ob, pa=pa: e.tensor_copy(ob[:, :], pa[:, :]), reads=[bpa], writes=[bob])
            out_dma(v_a_o[tok0 + j * 128: tok0 + (j + 1) * 128, :], ot[:, :], bot)
            if smode:
                if j == 0:
                    S.op("pool", lambda e, ob=ob: e.tensor_copy(SV["vnew"][:, :], ob[:, :]), reads=[bob], writes=[SV["b_vnew"]])
            else:
                S.dma("pool", lambda e, ob=ob, j=j: e.dma_start(out=vd[tok0 + j * 128: tok0 + (j + 1) * 128, :], in_=ob[:, :]),
                      reads=[bob], writes=[b_vd])
            if stage >= 2:
                pa, bpa = bank("acc")
                for kc in range(NKC):
                    S.op("pe", lambda e, pa=pa, kc=kc, j=j: e.matmul(pa[:, 0:16], xnT[:, kc, j * 128:(j + 1) * 128],
                                                                 wsm[:, kc, 128:144],
                                                                 start=(kc == 0), stop=(kc == NKC - 1)),
                         reads=[b_xnT, b_wsm], writes=[bpa], signal=(kc == NKC - 1))
                S.op("act", lambda e, pa=pa, j=j: e.activation(wtt[:, j, :], pa[:, 0:16], AF.Copy, scale=WT_SCALE),
                     reads=[bpa], writes=[b_wtt])
        if stage < 2:
            return
        def piped_heads(col0, nheads, post):
            wcache = {}

            def getw(c4):
                if c4 not in wcache:
                    wcache[c4] = load_w(*col0[0], col0[1] + c4 * 512, 512)
                return wcache[c4]

            def issue(h):
                wv_, bw = getw(h // 4)
                return proj_ws(wv_, bw, (h % 4) * 128)
            cur = issue(0)
            for h in range(nheads):
                nx = issue(h + 1) if h + 1 < nheads else None
                post(h, cur[0], cur[1])
                cur = nx

        def post_q(h, pa, bpa):
            rms_head(pa, bpa, 0)
            rope(kn[:, :], b_kn, 0, R128T, qT[:, h, :], [b_C])

        def post_qi(h, pa, bpa):
            rope(pa[:, 0:GT], bpa, 2, R64T, qiT[:, h, :], [b_D])
        piped_heads((("a_w_in", 0, D), 0), 16, post_q)
        piped_heads((("a_w_in", 0, D), 3072), 16, post_qi)

    def l0_attention(G):
        tok0 = G * GT
        for j in range(NTG):
            i = G * NTG + j
            nk = 128 * (i + 1)
            nb = i + 1
            for c0 in range(0, nk, 512):
                c1 = min(nk, c0 + 512)
                for h in range(16):
                    pa, bpa = bank("acc")
                    S.op("pe", lambda e, pa=pa, h=h, c0=c0, c1=c1: e.matmul(
                        pa[:, 0:c1 - c0], qiT[:, h, j * 128:(j + 1) * 128], kiT[:, c0:c1], start=True, stop=True),
                        reads=[b_D, b_kiT], writes=[bpa])
                    ri = nxt("rl"); r_, br = rl[ri], b_rl[ri]
                    S.op("act", lambda e, pa=pa, r_=r_, c0=c0, c1=c1: e.activation(r_[:, 0:c1 - c0], pa[:, 0:c1 - c0], AF.Relu),
                         reads=[bpa], writes=[br])
                    if h == 0:
                        S.op("dve", lambda e, r_=r_, c0=c0, c1=c1: e.tensor_scalar(
                            I_[:, c0:c1], r_[:, 0:c1 - c0], wtt[:, j, 0:1], None, ALU.mult),
                            reads=[br, b_wtt], writes=[b_A])
                    else:
                        S.op("dve", lambda e, r_=r_, h=h, c0=c0, c1=c1: e.scalar_tensor_tensor(
                            I_[:, c0:c1], r_[:, 0:c1 - c0], wtt[:, j, h:h + 1], I_[:, c0:c1], ALU.mult, ALU.add),
                            reads=[br, b_wtt, b_A], writes=[b_A])
            S.op("dve", lambda e: e.tensor_reduce(bis[:, 0:1], I_[:, 0:nk], AX.X, ALU.min), reads=[b_A], writes=[b_bis])
            S.op("dve", lambda e: e.tensor_tensor(I_[:, nk - 128:nk], I_[:, nk - 128:nk], cneg, ALU.add),
                 reads=[b_A, b_cst], writes=[b_A])
            S.op("dve", lambda e: e.tensor_scalar(bis[:, 0:1], bis[:, 0:1], -1.0, None, ALU.add),
                 reads=[b_bis], writes=[b_bis])
            if i >= 2:
                S.op("dve", lambda e: e.tensor_reduce(bis[:, 1:2], I_[:, 0:nk], AX.X, ALU.max), reads=[b_A], writes=[b_bis])
                S.op("dve", lambda e: e.scalar_tensor_tensor(bis[:, 2:3], bis[:, 1:2], 1.0, bis[:, 0:1], ALU.add, ALU.subtract),
                     reads=[b_bis], writes=[b_bis])
                for it in range(N_BISECT):
                    S.op("dve", lambda e, it=it: e.tensor_scalar(bis[:, 3:4], bis[:, 2:3], float(0.5 ** (it + 1)), None, ALU.mult),
                         reads=[b_bis], writes=[b_bis])
                    S.op("dve", lambda e: e.tensor_tensor(bis[:, 4:5], bis[:, 0:1], bis[:, 3:4], ALU.add),
                         reads=[b_bis], writes=[b_bis])
                    S.op("dve", lambda e: e.tensor_scalar(mb[:, 0:nk], I_[:, 0:nk], bis[:, 4:5], 0.0, ALU.is_ge, ALU.add,
                                                         accum_out=bis[:, 5:6]),
                         reads=[b_A, b_bis], writes=[b_B, b_bis])
                    S.op("dve", lambda e: e.tensor_scalar(bis[:, 6:7], bis[:, 5:6], 255.5, None, ALU.is_ge),
                         reads=[b_bis], writes=[b_bis])
                    S.op("dve", lambda e: e.scalar_tensor_tensor(bis[:, 0:1], bis[:, 6:7], bis[:, 3:4], bis[:, 0:1], ALU.mult, ALU.add),
                         reads=[b_bis], writes=[b_bis])
            S.op("dve", lambda e: e.tensor_scalar(mb[:, 0:nk], I_[:, 0:nk], bis[:, 0:1], -1.0, ALU.is_ge, ALU.add),
                 reads=[b_A, b_bis], writes=[b_B])
            for g in range(4):
                po, bpo = bank("aux")
                psm, bpsm = bank("tr_f32")
                for ch0 in range(0, nb, 16):
                    ch1 = min(nb, ch0 + 16)
                    ki_ = nxt("kv"); ks, bks, vs, bvs = kst[ki_], b_kst[ki_], vst[ki_], b_vst[ki_]
                    S.dma("sp", lambda e, ks=ks, g=g, ch0=ch0, ch1=ch1: e.dma_start(
                        out=ks[:, 0:(ch1 - ch0) * 128], in_=kTd[g, :, ch0 * 128:ch1 * 128]),
                        reads=[b_kTd], writes=[bks])
                    S.dma("sp", lambda e, vs=vs, g=g, ch0=ch0, ch1=ch1: e.dma_start(
                        out=vs[:, 0:ch1 - ch0, :],
                        in_=vd[ch0 * 128:ch1 * 128, g * 128:(g + 1) * 128].rearrange("(b p) d -> p b d", p=128)),
                        reads=[b_vd], writes=[bvs])
                    for b in range(ch0, ch1):
                        bl = b - ch0
                        pS, bpS = bank("acc")
                        S.op("pe", lambda e, pS=pS, ks=ks, bl=bl, g=g: e.matmul(
                            pS[:, :].rearrange("p (h t) -> p h t", t=128), ks[:, bl * 128:(bl + 1) * 128],
                            qT[:, 4 * g:4 * g + 4, j * 128:(j + 1) * 128], start=True, stop=False),
                            reads=[bks, b_C], writes=[bpS], signal=False)
                        S.op("pe", lambda e, pS=pS, b=b: e.matmul(pS[:, :], mb[:, b * 128:(b + 1) * 128], bigI4,
                                                               start=False, stop=True),
                             reads=[b_B, b_cstb], writes=[bpS])
                        pi_ = nxt("pt"); P_, bP = PT[pi_], b_PT[pi_]
                        S.op("act", lambda e, P_=P_, pS=pS: e.activation(P_[:, :], pS[:, :], AF.Exp, scale=QK_SCALE),
                             reads=[bpS], writes=[bP])
                        S.op("pe", lambda e, vs=vs, bl=bl, P_=P_, b=b: e.matmul(po[:, :], vs[:, bl, :], P_[:, :],
                                                                           start=(b == 0), stop=(b == nb - 1)),
                             reads=[bvs, bP], writes=[bpo], signal=False)
                        S.op("pe", lambda e, P_=P_, b=b: e.matmul(psm[:, :], onesb, P_[:, :],
                                                               start=(b == 0), stop=(b == nb - 1)),
                             reads=[b_cstb, bP], writes=[bpsm])
                S.op("dve", lambda e: e.reciprocal(rec[:, :], psm[:, :]), reads=[bpsm], writes=[b_rec])
                S.op("dve", lambda e, g=g: e.tensor_tensor(
                    oT[:, 4 * g:4 * g + 4, j * 128:(j + 1) * 128], po[:, :].rearrange("p (h t) -> p h t", t=128),
                    rec[:, :].rearrange("p (h t) -> p h t", t=128), ALU.mult),
                    reads=[bpo, b_rec], writes=[b_oT])

    def out_proj_residual(G, wkey, nk_, gate_idx, src_dram, b_src, dst_dram, b_dst, lhs_fn, b_lhs, seq=0, final_out=False):
        tok0 = G * GT
        sq0 = 1 if SM["on"] else seq
        S.dma("sp", lambda e: e.dma_start(out=gb[:, :], in_=modR[gate_idx, sq0:sq0 + 1, :].partition_broadcast(128)),
              reads=[b_modR], writes=[b_gb])
        if SM["on"]:
            for s_ in range(1, NS):
                S.dma("sp", lambda e, s_=s_: e.dma_start(out=gb[4 * s_:4 * s_ + 4, :],
                                                        in_=modR[gate_idx, 1 + s_:2 + s_, :].partition_broadcast(4)),
                      reads=[b_modR], writes=[b_gb])
        xs_ = []
        for j in range(NTG):
            i = nxt("x"); x_, bx = xt[i], b_xt[i]
            S.dma("sp", lambda e, x_=x_, j=j: e.dma_start(out=x_[:, :], in_=src_dram[tok0 + j * 128: tok0 + (j + 1) * 128, :]),
                  reads=[b_src], writes=[bx])
            xs_.append((x_, bx))
        for c4 in range(4):
            halves = [(0, nk_)] if nk_ <= 22 else [(0, 22), (22, nk_)]
            accs = [bank("to") for _ in range(NTG)]
            for (k0, k1) in halves:
                wv_, bw = load_w(wkey, k0 * 128, (k1 - k0) * 128, c4 * 512, 512)
                for j in range(NTG):
                    pa, bpa = accs[j]
                    for kc in range(k0, k1):
                        S.op("pe", lambda e, pa=pa, kc=kc, j=j, wv_=wv_, k0=k0: e.matmul(
                            pa[:, :], lhs_fn(kc, j), wv_[:, kc - k0, :], start=(kc == 0), stop=(kc == nk_ - 1)),
                            reads=[b_lhs, bw], writes=[bpa], signal=(kc == nk_ - 1 or kc == k1 - 1))
            for j in range(NTG):
                pa, bpa = accs[j]
                x_, bx = xs_[j]
                oi = nxt("o"); ot, bot = otok[oi], b_otok[oi]
                S.op("dve", lambda e, ot=ot, pa=pa, c4=c4: e.tensor_tensor(ot[:, :], pa[:, :], gb[:, c4 * 512:(c4 + 1) * 512], ALU.mult),
                     reads=[bpa, b_gb], writes=[bot])
                S.op("pool", lambda e, ot=ot, x_=x_, c4=c4: e.tensor_tensor(
                    x_[:, c4 * 512:(c4 + 1) * 512], ot[:, :], x_[:, c4 * 512:(c4 + 1) * 512], ALU.add),
                    reads=[bot, bx], writes=[bx])
        for j in range(NTG):
            x_, bx = xs_[j]
            S.dma("pool", lambda e, x_=x_, j=j: e.dma_start(out=dst_dram[tok0 + j * 128: tok0 + (j + 1) * 128, :], in_=x_[:, :]),
                  reads=[bx], writes=[b_dst], is_output=final_out)

    def ffn(G, layer, src_dram, b_src, dst_dram, b_dst, n_idx, gate_idx, final_out=False, last=False):
        tok0 = G * GT
        upk, dnk = ("up0", "dn0") if layer == 0 else ("up1", "dn1")
        load_norm_T(src_dram, b_src, tok0, n_idx)
        if SM["on"]:
            for q_ in range(8):
                for c4 in range(4):
                    S.dma("sp", lambda e, q_=q_, c4=c4: e.dma_start(
                        out=SV["sh"][:, q_, c4 * 22:(c4 + 1) * 22],
                        in_=sconv[layer, q_, c4 * 2816:(c4 + 1) * 2816].rearrange("(b p) -> p b", p=128),
                        allow_slow_non_contiguous=True), writes=[SV["b_sh"]])
        actT = actT_a[:, 0:NFB * GT].rearrange("p (f t) -> p f t", t=GT)
        b_act = [b_A, b_B]
        for c4 in range(11):
            wa, bwa = load_w(upk, 0, D, c4 * 512, 512)
            wb2, bwb2 = load_w(upk, 0, D, DFF + c4 * 512, 512)
            for hh in range(4):
                jf = c4 * 4 + hh
                pa, bpa = proj_ws(wa, bwa, hh * 128)
                pb_, bpb = proj_ws(wb2, bwb2, hh * 128)
                ui = nxt("u"); ua_, ub_ = ua[ui], ub[ui]
                for (u_, p_, bp_, blk) in ((ua_, pa, bpa, jf), (ub_, pb_, bpb, NFB + jf)):
                    S.op("act", lambda e, u_=u_, p_=p_: e.activation(u_[:, 2:2 + GT], p_[:, 0:GT], AF.Copy),
                         reads=[bp_], writes=[b_C])
                    S.op("pool", lambda e, u_=u_, blk=blk: e.tensor_copy(u_[:, 0:2], halo[:, layer, blk, :]),
                         reads=[b_halo], writes=[b_C])
                    S.op("pool", lambda e, u_=u_, blk=blk: e.tensor_copy(halo[:, layer, blk, :], u_[:, GT:GT + 2]),
                         reads=[b_C], writes=[b_halo])
                for (u_, y_, blk) in ((ua_, ya, jf), (ub_, yb, NFB + jf)):
                    S.op("dve", lambda e, u_=u_, y_=y_, blk=blk: e.tensor_scalar(
                        y_, u_[:, 2:2 + GT], cw[:, layer, blk, 2:3], cw[:, layer, blk, 3:4], ALU.mult, ALU.add),
                        reads=[b_C, b_cw], writes=[b_C])
                    if SM["on"]:
                        ue = SV["ue"]; sh = SV["sh"]; so = SV["so"]
                        S.op("dve", lambda e, blk=blk: e.tensor_copy(ue[:, :, 0:2], sh[:, :, blk].rearrange("p (s r) -> p s r", r=2)),
                             reads=[SV["b_sh"]], writes=[SV["b_ue"]])
                        S.op("dve", lambda e, u_=u_: e.tensor_copy(ue[:, :, 2:6], u_[:, 2:18].rearrange("p (s t) -> p s t", t=4)),
                             reads=[b_C], writes=[SV["b_ue"]])
                        y3 = y_[:, 0:16].rearrange("p (s t) -> p s t", t=4)
                        S.op("dve", lambda e, y3=y3, blk=blk: e.tensor_scalar(
                            y3, ue[:, :, 2:6], cw[:, layer, blk, 2:3], cw[:, layer, blk, 3:4], ALU.mult, ALU.add),
                            reads=[SV["b_ue"], b_cw], writes=[b_C])
                        S.op("dve", lambda e, y3=y3, blk=blk: e.scalar_tensor_tensor(
                            y3, ue[:, :, 1:5], cw[:, layer, blk, 1:2], y3, ALU.mult, ALU.add),
                            reads=[SV["b_ue"], b_cw, b_C], writes=[b_C])
                        S.op("dve", lambda e, y3=y3, blk=blk: e.scalar_tensor_tensor(
                            y3, ue[:, :, 0:4], cw[:, layer, blk, 0:1], y3, ALU.mult, ALU.add),
                            reads=[SV["b_ue"], b_cw, b_C], writes=[b_C])
                        S.op("dve", lambda e, blk=blk: e.tensor_copy(so[:, :, blk].rearrange("p (s r) -> p s r", r=2), ue[:, :, 4:6]),
                             reads=[SV["b_ue"]], writes=[SV["b_so"]])
                        continue
                    S.op("dve", lambda e, u_=u_, y_=y_, blk=blk: e.scalar_tensor_tensor(
                        y_, u_[:, 1:1 + GT], cw[:, layer, blk, 1:2], y_, ALU.mult, ALU.add),
                        reads=[b_C, b_cw], writes=[b_C])
                    S.op("dve", lambda e, u_=u_, y_=y_, blk=blk: e.scalar_tensor_tensor(
                        y_, u_[:, 0:GT], cw[:, layer, blk, 0:1], y_, ALU.mult, ALU.add),
                        reads=[b_C, b_cw], writes=[b_C])
                S.op("act", lambda e: e.activation(sa, ya, AF.Silu), reads=[b_C], writes=[b_C])
                S.op("dve", lambda e, jf=jf: e.tensor_tensor(actT[:, jf, :], sa, yb, ALU.mult),
                     reads=[b_C], writes=b_act)
        out_proj_residual(G, dnk, NFB, gate_idx, src_dram, b_src, dst_dram, b_dst,
                          lambda kc, j: actT[:, kc, j * 128:(j + 1) * 128], b_A, final_out=final_out)
        if SM["on"]:
            for q_ in range(8):
                for c4 in range(4):
                    S.dma("pool", lambda e, q_=q_, c4=c4: e.dma_start(
                        out=conv_s[layer, q_, c4 * 2816:(c4 + 1) * 2816].rearrange("(b p) -> p b", p=128),
                        in_=SV["so"][:, q_, c4 * 22:(c4 + 1) * 22], allow_slow_non_contiguous=True),
                        reads=[SV["b_so"]], is_output=True)
        if last:
            ht = otok[0][:, 0:176].rearrange("p (r b) -> p r b", b=88)
            S.op("dve", lambda e: e.tensor_copy(ht, halo[:, layer, :, :].rearrange("p b r -> p r b")),
                 reads=[b_halo], writes=[b_otok[0]])
            for r in range(2):
                for q4 in range(4):
                    S.dma("pool", lambda e, r=r, q4=q4: e.dma_start(
                        out=conv_p[layer, r, q4 * 2816:(q4 + 1) * 2816].rearrange("(b p) -> p b", p=128),
                        in_=ht[:, r, q4 * 22:(q4 + 1) * 22], allow_slow_non_contiguous=True),
                        reads=[b_otok[0]], is_output=True)

    def shared_kv(G):
        tok0 = G * GT
        smode = SM["on"]
        csrc = cs_s if smode else cs_p
        S.dma("sp", lambda e: e.dma_start(out=cs[:, :, :], in_=csrc[:, :, tok0:tok0 + GT].rearrange("a p t -> p a t")),
              writes=[b_cs])
        if smode:
            load_norm_T(hs1, b_hs1, 0, 4)
        else:
            load_norm_T(h1d, b_h1d, tok0, 4)
        do_out = smode or tok0 >= kb_row0
        kvw = {}
        kvp = {}

        def kv_issue(h):
            c4 = h // 4
            if c4 not in kvw:
                kvw[c4] = load_w("kv_w", 0, D, c4 * 512, 512)
            kvp[h] = proj_ws(kvw[c4][0], kvw[c4][1], (h % 4) * 128)
        kv_issue(0)
        for c4 in range(2):
            for hh in range(4):
                h = c4 * 4 + hh
                pa, bpa = kvp[h]
                if h + 1 < 8:
                    kv_issue(h + 1)
                rms_head(pa, bpa, 4)

                def kb_out(j, ap_, b_, h=h):
                    if smode:
                        if j == 0:
                            for s_ in range(NS):
                                out_dma(k_b_s[s_, 2044:2048, h * 128:(h + 1) * 128], ap_[4 * s_:4 * s_ + 4, :], b_)
                        return
                    out_dma(k_b_p[tok0 - kb_row0 + j * 128: tok0 - kb_row0 + (j + 1) * 128, h * 128:(h + 1) * 128], ap_, b_)
                rope(kn[:, :], b_kn, 0, R128T, krb[:, :], [b_krb], kb_out if do_out else None)
                if smode:
                    S.op("act", lambda e, h=h: e.activation(SV["kbn"][:, h, :], krb[:, 0:128], AF.Copy),
                         reads=[b_krb], writes=[SV["b_kbn"]])
                else:
                    S.dma("pool", lambda e, h=h: e.dma_start(out=kbTd[h, :, tok0:tok0 + GT], in_=krb[:, :]),
                          reads=[b_krb], writes=[b_kbTd])
        for c4 in range(2):
            wv_, bw = load_w("kv_w", 0, D, 1024 + c4 * 512, 512)
            for j in range(NTG):
                pa, bpa = bank("to")
                for kc in range(NKC):
                    S.op("pe", lambda e, pa=pa, kc=kc, j=j: e.matmul(pa[:, :], xnT[:, kc, j * 128:(j + 1) * 128], wv_[:, kc, :],
                                                                 start=(kc == 0), stop=(kc == NKC - 1)),
                         reads=[b_xnT, bw], writes=[bpa], signal=(kc == NKC - 1))
                oi = nxt("o"); ot, bot, ob, bob = otok[oi], b_otok[oi], otb[oi], b_otb[oi]
                S.op("dve", lambda e, ob=ob, pa=pa: e.tensor_copy(ob[:, :], pa[:, :]), reads=[bpa], writes=[bob])
                if smode:
                    if j == 0:
                        S.op("pool", lambda e, ob=ob, c4=c4: e.tensor_copy(SV["vbn"][:, c4 * 512:(c4 + 1) * 512], ob[:, :]),
                             reads=[bob], writes=[SV["b_vbn"]])
                        S.op("act", lambda e, ot=ot, pa=pa: e.activation(ot[:, :], pa[:, :], AF.Copy), reads=[bpa], writes=[bot])
                        for s_ in range(NS):
                            out_dma(v_b_s[s_, 2044:2048, c4 * 512:(c4 + 1) * 512], ot[4 * s_:4 * s_ + 4, :], bot)
                    continue
                S.dma("pool", lambda e, ob=ob, j=j, c4=c4: e.dma_start(
                    out=vbd[tok0 + j * 128: tok0 + (j + 1) * 128, c4 * 512:(c4 + 1) * 512], in_=ob[:, :]),
                    reads=[bob], writes=[b_vbd])
                if do_out:
                    S.op("act", lambda e, ot=ot, pa=pa: e.activation(ot[:, :], pa[:, :], AF.Copy), reads=[bpa], writes=[bot])
                    out_dma(v_b_p[tok0 - kb_row0 + j * 128: tok0 - kb_row0 + (j + 1) * 128, c4 * 512:(c4 + 1) * 512],
                            ot[:, :], bot)

    qbT = slabC[:, :]
    def qb_view(hidx):
        if hidx < 16:
            return slabC[:, hidx * GT:(hidx + 1) * GT], b_C
        return slabD[:, (hidx - 16) * GT:(hidx - 15) * GT], b_D

    def l1_mixer(G):
        tok0 = G * GT
        smode = SM["on"]
        csrc = cs_s if smode else cs_p
        S.dma("sp", lambda e: e.dma_start(out=cs[:, :, :], in_=csrc[:, :, tok0:tok0 + GT].rearrange("a p t -> p a t")),
              writes=[b_cs])
        if smode:
            load_norm_T(hs1, b_hs1, 0, 2)
        else:
            load_norm_T(h1d, b_h1d, tok0, 2)
        qbw = {}
        qbp = {}

        def qb_issue(h):
            c4 = h // 4
            if c4 not in qbw:
                qbw[c4] = load_w("b_w_q", 0, D, c4 * 512, 512)
            qbp[h] = proj_ws(qbw[c4][0], qbw[c4][1], (h % 4) * 128)
        qb_issue(0)
        for h in range(24):
            pa, bpa = qbp[h]
            if h + 1 < 24:
                qb_issue(h + 1)
            rms_head(pa, bpa, 5)
            dst, bd = qb_view(h)
            rope(kn[:, :], b_kn, 0, R128T, dst, [bd])
        if smode:
            sample_mixer_B()
            return
        for j in range(NTG):
            i = G * NTG + j
            blocks = [r for r in range(17) if i - r >= 0]
            lo_blk = i - blocks[-1]
            nbl = len(blocks)
            for s4 in range(2):
                po, bpo = bank("aux")
                psm, bpsm = bank("tr_f32")
                for sl in range(4):
                    s = s4 * 4 + sl
                    ki_ = nxt("kv"); ks, bks, vs, bvs = kst[ki_], b_kst[ki_], vst[ki_], b_vst[ki_]
                    S.dma("sp", lambda e, ks=ks, s=s: e.dma_start(
                        out=ks[:, 0:nbl * 128], in_=kbTd[s, :, lo_blk * 128:(i + 1) * 128]),
                        reads=[b_kbTd], writes=[bks])
                    S.dma("sp", lambda e, vs=vs, s=s: e.dma_start(
                        out=vs[:, 0:nbl, :],
                        in_=vbd[lo_blk * 128:(i + 1) * 128, s * 128:(s + 1) * 128].rearrange("(b p) d -> p b d", p=128)),
                        reads=[b_vbd], writes=[bvs])
                    first = True
                    for r in blocks:
                        bl = (i - r) - lo_blk
                        gs = [g for g in range(3) if r <= (1, 4, 16)[g]]
                        ng = len(gs)
                        pS, bpS = bank("acc")
                        for gi, g in enumerate(gs):
                            qv, bq = qb_view(g * 8 + s)
                            S.op("pe", lambda e, pS=pS, ks=ks, bl=bl, qv=qv, gi=gi: e.matmul(
                                pS[:, gi * 128:(gi + 1) * 128], ks[:, bl * 128:(bl + 1) * 128],
                                qv[:, j * 128:(j + 1) * 128], start=(gi == 0), stop=False),
                                reads=[bks, bq], writes=[bpS], signal=False)
                        for gi, g in enumerate(gs):
                            mi = mixb_mask_idx(g, r)
                            S.op("pe", lambda e, pS=pS, mi=mi, gi=gi, ng=ng: e.matmul(
                                pS[:, gi * 128:(gi + 1) * 128], mBm[:, mi, :], bigI4[:, 0:128],
                                start=False, stop=(gi == ng - 1)),
                                reads=[b_cstb], writes=[bpS], signal=(gi == ng - 1))
                        pi_ = nxt("pt"); P_, bP = PT[pi_], b_PT[pi_]
                        S.op("act", lambda e, P_=P_, pS=pS, ng=ng: e.activation(P_[:, 0:ng * 128], pS[:, 0:ng * 128], AF.Exp, scale=QK_SCALE),
                             reads=[bpS], writes=[bP])
                        for gi in range(ng):
                            lastmm = (r == blocks[-1]) and (gi == ng - 1)
                            S.op("pe", lambda e, vs=vs, bl=bl, P_=P_, gi=gi, sl=sl, first=first, lastmm=lastmm: e.matmul(
                                po[:, sl * 128:(sl + 1) * 128], vs[:, bl, :], P_[:, gi * 128:(gi + 1) * 128],
                                start=first, stop=lastmm), reads=[bvs, bP], writes=[bpo], signal=False)
                            S.op("pe", lambda e, P_=P_, gi=gi, sl=sl, first=first, lastmm=lastmm: e.matmul(
                                psm[:, sl * 128:(sl + 1) * 128], onesb, P_[:, gi * 128:(gi + 1) * 128],
                                start=first, stop=lastmm), reads=[b_cstb, bP], writes=[bpsm])
                            first = False
                S.op("dve", lambda e: e.reciprocal(rec[:, :], psm[:, :]), reads=[bpsm], writes=[b_rec])
                S.op("dve", lambda e, s4=s4: e.tensor_tensor(
                    oT[:, 4 * s4:4 * s4 + 4, j * 128:(j + 1) * 128], po[:, :].rearrange("p (h t) -> p h t", t=128),
                    rec[:, :].rearrange("p (h t) -> p h t", t=128), ALU.mult),
                    reads=[bpo, b_rec], writes=[b_oT])

    def smB_mask_idx(g, blk):
        if g == 0:
            return {15: 0, 16: 1}[blk]
        if g == 1:
            return 2 + (blk - 12)
        return 7 + blk

    def sample_mixer_B():
        kf, vf = SV["kf"], SV["vf"]
        S.op("dve", lambda e: e.memset(oT[:, :, :], 0.0), writes=[b_oT])
        for s_ in range(NS):
            po8, bpo8 = bank("aux")
            ps8, bps8 = bank("aux")
            S.op("pe", lambda e, po8=po8: e.matmul(po8[:, 0:32], zerosb, onesb[:, 0:32], start=True, stop=False),
                 reads=[b_cstb], writes=[bpo8], signal=False)
            S.op("pe", lambda e, ps8=ps8: e.matmul(ps8[:, 0:32], zerosb, onesb[:, 0:32], start=True, stop=False),
                 reads=[b_cstb], writes=[bps8], signal=False)
            for blk in range(17):
                if blk < 16:
                    bi = blk % 2
                    S.dma("sp", lambda e, bi=bi, blk=blk, s_=s_: e.dma_start(out=kf[bi][:, :], in_=skb[s_, blk * 128:(blk + 1) * 128, :]),
                          writes=[SV["b_kf"][bi]])
                    S.dma("sp", lambda e, bi=bi, blk=blk, s_=s_: e.dma_start(out=vf[bi][:, :], in_=svb[s_, blk * 128:(blk + 1) * 128, :]),
                          writes=[SV["b_vf"][bi]])
                    for hh in range(2):
                        pt_, bpt = bank("tr_f32")
                        for h4 in range(4):
                            h = hh * 4 + h4
                            S.op("pe", lambda e, pt_=pt_, bi=bi, h=h, h4=h4: e.transpose(
                                pt_[:, h4 * 128:(h4 + 1) * 128], kf[bi][:, h * 128:(h + 1) * 128], ident),
                                reads=[SV["b_kf"][bi], b_cst], writes=[bpt])
                        S.op("act", lambda e, pt_=pt_, hh=hh: e.activation(
                            SV["kblk"][:, hh * 4:(hh + 1) * 4, :], pt_[:, :].rearrange("p (h t) -> p h t", t=128), AF.Copy),
                            reads=[bpt], writes=[SV["b_kblk"]])
                    S.op("pool", lambda e, bi=bi: e.tensor_copy(SV["vblk"][:, :], vf[bi][:, :]),
                         reads=[SV["b_vf"][bi]], writes=[SV["b_vblk"]])
                    kb_, bkb, vb_, bvb = SV["kblk"], SV["b_kblk"], SV["vblk"], SV["b_vblk"]
                else:
                    kb_, bkb, vb_, bvb = SV["kbn"], SV["b_kbn"], SV["vbn"], SV["b_vbn"]
                gs = [g for g in range(3) if (g == 2 or blk == 16 or (g == 1 and blk >= 12) or (g == 0 and blk == 15))]
                ng = len(gs)
                for sl in range(8):
                    pS, bpS = bank("acc")
                    for gi, g in enumerate(gs):
                        qv, bq = qb_view(g * 8 + sl)
                        S.op("pe", lambda e, pS=pS, kb_=kb_, sl=sl, qv=qv, gi=gi, s_=s_: e.matmul(
                            pS[:, gi * 4:(gi + 1) * 4], kb_[:, sl, :], qv[:, 4 * s_:4 * s_ + 4], start=(gi == 0), stop=False),
                            reads=[bkb, bq], writes=[bpS], signal=False)
                    for gi, g in enumerate(gs):
                        mi = smB_mask_idx(g, blk)
                        S.op("pe", lambda e, pS=pS, mi=mi, gi=gi, ng=ng, s_=s_: e.matmul(
                            pS[:, gi * 4:(gi + 1) * 4], SV["msB"][:, mi, :], bigI4[:, 4 * s_:4 * s_ + 4],
                            start=False, stop=(gi == ng - 1)),
                            reads=[b_kiT, b_cstb], writes=[bpS], signal=(gi == ng - 1))
                    pi_ = nxt("pt"); P_, bP = PT[pi_], b_PT[pi_]
                    S.op("act", lambda e, P_=P_, pS=pS, ng=ng: e.activation(P_[:, 0:ng * 4], pS[:, 0:ng * 4], AF.Exp, scale=QK_SCALE),
                         reads=[bpS], writes=[bP])
                    for gi in range(ng):
                        lastmm = (blk == 16) and (sl == 7) and (gi == ng - 1)
                        S.op("pe", lambda e, vb_=vb_, P_=P_, gi=gi, sl=sl, lastmm=lastmm, po8=po8: e.matmul(
                            po8[:, sl * 4:(sl + 1) * 4], vb_[:, sl * 128:(sl + 1) * 128], P_[:, gi * 4:(gi + 1) * 4],
                            start=False, stop=lastmm), reads=[bvb, bP], writes=[bpo8], signal=False)
                        S.op("pe", lambda e, P_=P_, gi=gi, sl=sl, lastmm=lastmm, ps8=ps8: e.matmul(
                            ps8[:, sl * 4:(sl + 1) * 4], onesb, P_[:, gi * 4:(gi + 1) * 4],
                            start=False, stop=lastmm), reads=[b_cstb, bP], writes=[bps8])
            S.op("dve", lambda e, ps8=ps8: e.reciprocal(rec[:, 0:32], ps8[:, 0:32]), reads=[bps8], writes=[b_rec])
            S.op("dve", lambda e, po8=po8, s_=s_: e.tensor_tensor(
                oT[:, 0:8, 4 * s_:4 * s_ + 4], po8[:, 0:32].rearrange("p (h t) -> p h t", t=4),
                rec[:, 0:32].rearrange("p (h t) -> p h t", t=4), ALU.mult),
                reads=[bpo8, b_rec], writes=[b_oT])

    def sample_attn_A():
        I0 = slabA
        I1 = wbuf[0][:, :].bitcast(F32)[:, 0:4224]
        mb0 = slabB
        mb1 = wbuf[1][:, 0:4224]
        bI = [b_A, b_wbuf[0]]
        bM = [b_B, b_wbuf[1]]
        kin = kiT[:, 3584:3712]
        kpg, vpg, kipg = SV["kpg"], SV["vpg"], SV["kipg"]

        def Iseg(c0, c1):
            if c1 <= 4096:
                return I0[:, c0:c1], bI[0]
            return I1[:, c0 - 4096:c1 - 4096], bI[1]
        for s_ in range(NS):
            S.dma("sp", lambda e, s_=s_: e.dma_start(out=SV["pti"][:, :], in_=ptab[s_:s_ + 1, :].partition_broadcast(128)),
                  writes=[SV["b_pti"]])
            S.op("dve", lambda e: e.tensor_scalar(SV["idx"][:, :], SV["pti"][:, :], 128.0, SV["pidx"][:, 0:1], ALU.mult, ALU.add),
                 reads=[SV["b_pti"], SV["b_pidx"]], writes=[SV["b_idx"]])
            for c in range(17):
                if c < 16:
                    pt_, bpt = bank("tr_f32")
                    for pg in range(4):
                        jpg = c * 4 + pg
                        bi = nxt("kv")
                        S.dma("pool", lambda e, bi=bi, jpg=jpg: e.indirect_dma_start(
                            out=kipg[bi][:, :], out_offset=None, in_=cki,
                            in_offset=bass.IndirectOffsetOnAxis(ap=SV["idx"][:, jpg:jpg + 1], axis=0)),
                            reads=[SV["b_idx"]], writes=[SV["b_kipg"][bi]])
                        S.op("pe", lambda e, pt_=pt_, bi=bi, pg=pg: e.transpose(
                            pt_[:, pg * 128:(pg + 1) * 128], kipg[bi][:, :], ident),
                            reads=[SV["b_kipg"][bi], b_cst], writes=[bpt])
                    S.op("act", lambda e, pt_=pt_: e.activation(SV["kiC"][:, :], pt_[:, :], AF.Copy),
                         reads=[bpt], writes=[SV["b_kiC"]])
                    keys, bkeys, width = SV["kiC"][:, :], SV["b_kiC"], 512
                else:
                    keys, bkeys, width = kin, b_kiT, 128
                c0 = c * 512
                seg, bseg = Iseg(c0, c0 + width)
                for h in range(16):
                    pa, bpa = bank("acc")
                    S.op("pe", lambda e, pa=pa, h=h, keys=keys, width=width: e.matmul(
                        pa[:, 0:width], qiT[:, h, 0:128], keys, start=True, stop=True),
                        reads=[b_D, bkeys], writes=[bpa])
                    ri = nxt("rl"); r_, br = rl[ri], b_rl[ri]
                    S.op("act", lambda e, pa=pa, r_=r_, width=width: e.activation(r_[:, 0:width], pa[:, 0:width], AF.Relu),
                         reads=[bpa], writes=[br])
                    if h == 0:
                        S.op("dve", lambda e, r_=r_, seg=seg, width=width: e.tensor_scalar(
                            seg, r_[:, 0:width], wtt[:, 0, 0:1], None, ALU.mult), reads=[br, b_wtt], writes=[bseg])
                    else:
                        S.op("dve", lambda e, r_=r_, h=h, seg=seg, width=width: e.scalar_tensor_tensor(
                            seg, r_[:, 0:width], wtt[:, 0, h:h + 1], seg, ALU.mult, ALU.add),
                            reads=[br, b_wtt, bseg], writes=[bseg])
            segs = [(I0[:, 0:4096], bI[0], mb0[:, 0:4096], bM[0]), (I1[:, 0:4224], bI[1], mb1[:, 0:4224], bM[1])]
            S.op("dve", lambda e: e.tensor_reduce(bis[:, 0:1], I0[:, 0:4096], AX.X, ALU.min), reads=[bI[0]], writes=[b_bis])
            S.op("dve", lambda e: e.tensor_reduce(bis[:, 8:9], I1[:, 0:4224], AX.X, ALU.min), reads=[bI[1]], writes=[b_bis])
            S.op("dve", lambda e: e.tensor_tensor(bis[:, 0:1], bis[:, 0:1], bis[:, 8:9], ALU.min), reads=[b_bis], writes=[b_bis])
            S.op("dve", lambda e: e.tensor_tensor(I1[:, 4096:4224], I1[:, 4096:4224], SV["cnegN"][:, :], ALU.add),
                 reads=[bI[1], SV["b_cnegN"]], writes=[bI[1]])
            S.op("dve", lambda e: e.tensor_scalar(bis[:, 0:1], bis[:, 0:1], -1.0, None, ALU.add), reads=[b_bis], writes=[b_bis])
            S.op("dve", lambda e: e.tensor_reduce(bis[:, 1:2], I0[:, 0:4096], AX.X, ALU.max), reads=[bI[0]], writes=[b_bis])
            S.op("dve", lambda e: e.tensor_reduce(bis[:, 8:9], I1[:, 0:4224], AX.X, ALU.max), reads=[bI[1]], writes=[b_bis])
            S.op("dve", lambda e: e.tensor_tensor(bis[:, 1:2], bis[:, 1:2], bis[:, 8:9], ALU.max), reads=[b_bis], writes=[b_bis])
            S.op("dve", lambda e: e.scalar_tensor_tensor(bis[:, 2:3], bis[:, 1:2], 1.0, bis[:, 0:1], ALU.add, ALU.subtract),
                 reads=[b_bis], writes=[b_bis])
            for it in range(N_BISECT):
                S.op("dve", lambda e, it=it: e.tensor_scalar(bis[:, 3:4], bis[:, 2:3], float(0.5 ** (it + 1)), None, ALU.mult),
                     reads=[b_bis], writes=[b_bis])
                S.op("dve", lambda e: e.tensor_tensor(bis[:, 4:5], bis[:, 0:1], bis[:, 3:4], ALU.add), reads=[b_bis], writes=[b_bis])
                for si, (Is, bIs, ms, bms) in enumerate(segs):
                    S.op("dve", lambda e, Is=Is, ms=ms, si=si: e.tensor_scalar(ms, Is, bis[:, 4:5], 0.0, ALU.is_ge, ALU.add,
                                                                           accum_out=bis[:, 9 + si:10 + si]),
                         reads=[bIs, b_bis], writes=[bms, b_bis])
                S.op("dve", lambda e: e.tensor_tensor(bis[:, 5:6], bis[:, 9:10], bis[:, 10:11], ALU.add), reads=[b_bis], writes=[b_bis])
                S.op("dve", lambda e: e.tensor_scalar(bis[:, 6:7], bis[:, 5:6], 255.5, None, ALU.is_ge), reads=[b_bis], writes=[b_bis])
                S.op("dve", lambda e: e.scalar_tensor_tensor(bis[:, 0:1], bis[:, 6:7], bis[:, 3:4], bis[:, 0:1], ALU.mult, ALU.add),
                     reads=[b_bis], writes=[b_bis])
            for (Is, bIs, ms, bms) in segs:
                S.op("dve", lambda e, Is=Is, ms=ms: e.tensor_scalar(ms, Is, bis[:, 0:1], -1.0, ALU.is_ge, ALU.add),
                     reads=[bIs, b_bis], writes=[bms])
            po4, bpo4 = bank("aux")
            ps4, bps4 = bank("aux")
            S.op("pe", lambda e, po4=po4: e.matmul(po4[:, 0:64], zerosb, onesb[:, 0:64], start=True, stop=False),
                 reads=[b_cstb], writes=[bpo4], signal=False)
            S.op("pe", lambda e, ps4=ps4: e.matmul(ps4[:, 0:64], zerosb, onesb[:, 0:64], start=True, stop=False),
                 reads=[b_cstb], writes=[bps4], signal=False)
            for jb in range(65):
                if jb < 64:
                    bi = jb % 2
                    S.dma("pool", lambda e, bi=bi, jb=jb: e.indirect_dma_start(
                        out=kpg[bi][:, :], out_offset=None, in_=ck,
                        in_offset=bass.IndirectOffsetOnAxis(ap=SV["idx"][:, jb:jb + 1], axis=0)),
                        reads=[SV["b_idx"]], writes=[SV["b_kpg"][bi]])
                    S.dma("pool", lambda e, bi=bi, jb=jb: e.indirect_dma_start(
                        out=vpg[bi][:, :], out_offset=None, in_=cv,
                        in_offset=bass.IndirectOffsetOnAxis(ap=SV["idx"][:, jb:jb + 1], axis=0)),
                        reads=[SV["b_idx"]], writes=[SV["b_vpg"][bi]])
                    pt_, bpt = bank("tr_f32")
                    for g in range(4):
                        S.op("pe", lambda e, pt_=pt_, bi=bi, g=g: e.transpose(
                            pt_[:, g * 128:(g + 1) * 128], kpg[bi][:, g * 128:(g + 1) * 128], ident),
                            reads=[SV["b_kpg"][bi], b_cst], writes=[bpt])
                    S.op("act", lambda e, pt_=pt_: e.activation(SV["kpT"][:, :, :], pt_[:, :].rearrange("p (g t) -> p g t", t=128), AF.Copy),
                         reads=[bpt], writes=[SV["b_kpT"]])
                    S.op("pool", lambda e, bi=bi: e.tensor_copy(SV["vpb"][:, :], vpg[bi][:, :]),
                         reads=[SV["b_vpg"][bi]], writes=[SV["b_vpb"]])
                    kT_, bkT, vb_, bvb = SV["kpT"], SV["b_kpT"], SV["vpb"], SV["b_vpb"]
                else:
                    kT_, bkT, vb_, bvb = SV["kTn"], SV["b_kTn"], SV["vnew"], SV["b_vnew"]
                if jb < 32:
                    mseg, bmseg = mb0[:, jb * 128:(jb + 1) * 128], bM[0]
                else:
                    mseg, bmseg = mb1[:, (jb - 32) * 128:(jb - 31) * 128], bM[1]
                for g in range(4):
                    pS, bpS = bank("acc")
                    S.op("pe", lambda e, pS=pS, kT_=kT_, g=g, s_=s_: e.matmul(
                        pS[:, 0:16].rearrange("p (h t) -> p h t", t=4), kT_[:, g, :],
                        qT[:, 4 * g:4 * g + 4, 4 * s_:4 * s_ + 4], start=True, stop=False),
                        reads=[bkT, b_C], writes=[bpS], signal=False)
                    S.op("pe", lambda e, pS=pS, mseg=mseg, s_=s_: e.matmul(
                        pS[:, 0:16].rearrange("p (h t) -> p h t", t=4), mseg,
                        bigI4.rearrange("p (h t) -> p h t", t=128)[:, :, 4 * s_:4 * s_ + 4], start=False, stop=True),
                        reads=[bmseg, b_cstb], writes=[bpS])
                    pi_ = nxt("pt"); P_, bP = PT[pi_], b_PT[pi_]
                    S.op("act", lambda e, P_=P_, pS=pS: e.activation(P_[:, 0:16], pS[:, 0:16], AF.Exp, scale=QK_SCALE),
                         reads=[bpS], writes=[bP])
                    lastmm = (jb == 64) and (g == 3)
                    S.op("pe", lambda e, vb_=vb_, P_=P_, g=g, lastmm=lastmm, po4=po4: e.matmul(
                        po4[:, g * 16:(g + 1) * 16], vb_[:, g * 128:(g + 1) * 128], P_[:, 0:16], start=False, stop=lastmm),
                        reads=[bvb, bP], writes=[bpo4], signal=False)
                    S.op("pe", lambda e, P_=P_, g=g, lastmm=lastmm, ps4=ps4: e.matmul(
                        ps4[:, g * 16:(g + 1) * 16], onesb, P_[:, 0:16], start=False, stop=lastmm),
                        reads=[b_cstb, bP], writes=[bps4])
            S.op("dve", lambda e, ps4=ps4: e.reciprocal(rec[:, 0:64], ps4[:, 0:64]), reads=[bps4], writes=[b_rec])
            S.op("dve", lambda e, po4=po4, s_=s_: e.tensor_tensor(
                oT[:, :, 4 * s_:4 * s_ + 4], po4[:, 0:64].rearrange("p (h t) -> p h t", t=4),
                rec[:, 0:64].rearrange("p (h t) -> p h t", t=4), ALU.mult),
                reads=[bpo4, b_rec], writes=[b_oT])

    def sample_setup():
        kv_olds = b_kst + b_vst
        r0 = kst[0][:, :].bitcast(F32)
        r1 = kst[1][:, :].bitcast(F32)
        r2 = vst[0][:, :, :].rearrange("p a b -> p (a b)").bitcast(F32)
        r3 = vst[1][:, :, :].rearrange("p a b -> p (a b)")
        SV["kpg"] = [r0[:, 0:512], r0[:, 512:1024]]; SV["b_kpg"] = [alias_buf("kpg0", kv_olds), alias_buf("kpg1", kv_olds)]
        SV["vpg"] = [r1[:, 0:512], r1[:, 512:1024]]; SV["b_vpg"] = [alias_buf("vpg0", kv_olds), alias_buf("vpg1", kv_olds)]
        SV["kipg"] = [r2[:, 0:128], r2[:, 128:256]]; SV["b_kipg"] = [alias_buf("kipg0", kv_olds), alias_buf("kipg1", kv_olds)]
        SV["cnegN"] = r2[:, 256:384]; SV["b_cnegN"] = alias_buf("cnegN", kv_olds)
        SV["ue"] = r2[:, 384:408].rearrange("p (s t) -> p s t", t=6); SV["b_ue"] = alias_buf("ue", kv_olds)
        SV["pidx"] = r2[:, 408:409]; SV["b_pidx"] = alias_buf("pidx", kv_olds)
        SV["pti"] = r2[:, 416:480].bitcast(I32); SV["b_pti"] = alias_buf("pti", kv_olds)
        SV["idx"] = r2[:, 480:544].bitcast(I32); SV["b_idx"] = alias_buf("idx", kv_olds)
        SV["kiC"] = r2[:, 544:800].bitcast(BF16); SV["b_kiC"] = alias_buf("kiC", kv_olds)
        SV["kpT"] = r3[:, 0:512].rearrange("p (g t) -> p g t", t=128); SV["b_kpT"] = alias_buf("kpT", kv_olds)
        SV["vpb"] = r3[:, 512:1024]; SV["b_vpb"] = alias_buf("vpb", kv_olds)
        SV["kTn"] = r3[:, 1024:1536].rearrange("p (g t) -> p g t", t=128); SV["b_kTn"] = alias_buf("kTn", kv_olds)
        SV["vnew"] = r3[:, 1536:2048]; SV["b_vnew"] = alias_buf("vnew", kv_olds)
        SV["sh"] = slabD[:, :].bitcast(F32)[:, 0:704].rearrange("p (q b) -> p q b", b=88); SV["b_sh"] = b_D
        SV["so"] = oT[:, :, :].rearrange("p a b -> p (a b)").bitcast(F32)[:, 0:704].rearrange("p (q b) -> p q b", b=88)
        SV["b_so"] = b_oT
        ab = slabAB[:, :]
        SV["kf"] = [ab[:, 0:1024], ab[:, 1024:2048]]; SV["vf"] = [ab[:, 2048:3072], ab[:, 3072:4096]]
        abb = ab.bitcast(BF16)
        SV["kblk"] = abb[:, 8192:9216].rearrange("p (h t) -> p h t", t=128)
        SV["vblk"] = abb[:, 9216:10240]
        SV["kbn"] = abb[:, 10240:11264].rearrange("p (h t) -> p h t", t=128)
        SV["vbn"] = abb[:, 11264:12288]
        for nm in ("kblk", "vblk", "kbn", "vbn"):
            SV["b_" + nm] = b_B
        SV["b_kf"] = [b_A, b_A]; SV["b_vf"] = [b_A, b_A]
        SV["msB"] = kiT[:, 0:3072].rearrange("p (m k) -> p m k", k=128)
        for m8 in range(3):
            S.dma("pool", lambda e, m8=m8: e.dma_start(out=SV["msB"][:, m8 * 8:(m8 + 1) * 8, :], in_=smask[:, m8 * 8:(m8 + 1) * 8, :]),
                  writes=[b_kiT])
        S.dma("sp", lambda e: e.dma_start(out=SV["cnegN"], in_=cnegn), writes=[SV["b_cnegN"]])
        S.dma("sp", lambda e: e.dma_start(out=SV["pidx"], in_=pidx), writes=[SV["b_pidx"]])
        S.op("dve", lambda e: e.memset(oT[:, :, :], 0.0), writes=[b_oT])
        for s_ in range(NS):
            for q4 in range(4):
                for (dst_, src_) in ((k_b_s, skb), (v_b_s, svb)):
                    S.dma("sp", lambda e, dst_=dst_, src_=src_, s_=s_, q4=q4: e.dma_start(
                        out=dst_[s_, q4 * 511:(q4 + 1) * 511, :], in_=src_[s_, 4 + q4 * 511:4 + (q4 + 1) * 511, :]),
                        is_output=True)

    def sample_group():
        SM["on"] = True
        sample_setup()
        l0_project(0)
        sample_attn_A()
        out_proj_residual(0, "a_w_o", 16, 0, xs, b_xs, hs0, b_hs0,
                          lambda kc, j: oT[:, kc, j * 128:(j + 1) * 128], b_oT)
        ffn(0, 0, hs0, b_hs0, hs1, b_hs1, 1, 1)
        shared_kv(0)
        l1_mixer(0)
        out_proj_residual(0, "b_w_o", 8, 2, hs1, b_hs1, hs2, b_hs2,
                          lambda kc, j: oT[:, kc, j * 128:(j + 1) * 128], b_oT)
        ffn(0, 1, hs2, b_hs2, y_s, b_y_s, 3, 3, final_out=True)
        SM["on"] = False

    for G in range(NG):
        l0_project(G)
        if stage >= 2:
            l0_attention(G)
            out_proj_residual(G, "a_w_o", 16, 0, xp, b_xp, hmid0, b_hmid0,
                              lambda kc, j: oT[:, kc, j * 128:(j + 1) * 128], b_oT)
        if stage >= 3:
            ffn(G, 0, hmid0, b_hmid0, h1d, b_h1d, 1, 1, last=(G == NG - 1))
        if stage >= 4:
            shared_kv(G)
    if stage >= 5:
        for G in range(NG):
            l1_mixer(G)
            out_proj_residual(G, "b_w_o", 8, 2, h1d, b_h1d, hmid1, b_hmid1,
                              lambda kc, j: oT[:, kc, j * 128:(j + 1) * 128], b_oT)
            if stage >= 6:
                ffn(G, 1, hmid1, b_hmid1, y_p, Buf("y_p"), 3, 3, final_out=True, last=(G == NG - 1))

    if with_sample:
        sample_group()

    S.emit()
    st.close()
    return nc


def host_inputs(inp, core):
    b = core % 2
    f32 = np.float32
    c_all = np.zeros((8, D), f32)
    c_all[0] = inp["c_prompt"][b]
    if "c_sample" in inp:
        c_all[1:1 + NS] = inp["c_sample"][core * NS:(core + 1) * NS]
    cT = np.ascontiguousarray(c_all.reshape(8, NKC, 128).transpose(2, 1, 0))
    ab = inp["ada_b"]
    rows = [ab[0, 0:2048], ab[0, 2048:4096], ab[0, 6144:8192], ab[0, 8192:10240],
            ab[1, 0:2048], ab[1, 2048:4096], ab[1, 6144:8192], ab[1, 8192:10240],
            inp["kv_ada_b"][0:2048], inp["kv_ada_b"][2048:4096]]
    biasF = np.ascontiguousarray(np.stack(rows).reshape(10, NKC, 128).transpose(2, 0, 1)).astype(f32)
    gate_b = np.stack([ab[0, 4096:6144], ab[0, 10240:12288], ab[1, 4096:6144], ab[1, 10240:12288]]).astype(f32)
    ng = np.concatenate([inp["norm_g"].reshape(4, D), inp["kv_norm_g"].reshape(1, D)])
    norm_gT = np.ascontiguousarray(ng.reshape(5, NKC, 128).transpose(2, 0, 1)).astype(f32)
    a_g = np.zeros((128, 8), f32)
    for i, k in enumerate(["a_q_g", "a_k_g", "a_ki_g", "a_ki_b"]):
        a_g[:, i] = inp[k][0]
    a_g[:, 4] = inp["kv_k_g"]
    a_g[:, 5] = inp["b_q_g"][0]
    consts = np.zeros((128, 5, 128), f32)
    consts[:, 0, :] = np.eye(128, dtype=f32)
    consts[:, 1, :] = 1.0
    consts[:, 2, :] = rot_matT(128)
    consts[:, 3, :] = rot_matT(64)
    consts[:, 4, :] = np.where(np.arange(128)[None, :] > np.arange(128)[:, None], -1e30, 0.0)
    constb = np.zeros((128, 1920), f32)
    constb[:, 0:128] = np.eye(128)
    constb[:, 128:256] = 1.0
    for r in range(4):
        constb[:, 256 + r * 128:256 + (r + 1) * 128] = BIG * np.eye(128)
    constb[:, 768:1792] = mixb_masks().reshape(128, 1024)
    pos_p = np.arange(SEQ)
    c128, s128 = rope_tables(pos_p, 128)
    c64, s64 = rope_tables(pos_p, 64)
    cs_p = np.stack([c128, s128, c64, s64]).astype(f32)
    cwl = np.concatenate([inp["ffn_conv_w"], inp["ffn_conv_b"][:, None, :]], axis=1)
    convw = np.ascontiguousarray(cwl.reshape(2, 4, 88, 128).transpose(3, 0, 2, 1)).astype(f32)
    extra = {}
    if "x_sample" in inp:
        xs = np.zeros((GT, D), f32)
        xs[0:NS * TS] = inp["x_sample"][core * NS:(core + 1) * NS].reshape(NS * TS, D)
        pos_s = np.zeros(GT, np.int64)
        pos_s[0:NS * TS] = np.tile(PAST + np.arange(TS), NS)
        c128, s128 = rope_tables(pos_s, 128)
        c64, s64 = rope_tables(pos_s, 64)
        q = np.arange(128)[:, None]
        kk = np.arange(128)[None, :]
        real = (q < 16) & (kk < 16) & (q // 4 == kk // 4)
        cnegn = np.where(real & (kk % 4 <= q % 4), 0.0, -1e30).astype(f32)
        tiles = []
        tq = q % 4
        for g, (win, dil) in enumerate(B_PAT):
            blks = {0: [15], 1: [12, 13, 14, 15], 2: list(range(16))}[g]
            for blk in blks:
                delta = 2048 + tq - 128 * blk - kk
                valid = (q < 16) & (delta >= 0) & (delta <= win) & (delta % dil == 0)
                tiles.append(np.where(valid, 0.0, -1.0))
            dn = tq - kk % 4
            valid = real & (dn >= 0) & (dn % dil == 0)
            tiles.append(np.where(valid, 0.0, -1.0))
        smask = np.ascontiguousarray(np.stack(tiles, axis=1)).astype(f32)
        sl = slice(core * NS, (core + 1) * NS)
        extra = {
            "xs": xs, "cs_s": np.stack([c128, s128, c64, s64]).astype(f32),
            "ptab": np.ascontiguousarray(inp["page_table"][sl]).astype(np.int32),
            "ck": inp["cache_k_a"][0].reshape(2560 * 128, 512), "cv": inp["cache_v_a"][0].reshape(2560 * 128, 512),
            "cki": inp["cache_kidx_a"][0].reshape(2560 * 128, 128),
            "skb": inp["state_k_b"][sl].reshape(NS, 2048, 1024), "svb": inp["state_v_b"][sl].reshape(NS, 2048, 1024),
            "sconv": np.ascontiguousarray(inp["state_conv"][:, sl].reshape(2, NS * 2, 2 * DFF)),
            "smask": smask, "cnegn": cnegn, "pidx": np.arange(128, dtype=f32).reshape(128, 1),
        }
    return {
        **extra,
        "xp": np.ascontiguousarray(inp["x_prompt"][b]),
        "cT": cT, "ada_w": inp["ada_w"], "kv_ada_w": inp["kv_ada_w"], "biasF": biasF, "gate_b": gate_b,
        "norm_gT": norm_gT, "a_g": a_g, "consts": consts, "constb": constb, "cs_p": cs_p, "convw": convw,
        "a_w_in": np.ascontiguousarray(inp["a_w_in"][0]), "a_w_o": np.ascontiguousarray(inp["a_w_o"][0]),
        "kv_w": inp["kv_w"], "b_w_q": np.ascontiguousarray(inp["b_w_q"][0]),
        "b_w_o": np.ascontiguousarray(inp["b_w_o"][0]),
        "up0": np.ascontiguousarray(inp["ffn_w_up"][0]), "up1": np.ascontiguousarray(inp["ffn_w_up"][1]),
        "dn0": np.ascontiguousarray(inp["ffn_w_down"][0]), "dn1": np.ascontiguousarray(inp["ffn_w_down"][1]),
    }


def kernel(**inputs):
    inp = {k: np.asarray(v) for k, v in inputs.items()}
    nc = build_program(NG=16, stage=6, with_sample=True)
    in_maps = [host_inputs(inp, c) for c in range(8)]
    res = run_bass_kernel_spmd(nc, in_maps, core_ids=list(range(8)))
    r = res.results
    y_prompt = np.stack([r[b]["y_p"] for b in range(2)])
    k_a_p = np.stack([r[b]["k_a_p"].reshape(SEQ, 4, 128) for b in range(2)])[None]
    v_a_p = np.stack([r[b]["v_a_p"].reshape(SEQ, 4, 128) for b in range(2)])[None]
    kidx_a_p = np.stack([r[b]["kidx_a_p"] for b in range(2)])[None]
    k_b_p = np.stack([r[b]["k_b_p"].reshape(2048, 8, 128) for b in range(2)])
    v_b_p = np.stack([r[b]["v_b_p"].reshape(2048, 8, 128) for b in range(2)])
    conv_p = np.stack([r[b]["conv_p"] for b in range(2)], axis=1)
    n16 = NS * TS
    y_sample = np.concatenate([r[c]["y_s"][:n16].reshape(NS, TS, D) for c in range(8)])
    k_a_s = np.concatenate([r[c]["k_a_s"][:n16].reshape(NS, TS, 4, 128) for c in range(8)])[None]
    v_a_s = np.concatenate([r[c]["v_a_s"][:n16].reshape(NS, TS, 4, 128) for c in range(8)])[None]
    kidx_a_s = np.concatenate([r[c]["kidx_a_s"][:n16].reshape(NS, TS, 128) for c in range(8)])[None]
    k_b_s = np.concatenate([r[c]["k_b_s"].reshape(NS, 2048, 8, 128) for c in range(8)])
    v_b_s = np.concatenate([r[c]["v_b_s"].reshape(NS, 2048, 8, 128) for c in range(8)])
    conv_s = np.concatenate([r[c]["conv_s"].reshape(2, NS, 2, 2 * DFF) for c in range(8)], axis=1)
    return (y_prompt, y_sample, k_a_p, v_a_p, kidx_a_p, k_b_p, v_b_p, conv_p,
            k_a_s, v_a_s, kidx_a_s, k_b_s, v_b_s, conv_s)
```
